# Optimizing a Trainium2 kernel written in Bass

```python
import jax, jax.numpy as jnp
from jax import lax
import numpy as np

D_MODEL = 1024
BATCH = 4
SEQ = 8192
DEPTH = 1

CHUNK = 64
LEFT_CHUNKS = 8
BAND = LEFT_CHUNKS + 1
EPS = 1e-5
D_FF = 2816
SSD_HEADS = 16
SSD_HEAD_DIM = 64
SSD_INNER = SSD_HEADS * SSD_HEAD_DIM
SSD_GROUPS = 2
SSD_STATE = 128
CONV_WIDTH = 4
CONV_DIM = SSD_INNER + 2 * SSD_GROUPS * SSD_STATE
ATT_HEADS = 8
ATT_HEAD_DIM = 64
ATT_INNER = ATT_HEADS * ATT_HEAD_DIM
MAX_REL = 256
MIX_WIDTH = SSD_INNER + ATT_INNER
PROJ_DIM = SSD_INNER + CONV_DIM + SSD_HEADS + 3 * ATT_INNER

kernel_name = "hybrid_ssd_chunkattn_macaron_block"


def rms_norm(x, w):
    xf = x.astype(jnp.float32)
    y = xf * lax.rsqrt(jnp.mean(xf * xf, axis=-1, keepdims=True) + EPS)
    return (y * w.astype(jnp.float32)).astype(x.dtype)


def swiglu(h, w_gate, w_up, w_down):
    return (jax.nn.silu(h @ w_gate) * (h @ w_up)) @ w_down


def causal_depthwise_conv(u, w, b):
    k = w.shape[0]
    out = lax.conv_general_dilated(
        u, w.astype(u.dtype)[:, None, :], window_strides=(1,), padding=[(k - 1, 0)],
        dimension_numbers=('NWC', 'WIO', 'NWC'), feature_group_count=u.shape[-1])
    return out + b.astype(u.dtype)


def ssd_chunked(xs, dt, a, bmat, cmat, d_skip):
    bsz, seqlen, n_heads, hd = xs.shape
    n_groups, n_state = bmat.shape[-2], bmat.shape[-1]
    m = n_heads // n_groups
    nc = seqlen // CHUNK
    xf = xs.astype(jnp.float32)
    xdt = (xf * dt[..., None]).reshape(bsz, nc, CHUNK, n_groups, m, hd)
    a_cs = jnp.cumsum((dt * a).reshape(bsz, nc, CHUNK, n_groups, m), axis=2)
    bc = bmat.astype(jnp.float32).reshape(bsz, nc, CHUNK, n_groups, n_state)
    cc = cmat.astype(jnp.float32).reshape(bsz, nc, CHUNK, n_groups, n_state)
    causal = jnp.tril(jnp.ones((CHUNK, CHUNK), dtype=bool))[:, :, None, None]
    seg = a_cs[:, :, :, None] - a_cs[:, :, None, :]
    decay = jnp.exp(jnp.where(causal, seg, -jnp.inf))
    cb = jnp.einsum('bclgn,bcsgn->bclsg', cc, bc)
    y_diag = jnp.einsum('bclsg,bclsgm,bcsgmp->bclgmp', cb, decay, xdt)
    decay_out = jnp.exp(a_cs[:, :, -1:] - a_cs)
    states = jnp.einsum('bclgn,bclgm,bclgmp->bcgmpn', bc, decay_out, xdt)
    chunk_decay = jnp.exp(a_cs[:, :, -1])

    def step(h, inp):
        s_c, d_c = inp
        return h * d_c[..., None, None] + s_c, h

    h0 = jnp.zeros((bsz, n_groups, m, hd, n_state), jnp.float32)
    _, prev = lax.scan(step, h0, (jnp.moveaxis(states, 1, 0), jnp.moveaxis(chunk_decay, 1, 0)))
    prev = jnp.moveaxis(prev, 0, 1)
    y_off = jnp.einsum('bclgn,bcgmpn,bclgm->bclgmp', cc, prev, jnp.exp(a_cs))
    y = (y_diag + y_off).reshape(bsz, seqlen, n_heads, hd)
    return y + xf * d_skip.astype(jnp.float32)[:, None]


def chunked_rel_attention(q, k, v, rel_bias):
    bsz, seqlen, n_heads, hd = q.shape
    nc = seqlen // CHUNK
    qc = q.reshape(bsz, nc, CHUNK, n_heads, hd)
    pad = ((0, 0), (LEFT_CHUNKS * CHUNK, 0), (0, 0), (0, 0))
    kp = jnp.pad(k, pad).reshape(bsz, nc + LEFT_CHUNKS, CHUNK, n_heads, hd)
    vp = jnp.pad(v, pad).reshape(bsz, nc + LEFT_CHUNKS, CHUNK, n_heads, hd)
    kb = jnp.stack([kp[:, o:o + nc] for o in range(BAND)], axis=2).reshape(bsz, nc, BAND * CHUNK, n_heads, hd)
    vb = jnp.stack([vp[:, o:o + nc] for o in range(BAND)], axis=2).reshape(bsz, nc, BAND * CHUNK, n_heads, hd)
    qpos = LEFT_CHUNKS * CHUNK + jnp.arange(CHUNK)
    kpos = jnp.arange(BAND * CHUNK)
    rel = jnp.clip(qpos[:, None] - kpos[None, :], -MAX_REL, MAX_REL) + MAX_REL
    bias = rel_bias.astype(jnp.float32)[:, rel]
    s = jnp.einsum('bcqhd,bckhd->bhcqk', qc, kb).astype(jnp.float32) * (hd ** -0.5) + bias[None, :, None]
    valid = (jnp.arange(nc)[:, None] - LEFT_CHUNKS + jnp.arange(BAND)[None, :]) >= 0
    valid = jnp.repeat(valid, CHUNK, axis=1)
    s = jnp.where(valid[None, None, :, None, :], s, jnp.finfo(jnp.float32).min)
    p = jax.nn.softmax(s, axis=-1).astype(v.dtype)
    o = jnp.einsum('bhcqk,bckhd->bcqhd', p, vb)
    return o.reshape(bsz, seqlen, n_heads * hd)


def hybrid_mixer(x, mix_norm, w_in, conv_w, conv_b, dt_bias, a_log, d_skip, ssd_norm, rel_bias, w_out):
    bsz, seqlen, _ = x.shape
    h = rms_norm(x, mix_norm)
    proj = h @ w_in
    o1 = SSD_INNER
    o2 = o1 + CONV_DIM
    o3 = o2 + SSD_HEADS
    o4 = o3 + ATT_INNER
    o5 = o4 + ATT_INNER
    z, xbc, dt_raw = proj[..., :o1], proj[..., o1:o2], proj[..., o2:o3]
    q, k, v = proj[..., o3:o4], proj[..., o4:o5], proj[..., o5:]
    xbc = jax.nn.silu(causal_depthwise_conv(xbc, conv_w, conv_b))
    gn = SSD_GROUPS * SSD_STATE
    xs = xbc[..., :SSD_INNER].reshape(bsz, seqlen, SSD_HEADS, SSD_HEAD_DIM)
    bm = xbc[..., SSD_INNER:SSD_INNER + gn].reshape(bsz, seqlen, SSD_GROUPS, SSD_STATE)
    cm = xbc[..., SSD_INNER + gn:].reshape(bsz, seqlen, SSD_GROUPS, SSD_STATE)
    dt = jax.nn.softplus(dt_raw.astype(jnp.float32) + dt_bias.astype(jnp.float32))
    a = -jnp.exp(a_log.astype(jnp.float32))
    y = ssd_chunked(xs, dt, a, bm, cm, d_skip).reshape(bsz, seqlen, SSD_INNER)
    y = rms_norm(y * jax.nn.silu(z.astype(jnp.float32)), ssd_norm).astype(x.dtype)
    att = chunked_rel_attention(
        q.reshape(bsz, seqlen, ATT_HEADS, ATT_HEAD_DIM),
        k.reshape(bsz, seqlen, ATT_HEADS, ATT_HEAD_DIM),
        v.reshape(bsz, seqlen, ATT_HEADS, ATT_HEAD_DIM), rel_bias).astype(x.dtype)
    return jnp.concatenate([y, att], axis=-1) @ w_out


def setup_inputs(seed: int = 0) -> dict:
    key = jax.random.key(seed)
    ks = jax.random.split(key, 24)

    def nrm(k, shape, scale):
        return jax.random.normal(k, shape, jnp.float32) * scale

    def gain(k, n):
        return 1.0 + nrm(k, (DEPTH, n), 0.02)

    u = jax.random.uniform(ks[10], (DEPTH, SSD_HEADS), jnp.float32)
    dt0 = jnp.exp(u * (np.log(0.1) - np.log(0.001)) + np.log(0.001))
    dt_bias = dt0 + jnp.log(-jnp.expm1(-dt0))
    a_log = jnp.log(jax.random.uniform(ks[11], (DEPTH, SSD_HEADS), jnp.float32, 1.0, 16.0))
    return {
        "x": nrm(ks[0], (BATCH, SEQ, D_MODEL), 1.0),
        "ffn1_norm": gain(ks[1], D_MODEL),
        "ffn1_w_gate": nrm(ks[2], (DEPTH, D_MODEL, D_FF), D_MODEL ** -0.5),
        "ffn1_w_up": nrm(ks[3], (DEPTH, D_MODEL, D_FF), D_MODEL ** -0.5),
        "ffn1_w_down": nrm(ks[4], (DEPTH, D_FF, D_MODEL), D_FF ** -0.5),
        "mix_norm": gain(ks[5], D_MODEL),
        "w_in": nrm(ks[6], (DEPTH, D_MODEL, PROJ_DIM), D_MODEL ** -0.5),
        "conv_w": nrm(ks[7], (DEPTH, CONV_WIDTH, CONV_DIM), CONV_WIDTH ** -0.5),
        "conv_b": nrm(ks[8], (DEPTH, CONV_DIM), 0.02),
        "dt_bias": dt_bias,
        "a_log": a_log,
        "d_skip": 1.0 + nrm(ks[12], (DEPTH, SSD_HEADS), 0.1),
        "ssd_norm": gain(ks[13], SSD_INNER),
        "rel_bias": nrm(ks[14], (DEPTH, ATT_HEADS, 2 * MAX_REL + 1), 0.1),
        "w_out": nrm(ks[15], (DEPTH, MIX_WIDTH, D_MODEL), MIX_WIDTH ** -0.5),
        "ffn2_norm": gain(ks[16], D_MODEL),
        "ffn2_w_gate": nrm(ks[17], (DEPTH, D_MODEL, D_FF), D_MODEL ** -0.5),
        "ffn2_w_up": nrm(ks[18], (DEPTH, D_MODEL, D_FF), D_MODEL ** -0.5),
        "ffn2_w_down": nrm(ks[19], (DEPTH, D_FF, D_MODEL), D_FF ** -0.5),
        "final_norm": 1.0 + nrm(ks[20], (D_MODEL,), 0.02),
    }


def reference(x, ffn1_norm, ffn1_w_gate, ffn1_w_up, ffn1_w_down, mix_norm, w_in, conv_w, conv_b,
              dt_bias, a_log, d_skip, ssd_norm, rel_bias, w_out, ffn2_norm, ffn2_w_gate, ffn2_w_up,
              ffn2_w_down, final_norm):
    for i in range(DEPTH):
        x = x + 0.5 * swiglu(rms_norm(x, ffn1_norm[i]), ffn1_w_gate[i], ffn1_w_up[i], ffn1_w_down[i])
        x = x + hybrid_mixer(x, mix_norm[i], w_in[i], conv_w[i], conv_b[i], dt_bias[i], a_log[i],
                             d_skip[i], ssd_norm[i], rel_bias[i], w_out[i])
        x = x + 0.5 * swiglu(rms_norm(x, ffn2_norm[i]), ffn2_w_gate[i], ffn2_w_up[i], ffn2_w_down[i])
    return rms_norm(x, final_norm)
```

```python
import numpy as np
from contextlib import ExitStack
import concourse.bass as bass
import concourse.mybir as mybir
from concourse.bass_utils import run_bass_kernel_spmd

F32 = mybir.dt.float32
BF16 = mybir.dt.bfloat16
AF = mybir.ActivationFunctionType
ALU = mybir.AluOpType

ENGS = ("tensor", "vector", "scalar", "gpsimd", "sync")

D = 1024
DFF = 2816
NFT = DFF // 128
TT = 512
EPS = 1e-5
PROJ = 4112
O_Z, O_XBC, O_DT, O_Q, O_K, O_V = 0, 1024, 2560, 2576, 3088, 3600


class Buf:
    __slots__ = ("name", "w", "r")

    def __init__(self, name):
        self.name = name
        self.w = None
        self.r = []


class Prog:
    def __init__(self, nc):
        self.nc = nc
        self.q = {e: [] for e in ENGS}
        self.cnt = {e: 0 for e in ENGS}
        self.waited = {e: {} for e in ENGS}
        self.semkeys = list(ENGS)
        self.same_engine_sync = True
        self.fam = {}
        self.famn = {}

    def new_sem(self, key):
        self.semkeys.append(key)
        self.cnt[key] = 0
        return key

    def _deps_for(self, reads, writes):
        deps = []
        for b in reads:
            if b.w is not None:
                deps.append(b.w)
        for b in writes:
            if b.w is not None:
                deps.append(b.w)
            deps.extend(b.r)
        return deps

    def _mark(self, tok, reads, writes):
        for b in reads:
            b.r.append(tok)
            if len(b.r) > 8:
                best = {}
                for (k, v) in b.r:
                    if best.get(k, 0) < v:
                        best[k] = v
                b.r = list(best.items())
        for b in writes:
            b.w = tok
            b.r = []

    def _waits(self, eng, alld, skip_same):
        waits = []
        for d in alld:
            if d is None:
                continue
            k, v = d
            if k == eng and (skip_same or v > self.cnt[eng]):
                continue
            if self.waited[eng].get(k, 0) < v:
                self.waited[eng][k] = v
                waits.append((k, v))
        return waits

    def op(self, eng, fn, reads=(), writes=(), deps=(), inc=True):
        alld = list(deps) + self._deps_for(reads, writes)
        skip_same = (eng == "tensor") or (not self.same_engine_sync)
        waits = self._waits(eng, alld, skip_same)
        tok = None
        if inc:
            self.cnt[eng] += 1
            tok = (eng, self.cnt[eng])
        else:
            tok = (eng, self.cnt[eng] + 1)
        self.q[eng].append((waits, fn, eng if inc else None, 1))
        self._mark(tok, reads, writes)
        return tok

    def dma(self, eng, semkey, fn, reads=(), writes=(), deps=(), incv=16):
        alld = list(deps) + self._deps_for(reads, writes)
        if semkey in self.fam:
            names = self.fam[semkey]
            key = names[self.famn[semkey] % len(names)]
            self.famn[semkey] += 1
            if self.cnt[key] > 0:
                alld.append((key, self.cnt[key]))
        else:
            key = semkey
        waits = self._waits(eng, alld, False)
        self.cnt[key] += (1 if incv == -1 else incv)
        tok = (key, self.cnt[key])
        self.q[eng].append((waits, fn, key, incv))
        self._mark(tok, reads, writes)
        return tok

    def new_family(self, fam, n):
        self.fam[fam] = [self.new_sem(f"{fam}{i}") for i in range(n)]
        self.famn[fam] = 0

    def dma_toks(self):
        out = []
        for names in self.fam.values():
            out.extend((k, self.cnt[k]) for k in names if self.cnt[k] > 0)
        return out

    def wait_all(self, eng, toks):
        waits = self._waits(eng, toks, False)
        if waits:
            self.q[eng].append((waits, None, None, 0))

    def fence(self, exclude=()):
        ex = set(exclude)
        for f_ in exclude:
            ex.update(self.fam.get(f_, []))
        toks = [(k, self.cnt[k]) for k in self.semkeys if self.cnt[k] > 0 and k not in ex]
        for e in ENGS:
            self.wait_all(e, [tk for tk in toks if tk[0] != e])

    def emit(self, sems):
        def run(engname):
            def body(e):
                for (waits, fn, inckey, incv) in self.q[engname]:
                    if fn is None:
                        for (k, v) in waits:
                            e.wait_ge(sems[k], v)
                        continue
                    for (k, v) in waits[1:]:
                        e.wait_ge(sems[k], v)
                    ins = fn(e)
                    if waits:
                        ins._wait_ge(sems[waits[0][0]], waits[0][1])
                    if inckey is not None:
                        if incv == -1:
                            ins.then_inc(sems[inckey])
                        else:
                            ins.then_inc(sems[inckey], incv)
            return body
        return run


class SB:
    def __init__(self, nc, nbytes):
        self.nbytes = nbytes
        self.t = nc.alloc_sbuf_tensor("sb_all", [128, nbytes // 2], BF16)

    def view(self, off, shape, dtype):
        assert off % 4 == 0
        esz = 4 if dtype == F32 else 2
        n = 1
        for s in shape[1:]:
            n *= s
        nb = n * esz
        assert off + nb <= self.nbytes, (off, nb, self.nbytes)
        v = self.t[:, off // 2:(off + nb) // 2]
        if dtype == F32:
            v = v.bitcast(F32)
        if len(shape) == 3:
            v = v.rearrange("p (a b) -> p a b", a=shape[1])
        elif len(shape) == 4:
            v = v.rearrange("p (a b c) -> p a b c", a=shape[1], b=shape[2])
        return v


class Carver:
    def __init__(self, sb, lo, hi):
        self.sb, self.lo, self.hi, self.cur = sb, lo, hi, lo

    def get(self, shape, dtype, align=32):
        self.cur = (self.cur + align - 1) // align * align
        esz = 4 if dtype == F32 else 2
        n = 1
        for s in shape[1:]:
            n *= s
        v = self.sb.view(self.cur, shape, dtype)
        self.cur += n * esz
        assert self.cur <= self.hi, ("SBUF carve overflow", self.cur, self.hi)
        return v


DBG = {}
ARENA = 3 * 8 * DFF * 2
SB_TOTAL = 212800
PERS = 2688
WIN_BYTES = 8 * PROJ * 2
WOUT_OFF = ARENA - 12 * D * 2


def bc_last(ap, n):
    p, a = ap.shape
    return ap.rearrange("p (a o) -> p a o", o=1).broadcast_to([p, a, n])


def bc_mid(ap, n):
    p, l = ap.shape
    return ap.rearrange("p (o l) -> p o l", o=1).broadcast_to([p, n, l])


def build_program(NT=8, phases=(1, 2, 3, 4), debug_out=False, n_cores=8):
    nc = bass.Bass("TRN2", target_bir_lowering=False)
    NTH = NT + 1
    TOK = NT * TT
    TOKH = NTH * TT
    NCH = NT * 4

    def din(name, shape, dt=F32):
        return nc.dram_tensor(name, list(shape), dt, kind="ExternalInput").ap()

    xT = din("xT", [D, TOKH])
    w1g, w1u, w1d = din("w1g", [D, DFF]), din("w1u", [D, DFF]), din("w1d", [DFF, D])
    w2g, w2u, w2d = din("w2g", [D, DFF]), din("w2u", [D, DFF]), din("w2d", [DFF, D])
    w_in = din("w_in", [D, PROJ])
    w_out = din("w_out", [1536, D])
    gains = din("gains", [128, 32])
    convp = din("convp", [128, 60])
    hvec = din("hvec", [128, 48])
    dcol = din("dcol", [128, 8])
    flag = din("flag", [128, 1])
    cmats = din("cmats", [128, 3 * 128])
    identb = din("identb", [128, 128])
    biasT = din("biasT", [128, 8 * 5 * 128])
    amask = din("amask", [128, 5 * 128])
    ssdn = din("ssdn", [128, 1024])
    outT = nc.dram_tensor("outT", [D, TOK], F32, kind="ExternalOutput").ap()

    kindS = "ExternalOutput" if debug_out else "Internal"
    x1S = nc.dram_tensor("x1S", [D, TOKH], F32, kind=kindS).ap()
    hmS = nc.dram_tensor("hmS", [D, TOKH], BF16, kind=kindS).ap()
    x2S = nc.dram_tensor("x2S", [D, TOK], F32, kind=kindS).ap()
    ylS = nc.dram_tensor("ylS", [TOK, 1024], F32, kind=kindS).ap()
    szS = nc.dram_tensor("szS", [TOK, 1024], BF16, kind=kindS).ap()
    attS = nc.dram_tensor("attS", [TOK, 512], BF16, kind=kindS).ap()
    cS = nc.dram_tensor("cS", [256, TOK], BF16, kind=kindS).ap()
    hloc = nc.dram_tensor("hloc", [128, 1024], F32, kind="Internal").ap()
    hpair = nc.dram_tensor("hpair", [256, 1024], F32, kind="Internal").ap()
    if debug_out:
        hdbg = nc.dram_tensor("hdbg", [128, 1024], F32, kind="ExternalOutput").ap()
        hindbg = nc.dram_tensor("hindbg", [128, 1024], F32, kind="ExternalOutput").ap()

    P = Prog(nc)
    P.new_family("ld", 24)
    P.new_family("st", 24)
    P.new_family("wq", 24)
    P.new_sem("cc")

    es = ExitStack()
    sb = SB(nc, SB_TOTAL)
    banks = [nc.alloc_psum_tensor(f"bank{i}", [128, 512], F32) for i in range(8)]
    bankB = [Buf(f"bank{i}") for i in range(8)]
    sems = {k: es.enter_context(nc.semaphore(k)) for k in P.semkeys}

    pers = Carver(sb, SB_TOTAL - PERS, SB_TOTAL)
    ones_s = pers.get([128, 128], BF16)
    gains_sb = pers.get([128, 32], F32)
    eps_sb = pers.get([128, 1], F32)
    one_c = pers.get([128, 1], F32)
    flag_sb = pers.get([128, 1], F32)
    runtot = pers.get([128, 16], F32)
    eag_all = pers.get([128, NCH, 16], F32)
    onesB, gainsB, epsB = Buf("ones_s"), Buf("gains"), Buf("eps")
    onecB, flagB, runtotB, eagB = Buf("one_c"), Buf("flag"), Buf("runtot"), Buf("eag")
    P.op("gpsimd", lambda e: e.memset(eps_sb, EPS), writes=[epsB])
    P.op("gpsimd", lambda e: e.memset(one_c, 1.0), writes=[onecB])
    P.op("gpsimd", lambda e: e.memset(runtot, 0.0), writes=[runtotB])
    P.op("gpsimd", lambda e: e.memset(ones_s, 1.0 / 1024.0), writes=[onesB])
    P.dma("sync", "ld", lambda e: e.dma_start(out=gains_sb, in_=gains), writes=[gainsB])
    P.dma("sync", "ld", lambda e: e.dma_start(out=flag_sb, in_=flag), writes=[flagB])

    def ffn_weight_views():
        wg = sb.view(0, [128, 8, DFF], BF16)
        wu = sb.view(8 * DFF * 2, [128, 8, DFF], BF16)
        wd = sb.view(2 * 8 * DFF * 2, [128, NFT, D], BF16)
        return wg, wu, wd

    NWC = 4
    WCW = DFF // NWC
    wgB = [Buf(f"wg{i}") for i in range(NWC)]
    wuB = [Buf(f"wu{i}") for i in range(NWC)]
    wdB = [Buf(f"wd{i}") for i in range(2)]

    def load_ffn_weights(gd, ud, dd, do_parts=(0, 1)):
        wg, wu, wd = ffn_weight_views()
        gsrc = gd.rearrange("(kt p) f -> p kt f", p=128)
        usrc = ud.rearrange("(kt p) f -> p kt f", p=128)
        dsrc = dd.rearrange("(ft p) d -> p ft d", p=128)
        for c in range(NWC):
            cs = slice(c * WCW, (c + 1) * WCW)
            P.dma("gpsimd", "wq", lambda e, cs=cs: e.dma_start(out=wg[:, :, cs], in_=gsrc[:, :, cs]), writes=[wgB[c]])
            P.dma("gpsimd", "wq", lambda e, cs=cs: e.dma_start(out=wu[:, :, cs], in_=usrc[:, :, cs]), writes=[wuB[c]])
        for part in do_parts:
            fs = slice(0, 10) if part == 0 else slice(10, NFT)
            P.dma("gpsimd", "wq", lambda e, fs=fs: e.dma_start(out=wd[:, fs, :], in_=dsrc[:, fs, :]), writes=[wdB[part]])

    def ffn_phase(src, ntiles, gcol_pre, gcol_post, post_mode, dst_x, dst_h, dst_tok0):
        wg, wu, wd = ffn_weight_views()
        c = Carver(sb, ARENA, SB_TOTAL - 1024)
        xt = [c.get([128, 8, TT], F32) for _ in range(2)]
        s1 = c.get([128, 8, TT], BF16)
        hb = c.get([128, 8, TT], BF16)
        hid = c.get([128, NFT, TT], BF16)
        rstd = c.get([128, TT], F32)
        xtB = [Buf("xt0"), Buf("xt1")]
        s1B, hB, rstdB = Buf("s1"), Buf("h"), Buf("rstd")
        hidB = [Buf(f"hid{i}") for i in range(NFT)]
        srcv = src.rearrange("(kt p) t -> p kt t", p=128)
        dxv = dst_x.rearrange("(kt p) t -> p kt t", p=128)
        dhv = dst_h.rearrange("(kt p) t -> p kt t", p=128) if dst_h is not None else None
        PB_G, PB_U, PB_D, PB_N = (0, 1), (2, 3), (4, 5), 6

        def load_x(t):
            b = t % 2
            P.dma("sync", "ld", lambda e: e.dma_start(out=xt[b], in_=srcv[:, :, t * TT:(t + 1) * TT]), writes=[xtB[b]])

        def norm_sq(t):
            b = t % 2
            P.op("scalar", lambda e: e.activation(out=s1, in_=xt[b], func=AF.Square), reads=[xtB[b]], writes=[s1B])

        def norm_mm(t):
            for kt in range(8):
                P.op("tensor", lambda e, kt=kt: e.matmul(banks[PB_N][:, :], lhsT=ones_s, rhs=s1[:, kt, :], start=(kt == 0), stop=(kt == 7)),
                     reads=[s1B, onesB], writes=[bankB[PB_N]], inc=(kt == 7))

        def norm_rstd(t):
            P.op("scalar", lambda e: e.activation(out=rstd, in_=banks[PB_N][:, :], func=AF.Sqrt, bias=eps_sb[:, 0:1]),
                 reads=[bankB[PB_N], epsB], writes=[rstdB])
            P.op("vector", lambda e: e.reciprocal(out=rstd, in_=rstd), reads=[rstdB], writes=[rstdB])

        def pre_scale(t):
            b = t % 2
            for kt in range(8):
                P.op("vector", lambda e, kt=kt: e.scalar_tensor_tensor(out=hb[:, kt, :], in0=xt[b][:, kt, :], scalar=gains_sb[:, gcol_pre + kt:gcol_pre + kt + 1],
                                                                        in1=rstd, op0=ALU.mult, op1=ALU.mult),
                     reads=[xtB[b], rstdB, gainsB], writes=[hB], inc=(kt == 7))

        def post_scale_store(t):
            b = t % 2
            ts = slice(dst_tok0 + t * TT, dst_tok0 + (t + 1) * TT)
            if post_mode == "mix":
                P.dma("sync", "st", lambda e: e.dma_start(out=dxv[:, :, ts], in_=xt[b]), reads=[xtB[b]])
                for kt in range(8):
                    P.op("vector", lambda e, kt=kt: e.scalar_tensor_tensor(out=s1[:, kt, :], in0=xt[b][:, kt, :], scalar=gains_sb[:, gcol_post + kt:gcol_post + kt + 1],
                                                                            in1=rstd, op0=ALU.mult, op1=ALU.mult),
                         reads=[xtB[b], rstdB, gainsB], writes=[s1B], inc=(kt == 7))
                return P.dma("sync", "st", lambda e: e.dma_start(out=dhv[:, :, ts], in_=s1), reads=[s1B])
            else:
                for kt in range(8):
                    P.op("vector", lambda e, kt=kt: e.scalar_tensor_tensor(out=xt[b][:, kt, :], in0=xt[b][:, kt, :], scalar=gains_sb[:, gcol_post + kt:gcol_post + kt + 1],
                                                                            in1=rstd, op0=ALU.mult, op1=ALU.mult),
                         reads=[rstdB, gainsB], writes=[xtB[b]], inc=(kt == 7))
                return P.dma("sync", "st", lambda e: e.dma_start(out=dxv[:, :, ts], in_=xt[b]), reads=[xtB[b]])

        def gate_up_ft(t, ft):
            pg, pu = PB_G[ft % 2], PB_U[ft % 2]
            fs = slice(ft * 128, (ft + 1) * 128)
            wcs = sorted(set([(ft * 128) // WCW, (ft * 128 + 127) // WCW]))
            for kt in range(8):
                P.op("tensor", lambda e, kt=kt: e.matmul(banks[pg][:, :], lhsT=wg[:, kt, fs], rhs=hb[:, kt, :], start=(kt == 0), stop=(kt == 7)),
                     reads=[hB] + [wgB[i] for i in wcs], writes=[bankB[pg]], inc=(kt == 7))
            for kt in range(8):
                P.op("tensor", lambda e, kt=kt: e.matmul(banks[pu][:, :], lhsT=wu[:, kt, fs], rhs=hb[:, kt, :], start=(kt == 0), stop=(kt == 7)),
                     reads=[hB] + [wuB[i] for i in wcs], writes=[bankB[pu]], inc=(kt == 7))
            P.op("scalar", lambda e: e.activation(out=hid[:, ft, :], in_=banks[pg][:, :], func=AF.Silu), reads=[bankB[pg]], writes=[hidB[ft]])
            P.op("vector", lambda e: e.tensor_tensor(out=hid[:, ft, :], in0=hid[:, ft, :], in1=banks[pu][:, :], op=ALU.mult),
                 reads=[bankB[pu]], writes=[hidB[ft]])

        def down(t):
            b = t % 2
            for dt_ in range(8):
                pd = PB_D[dt_ % 2]
                ds_ = slice(dt_ * 128, (dt_ + 1) * 128)
                for ft in range(NFT):
                    P.op("tensor", lambda e, ft=ft, pd=pd, ds_=ds_: e.matmul(banks[pd][:, :], lhsT=wd[:, ft, ds_], rhs=hid[:, ft, :], start=(ft == 0), stop=(ft == NFT - 1)),
                         reads=[hidB[ft], wdB[0 if ft < 10 else 1]], writes=[bankB[pd]], inc=(ft == NFT - 1))
                P.op("vector", lambda e, pd=pd, dt_=dt_: e.scalar_tensor_tensor(out=xt[b][:, dt_, :], in0=banks[pd][:, :], scalar=0.5, in1=xt[b][:, dt_, :], op0=ALU.mult, op1=ALU.add),
                     reads=[bankB[pd]], writes=[xtB[b]])

        last = None
        load_x(0)
        if ntiles > 1:
            load_x(1)
        norm_sq(0); norm_mm(0); norm_rstd(0); pre_scale(0)
        for t in range(ntiles + 1):
            for ft in range(NFT):
                if t < ntiles:
                    gate_up_ft(t, ft)
                if t >= 1:
                    if ft == 1:
                        norm_sq(t - 1)
                    if ft == 3:
                        norm_mm(t - 1)
                    if ft == 5:
                        norm_rstd(t - 1)
                        last = post_scale_store(t - 1)
                        if t + 1 < ntiles:
                            load_x(t + 1)
                if t + 1 < ntiles:
                    if ft == 13:
                        norm_sq(t + 1)
                    if ft == 16:
                        norm_mm(t + 1)
                    if ft == 18:
                        norm_rstd(t + 1)
            if t + 1 < ntiles:
                pre_scale(t + 1)
            if t < ntiles:
                down(t)
        return last

    WCH = [("k", O_K, O_K + 512), ("v", O_V, O_V + 512), ("xbc0", O_XBC, O_XBC + 768),
           ("xbc1", O_XBC + 768, O_XBC + 1536), ("dt", O_DT, O_DT + 16), ("q", O_Q, O_Q + 512),
           ("z0", O_Z, O_Z + 512), ("z1", O_Z + 512, O_Z + 1024)]
    winB = {n: Buf("win_" + n) for n, _, _ in WCH}
    woutB = Buf("wout")
    hlocB, hpairB = Buf("hloc"), Buf("hpair")
    ylSB, szSB, attSB, cSB, x2SB = Buf("ylS"), Buf("szS"), Buf("attS"), Buf("cS"), Buf("x2S")

    def win_buf(col):
        for n, lo, hi in WCH:
            if lo <= col < hi:
                return winB[n]
        raise AssertionError(col)

    def load_w_in():
        win = sb.view(0, [128, 8, PROJ], BF16)
        src = w_in.rearrange("(kt p) f -> p kt f", p=128)
        first = True
        for n, lo, hi in WCH:
            P.dma("gpsimd", "wq", lambda e, lo=lo, hi=hi: e.dma_start(out=win[:, :, lo:hi], in_=src[:, :, lo:hi]),
                  writes=[winB[n]] + ((wgB + wuB) if first else []))
            first = False

    RG = [[2 * i, 2 * i + 1] for i in range(n_cores // 2)]

    def mixer_pass1():
        win = sb.view(0, [128, 8, PROJ], BF16)
        c = Carver(sb, WIN_BYTES, SB_TOTAL - PERS)
        A = 16
        hm = [c.get([128, 8, TT], BF16, A) for _ in range(2)]
        xbc = c.get([128, 12, TT + 3], F32, A)
        xs_fm = c.get([128, 8, TT], BF16, A)
        xsD_fm = c.get([128, 8, TT], BF16, A)
        B_fm = c.get([128, 2, TT], BF16, A)
        C_fm = c.get([128, 2, TT], BF16, A)
        q_fm = c.get([128, 4, TT], BF16, A)
        k_fm = c.get([128, 4, 2 * TT], BF16, A)
        v_tm = c.get([128, 8, 8, 65], BF16, A)
        sz = [c.get([128, 1024], BF16, A)] * 2
        dtp = c.get([128, 4, 16], F32, A)
        SS = c.get([128, 7, 4, 16], F32, A)
        Rh = [c.get([128, 4, 128], BF16, A) for _ in range(2)]
        Rl = [c.get([128, 4, 128], BF16, A) for _ in range(2)]
        lmb = c.get([128, 128], BF16, A)
        dhl = c.get([128, 2, 4, 16], BF16, A)
        E2 = [c.get([128, 16, 128], BF16, A) for _ in range(2)]
        E = E2[0]
        cbm2 = [c.get([128, 2, 128], BF16, A) for _ in range(2)]
        xdt2 = [c.get([128, 16, 64], BF16, A) for _ in range(2)]
        xw2 = [c.get([128, 16, 64], BF16, A) for _ in range(2)]
        Btm2 = [c.get([128, 2, 128], BF16, A) for _ in range(2)]
        H = c.get([128, 16, 64], F32, A)
        Hbf = c.get([128, 16, 64], BF16, A)
        big = c.get([128, 3, 1024], F32, A)
        t1 = big[:, 0, :]
        yl1 = big[:, 1, :]
        etmp = big.rearrange("p a b -> p (a b)")[:, 0:2560].rearrange("p (h j q) -> p h j q", h=4, j=5)
        mtmp = E.rearrange("p a b -> p (a b)")[:, 0:1280].bitcast(F32).rearrange("p (j q) -> p j q", j=5)
        ET = c.get([128, 8, 5, 128], BF16, A)
        Pb = [c.get([128, 5, 128], BF16, A) for _ in range(2)]
        att = [c.get([128, 8, 64], BF16, A)] * 2
        rec = c.get([128, 8], F32, A)
        cm = c.get([128, 3, 128], F32, A)
        ident = c.get([128, 128], BF16, A)
        hv = c.get([128, 48], F32, A)
        a_bc = c.get([128, 16], F32, A)
        cvp = c.get([128, 12, 5], F32, A)
        dcl = c.get([128, 8], F32, A)

        DBG.update({k_: v_ for k_, v_ in locals().items() if k_ not in ('c', 'win')})
        hmB = [Buf("hm0"), Buf("hm1")]
        xbcB = [Buf(f"xbc{i}") for i in range(12)]
        xsB = [Buf(f"xs{i}") for i in range(8)]
        xsDB = [Buf(f"xsD{i}") for i in range(8)]
        BfB = [Buf("Bf0"), Buf("Bf1")]
        CfB = [Buf("Cf0"), Buf("Cf1")]
        qB = Buf("q")
        kB = [Buf("k0"), Buf("k1")]
        vB = [Buf(f"v{i}") for i in range(8)]
        szB = [Buf("sz0")] * 2
        dtpB = Buf("dtp")
        SSB = Buf("SS")
        RhB = [Buf("Rh0"), Buf("Rh1")]
        RlB = [Buf("Rl0"), Buf("Rl1")]
        lmbB, dhlB = Buf("lmb"), Buf("dhl")
        E2B = [Buf("E0"), Buf("E1")]
        EB = E2B[0]
        cbm2B = [Buf("cbm0"), Buf("cbm1")]
        xdt2B = [Buf("xdt0"), Buf("xdt1")]
        xw2B = [Buf("xw0"), Buf("xw1")]
        Btm2B = [Buf("Btm0"), Buf("Btm1")]
        HB, HbfB, t1B = Buf("H"), Buf("Hbf"), Buf("t1")
        yl1B = Buf("yl1")
        ylB = [yl1B, Buf("ylx")]
        ETB = Buf("ET")
        PbB = [Buf("P0"), Buf("P1")]
        attB = [Buf("att0")] * 2
        recB, cmB, identB_, hvB, abcB, cvpB, dclB = (Buf(n) for n in "rec cm ident hv abc cvp dcl".split())

        b2bf = banks[2][:, :].bitcast(BF16)
        b7bf = banks[7][:, :].bitcast(BF16)

        P.dma("sync", "ld", lambda e: e.dma_start(out=cm.rearrange("p a b -> p (a b)"), in_=cmats), writes=[cmB])
        P.dma("gpsimd", "wq", lambda e: e.dma_start(out=ident, in_=identb), writes=[identB_])
        P.dma("sync", "ld", lambda e: e.dma_start(out=hv, in_=hvec), writes=[hvB])
        P.dma("sync", "ld", lambda e: e.dma_start(out=cvp.rearrange("p a b -> p (a b)"), in_=convp), writes=[cvpB])
        P.dma("sync", "ld", lambda e: e.dma_start(out=dcl, in_=dcol), writes=[dclB])
        P.op("scalar", lambda e: e.activation(out=a_bc, in_=hv[:, 16:32], func=AF.Exp), reads=[hvB], writes=[abcB])
        P.op("vector", lambda e: e.tensor_scalar(out=a_bc, in0=a_bc, scalar1=-1.0, scalar2=None, op0=ALU.mult), reads=[abcB], writes=[abcB])
        P.op("vector", lambda e: e.tensor_copy(out=lmb, in_=cm[:, 1, :]), reads=[cmB], writes=[lmbB])
        P.op("gpsimd", lambda e: e.memset(H, 0.0), writes=[HB])
        P.op("gpsimd", lambda e: e.memset(Hbf, 0.0), writes=[HbfB])
        def build_ET():
            P.dma("sync", "ld", lambda e: e.dma_start(out=mtmp.rearrange("p a b -> p (a b)"), in_=amask), writes=[EB])
            for half in range(2):
                P.dma("sync", "ld", lambda e, half=half: e.dma_start(out=etmp.rearrange("p h j q -> p (h j q)"), in_=biasT[:, half * 2560:(half + 1) * 2560]),
                      writes=[t1B, ylB[0], ylB[1]])
                for hh in range(4):
                    h = half * 4 + hh
                    P.op("scalar", lambda e, hh=hh, h=h: e.activation(out=ET[:, h, :, :], in_=etmp[:, hh, :, :], func=AF.Exp),
                         reads=[t1B, ylB[0], ylB[1]], writes=[ETB])
                    P.op("vector", lambda e, h=h: e.tensor_tensor(out=ET[:, h, :, :], in0=ET[:, h, :, :], in1=mtmp, op=ALU.mult),
                         reads=[EB], writes=[ETB])

        rot = [0]

        def pbank():
            b = rot[0] % 2
            rot[0] += 1
            return b

        def load_hm(t):
            b = t % 2
            src = hmS.rearrange("(kt p) t -> p kt t", p=128)
            P.dma("sync", "ld", lambda e: e.dma_start(out=hm[b], in_=src[:, :, t * TT:(t + 1) * TT]), writes=[hmB[b]])

        def mm_fm(t, col0, ncols=TT, c0=0):
            b = pbank()
            hmv, hb_ = hm[t % 2], hmB[t % 2]
            for kt in range(8):
                P.op("tensor", lambda e, kt=kt: e.matmul(banks[b][:, 0:ncols], lhsT=win[:, kt, col0:col0 + 128], rhs=hmv[:, kt, c0:c0 + ncols],
                                                          start=(kt == 0), stop=(kt == 7)),
                     reads=[hb_, win_buf(col0)], writes=[bankB[b]], inc=(kt == 7))
            return b

        def mm_tm(t, tb, col0, ncols, out_ap, outB):
            hmv, hb_ = hm[t % 2], hmB[t % 2]
            for kt in range(8):
                P.op("tensor", lambda e, kt=kt: e.matmul(out_ap, lhsT=hmv[:, kt, tb * 128:(tb + 1) * 128], rhs=win[:, kt, col0:col0 + ncols],
                                                          start=(kt == 0), stop=(kt == 7)),
                     reads=[hb_, win_buf(col0)], writes=[outB], inc=(kt == 7))

        def projections(t):
            half = t % 2

            def k_unit(j):
                b = mm_fm(t, O_K + j * 128)
                P.op("scalar", lambda e: e.activation(out=k_fm[:, j, half * TT:(half + 1) * TT], in_=banks[b][:, :], func=AF.Copy),
                     reads=[bankB[b]], writes=[kB[half]])

            def v_unit(tb):
                b = pbank()
                blk = half * 4 + tb
                mm_tm(t, tb, O_V, 512, banks[b][:, :], bankB[b])
                P.op("vector", lambda e: e.tensor_copy(out=v_tm[:, blk, :, 0:64], in_=banks[b][:, :].rearrange("p (h d) -> p h d", h=8)),
                     reads=[bankB[b]], writes=[vB[blk]])
                if t == 0:
                    P.op("vector", lambda e: e.tensor_copy(out=v_tm[:, blk, :, 64:65],
                                                           in_=flag_sb.rearrange("p (a o) -> p a o", a=1).broadcast_to([128, 8, 1])),
                         reads=[flagB], writes=[vB[blk]])
                else:
                    P.op("gpsimd", lambda e: e.memset(v_tm[:, blk, :, 64:65], 1.0), writes=[vB[blk]])

            def xbc_unit(ct):
                b = mm_fm(t, O_XBC + ct * 128)
                P.op("scalar", lambda e: e.activation(out=xbc[:, ct, 3:TT + 3], in_=banks[b][:, :], func=AF.Copy),
                     reads=[bankB[b]], writes=[xbcB[ct]])

            def q_unit(j):
                b = mm_fm(t, O_Q + j * 128)
                P.op("scalar", lambda e: e.activation(out=q_fm[:, j, :], in_=banks[b][:, :], func=AF.Copy, scale=0.125),
                     reads=[bankB[b]], writes=[qB])

            def dt_unit(tb):
                mm_tm(t, tb, O_DT, 16, banks[2][:, tb * 16:(tb + 1) * 16], bankB[2])
                P.op("vector", lambda e: e.tensor_tensor(out=dtp[:, tb, :], in0=banks[2][:, tb * 16:(tb + 1) * 16], in1=hv[:, 0:16], op=ALU.add),
                     reads=[bankB[2], hvB], writes=[dtpB])

            def z_unit(tb, zh):
                i = tb % 2
                b = pbank()
                mm_tm(t, tb, O_Z + zh * 512, 512, banks[b][:, :], bankB[b])
                P.op("scalar", lambda e: e.activation(out=sz[i][:, zh * 512:(zh + 1) * 512], in_=banks[b][:, :], func=AF.Silu),
                     reads=[bankB[b]], writes=[szB[i]])
                if zh == 1:
                    r0 = (t - 1) * TT + tb * 128
                    P.dma("sync", "st", lambda e: e.dma_start(out=szS[r0:r0 + 128, :], in_=sz[i]), reads=[szB[i]])

            if t == 0:
                for j in range(4):
                    k_unit(j)
                for tb in range(4):
                    v_unit(tb)
                for ct in range(12):
                    b = mm_fm(t, O_XBC + ct * 128, ncols=16, c0=TT - 16)
                    P.op("scalar", lambda e, b=b, ct=ct: e.activation(out=xbc[:, ct, 0:3], in_=banks[b][:, 13:16], func=AF.Copy),
                         reads=[bankB[b]], writes=[xbcB[ct]])
                return
            for ct in range(12):
                xbc_unit(ct)
            rest = ([(lambda j=j: k_unit(j)) for j in range(4)] + [(lambda tb=tb: v_unit(tb)) for tb in range(4)]
                    + [(lambda j=j: q_unit(j)) for j in range(4)] + [(lambda tb=tb: dt_unit(tb)) for tb in range(4)]
                    + [(lambda tb=tb, zh=zh: z_unit(tb, zh)) for tb in range(4) for zh in range(2)])
            for idx, u in enumerate(rest):
                u()
                if idx % 2 == 1 and idx // 2 < 12:
                    conv_ct(t, idx // 2)
            conv_tail(t)

        def conv(t):
            for ct in range(12):
                conv_ct(t, ct)
            conv_tail(t)

        def conv_ct(t, ct):
            if True:
                ab = 3 + ct % 2
                acc = banks[ab][:, :]
                P.op("vector", lambda e, ct=ct, acc=acc: e.tensor_scalar(out=acc, in0=xbc[:, ct, 0:TT], scalar1=cvp[:, ct, 0:1], scalar2=cvp[:, ct, 4:5],
                                                                         op0=ALU.mult, op1=ALU.add),
                     reads=[xbcB[ct], cvpB], writes=[bankB[ab]])
                for k in range(1, 4):
                    P.op("vector", lambda e, ct=ct, acc=acc, k=k: e.scalar_tensor_tensor(out=acc, in0=xbc[:, ct, k:k + TT], scalar=cvp[:, ct, k:k + 1], in1=acc,
                                                                                         op0=ALU.mult, op1=ALU.add),
                         reads=[xbcB[ct], cvpB], writes=[bankB[ab]])
                if ct < 8:
                    dst, dB = xs_fm[:, ct, :], xsB[ct]
                elif ct < 10:
                    dst, dB = B_fm[:, ct - 8, :], BfB[ct - 8]
                else:
                    dst, dB = C_fm[:, ct - 10, :], CfB[ct - 10]
                P.op("scalar", lambda e, acc=acc, dst=dst: e.activation(out=dst, in_=acc, func=AF.Silu), reads=[bankB[ab]], writes=[dB])
                if ct < 8:
                    P.op("gpsimd", lambda e, ct=ct: e.tensor_scalar(out=xsD_fm[:, ct, :], in0=xs_fm[:, ct, :], scalar1=dcl[:, ct:ct + 1], scalar2=None, op0=ALU.mult),
                         reads=[xsB[ct], dclB], writes=[xsDB[ct]])
        def conv_tail(t):
            P.op("gpsimd", lambda e: e.tensor_copy(out=xbc[:, :, 0:3], in_=xbc[:, :, TT:TT + 3]), reads=xbcB, writes=xbcB)
            c0 = (t - 1) * TT
            P.dma("sync", "st", lambda e: e.dma_start(out=cS.rearrange("(g p) t -> p g t", p=128)[:, :, c0:c0 + TT], in_=C_fm), reads=CfB)

        ACT, DVE, PE, POOL = "scalar", "vector", "tensor", "gpsimd"

        def ssd_ctx(t, j):
            cidx = (t - 1) * 4 + j
            p = cidx % 2
            return dict(t=t, j=j, cs=slice(j * 128, (j + 1) * 128), cidx=cidx, p=p, SB_=SSB, dtv=SS[:, 0, j, :], dta=SS[:, 1, j, :], ea=SS[:, 3, j, :], cd=SS[:, 5, j, :], ddo=SS[:, 6, j, :],
                        E=E2[p], EB=E2B[p], cbm=cbm2[p], cbmB=cbm2B[p], xdt=xdt2[p], xdtB=xdt2B[p],
                        xw=xw2[p], xwB=xw2B[p], Btm=Btm2[p], BtmB=Btm2B[p])

        def smalls_tile(t):
            fl = lambda ap: ap.rearrange("p a b -> p (a b)")
            P.op(ACT, lambda e: e.activation(out=fl(dtp), in_=fl(dtp), func=AF.Exp), reads=[dtpB], writes=[dtpB])
            P.op(ACT, lambda e: e.activation(out=fl(SS[:, 0]), in_=fl(dtp), func=AF.Ln, bias=one_c[:, 0:1]), reads=[dtpB, onecB], writes=[SSB])
            P.op(DVE, lambda e: e.tensor_tensor(out=SS[:, 1], in0=SS[:, 0], in1=a_bc.rearrange("p (o h) -> p o h", o=1).broadcast_to([128, 4, 16]), op=ALU.mult),
                 reads=[SSB, abcB], writes=[SSB])
            P.op(DVE, lambda e: e.tensor_copy(out=dhl[:, 0], in_=SS[:, 1]), reads=[SSB], writes=[dhlB])
            P.op(DVE, lambda e: e.tensor_tensor(out=dhl[:, 1], in0=SS[:, 1], in1=dhl[:, 0], op=ALU.subtract), reads=[SSB], writes=[dhlB])
            acs_ps, tot_ps = banks[2][:, 64:128], banks[2][:, 0:64]
            P.op(PE, lambda e: e.matmul(acs_ps, lhsT=cm[:, 0, :], rhs=fl(SS[:, 1]), start=True, stop=True), reads=[SSB, cmB], writes=[bankB[2]])
            P.op(PE, lambda e: e.matmul(tot_ps, lhsT=cm[:, 2, :], rhs=fl(SS[:, 1]), start=True, stop=True), reads=[SSB, cmB], writes=[bankB[2]])
            P.op(DVE, lambda e: e.tensor_copy(out=fl(SS[:, 2]), in_=acs_ps), reads=[bankB[2]], writes=[SSB])
            for j in range(4):
                P.op(DVE, lambda e, j=j: e.tensor_tensor(out=dtp[:, j, :], in0=acs_ps[:, j * 16:(j + 1) * 16], in1=runtot, op=ALU.add), reads=[bankB[2], runtotB], writes=[dtpB])
                P.op(DVE, lambda e, j=j: e.tensor_tensor(out=runtot, in0=tot_ps[:, j * 16:(j + 1) * 16], in1=runtot, op=ALU.add), reads=[bankB[2]], writes=[runtotB])
            P.op(DVE, lambda e: e.tensor_tensor(out=fl(SS[:, 4]), in0=tot_ps, in1=fl(SS[:, 2]), op=ALU.subtract), reads=[bankB[2]], writes=[SSB])
            P.op(ACT, lambda e: e.activation(out=fl(SS[:, 5]), in_=tot_ps, func=AF.Exp), reads=[bankB[2]], writes=[SSB])
            P.op(ACT, lambda e: e.activation(out=fl(SS[:, 3]), in_=fl(SS[:, 2]), func=AF.Exp), reads=[SSB], writes=[SSB])
            c0 = (t - 1) * 4
            P.op(ACT, lambda e: e.activation(out=eag_all[:, c0:c0 + 4, :], in_=dtp, func=AF.Exp), reads=[dtpB], writes=[eagB])
            P.op(ACT, lambda e: e.activation(out=fl(SS[:, 4]), in_=fl(SS[:, 4]), func=AF.Exp), reads=[SSB], writes=[SSB])
            P.op(DVE, lambda e: e.tensor_tensor(out=SS[:, 6], in0=SS[:, 0], in1=SS[:, 4], op=ALU.mult), reads=[SSB], writes=[SSB])

        def a3(X, qd):
            j, E, EB = X["j"], X["E"], X["EB"]
            rh, rhB, rl, rlB = Rh[qd % 2], RhB[qd % 2], Rl[qd % 2], RlB[qd % 2]
            sbk = 3 + qd % 2
            P.op(POOL, lambda e: e.tensor_tensor(out=rh, in0=bc_last(dhl[:, 0, j, 4 * qd:4 * qd + 4], 128), in1=bc_mid(cm[:, 0, :], 4), op=ALU.mult),
                 reads=[dhlB, cmB], writes=[rhB])
            P.op(DVE, lambda e: e.tensor_tensor(out=rl, in0=bc_last(dhl[:, 1, j, 4 * qd:4 * qd + 4], 128), in1=bc_mid(cm[:, 0, :], 4), op=ALU.mult),
                 reads=[dhlB, cmB], writes=[rlB])
            P.op(PE, lambda e: e.matmul(banks[sbk][:, :], lhsT=lmb, rhs=rh.rearrange("p a b -> p (a b)"), start=True, stop=False),
                 reads=[rhB, lmbB], writes=[bankB[sbk]], inc=False)
            P.op(PE, lambda e: e.matmul(banks[sbk][:, :], lhsT=lmb, rhs=rl.rearrange("p a b -> p (a b)"), start=False, stop=True),
                 reads=[rlB, lmbB], writes=[bankB[sbk]])
            P.op(ACT, lambda e: e.activation(out=E[:, 4 * qd:4 * qd + 4, :], in_=banks[sbk][:, :].rearrange("p (a b) -> p a b", a=4), func=AF.Exp),
                 reads=[bankB[sbk]], writes=[EB])

        def a4(X):
            cs, cbm, cbmB = X["cs"], X["cbm"], X["cbmB"]
            for g in range(2):
                P.op(PE, lambda e, g=g: e.matmul(banks[2][:, 128 + g * 128:256 + g * 128], lhsT=B_fm[:, g, cs], rhs=C_fm[:, g, cs], start=True, stop=True),
                     reads=[BfB[g], CfB[g]], writes=[bankB[2]])
            P.op(DVE, lambda e: e.tensor_tensor(out=cbm, in0=banks[2][:, 128:384].rearrange("p (g l) -> p g l", g=2), in1=bc_mid(cm[:, 0, :], 2), op=ALU.mult),
                 reads=[bankB[2], cmB], writes=[cbmB])
            for g in range(2):
                P.op(PE, lambda e, g=g: e.transpose(out=b2bf[:, 768 + g * 128:896 + g * 128], in_=B_fm[:, g, cs], identity=ident),
                     reads=[BfB[g], identB_], writes=[bankB[2]])
            Btm, BtmB = X["Btm"], X["BtmB"]
            P.op(ACT, lambda e: e.activation(out=Btm.rearrange("p a b -> p (a b)"), in_=b2bf[:, 768:1024], func=AF.Copy), reads=[bankB[2]], writes=[BtmB])

        def a5(X):
            E, EB, cbm, cbmB = X["E"], X["EB"], X["cbm"], X["cbmB"]
            for g in range(2):
                P.op(DVE, lambda e, g=g: e.tensor_tensor(out=E[:, 8 * g:8 * g + 8, :], in0=E[:, 8 * g:8 * g + 8, :], in1=cbm[:, g:g + 1, :].broadcast_to([128, 8, 128]), op=ALU.mult),
                     reads=[cbmB], writes=[EB])

        def a6(X):
            cs, dtv, ddo, SB_ = X["cs"], X["dtv"], X["ddo"], X["SB_"]
            for kt in range(8):
                P.op(PE, lambda e, kt=kt: e.transpose(out=b7bf[:, kt * 128:(kt + 1) * 128], in_=xs_fm[:, kt, cs], identity=ident),
                     reads=[xsB[kt], identB_], writes=[bankB[7]], inc=(kt == 7))
            xsT = b7bf.rearrange("p (h d) -> p h d", h=16)
            xdt, xdtB, xw, xwB = X["xdt"], X["xdtB"], X["xw"], X["xwB"]
            P.op(DVE, lambda e: e.tensor_tensor(out=xdt, in0=xsT, in1=bc_last(dtv, 64), op=ALU.mult), reads=[bankB[7], SB_], writes=[xdtB])
            P.op(DVE, lambda e: e.tensor_tensor(out=xw, in0=xsT, in1=bc_last(ddo, 64), op=ALU.mult), reads=[bankB[7], SB_], writes=[xwB])

        def b1(X, hh):
            cs, ea, SB_, E, EB, xdt, xdtB, cidx = X["cs"], X["ea"], X["SB_"], X["E"], X["EB"], X["xdt"], X["xdtB"], X["cidx"]
            for hq in range(8):
                h = 8 * hh + hq
                P.op(PE, lambda e, h=h, hq=hq: e.matmul(banks[5][:, hq * 64:(hq + 1) * 64], lhsT=E[:, h, :], rhs=xdt[:, h, :], start=(hq == 0), stop=False,
                                                        skip_group_check=True),
                     reads=[EB, xdtB], writes=[bankB[5]], inc=False)
            for kq in range(4):
                kt = 4 * hh + kq
                P.op(PE, lambda e, kt=kt, kq=kq: e.matmul(banks[5][:, kq * 128:(kq + 1) * 128], lhsT=xsD_fm[:, kt, cs], rhs=ident, start=False, stop=(kq == 3),
                                                          skip_group_check=True),
                     reads=[xsDB[kt], identB_], writes=[bankB[5]], inc=(kq == 3))
            P.op(PE, lambda e: e.matmul(banks[6][:, :], lhsT=C_fm[:, hh, cs], rhs=Hbf[:, 8 * hh:8 * hh + 8, :].rearrange("p a b -> p (a b)"), start=True, stop=True),
                 reads=[CfB[hh], HbfB], writes=[bankB[6]])
            P.op(DVE, lambda e: e.tensor_tensor(out=t1[:, hh * 512:(hh + 1) * 512].rearrange("p (a b) -> p a b", a=8),
                                                in0=banks[6][:, :].rearrange("p (a b) -> p a b", a=8),
                                                in1=bc_last(ea[:, 8 * hh:8 * hh + 8], 64), op=ALU.mult),
                 reads=[bankB[6], SB_], writes=[t1B])
            P.op(DVE, lambda e: e.tensor_tensor(out=yl1[:, hh * 512:(hh + 1) * 512], in0=banks[5][:, :], in1=t1[:, hh * 512:(hh + 1) * 512], op=ALU.add),
                 reads=[bankB[5], t1B], writes=[yl1B])
            if hh == 1:
                P.dma("sync", "st", lambda e: e.dma_start(out=ylS[cidx * 128:(cidx + 1) * 128, :], in_=yl1), reads=[yl1B])

        def b2(X, g):
            cd, SB_, xw, xwB, Btm, BtmB = X["cd"], X["SB_"], X["xw"], X["xwB"], X["Btm"], X["BtmB"]
            P.op(PE, lambda e: e.matmul(banks[6][:, :], lhsT=Btm[:, g, :], rhs=xw[:, 8 * g:8 * g + 8, :].rearrange("p a b -> p (a b)"), start=True, stop=True),
                 reads=[BtmB, xwB], writes=[bankB[6]])
            P.op(POOL, lambda e: e.tensor_tensor(out=H[:, 8 * g:8 * g + 8, :], in0=H[:, 8 * g:8 * g + 8, :], in1=bc_last(cd[:, 8 * g:8 * g + 8], 64), op=ALU.mult),
                 reads=[SB_], writes=[HB])
            P.op(DVE, lambda e: e.tensor_tensor(out=H[:, 8 * g:8 * g + 8, :], in0=banks[6][:, :].rearrange("p (a b) -> p a b", a=8), in1=H[:, 8 * g:8 * g + 8, :], op=ALU.add),
                 reads=[bankB[6]], writes=[HB])
            if g == 1:
                P.op(ACT, lambda e: e.activation(out=Hbf, in_=H, func=AF.Copy), reads=[HB], writes=[HbfB])

        def ssd_tile(t):
            Xs = [ssd_ctx(t, j) for j in range(4)]

            def run(XA, XB):
                if XB is not None:
                    b1(XB, 0)
                if XA is not None:
                    a3(XA, 0); a3(XA, 1)
                if XB is not None:
                    b1(XB, 1)
                if XA is not None:
                    a4(XA); a3(XA, 2); a3(XA, 3)
                if XB is not None:
                    b2(XB, 0)
                if XA is not None:
                    a5(XA); a6(XA)
                if XB is not None:
                    b2(XB, 1)

            smalls_tile(t)
            run(Xs[0], None)
            for j in range(4):
                run(Xs[j + 1] if j + 1 < 4 else None, Xs[j])

        def attn_tile(t, fillers=()):
            def pblk(lb):
                return ((t + 1) % 2) * 4 + lb if lb < 4 else (t % 2) * 4 + (lb - 4)

            items = [(m, h) for m in range(4) for h in range(8)]

            def qk(i):
                m, h = items[i]
                qs = slice(m * 128, (m + 1) * 128)
                hp, po = h // 2, (h % 2) * 64
                sA, sB2 = (3, 4) if i % 2 == 0 else (5, 6)
                for jj in range(5):
                    pb = pblk(m + jj)
                    o = banks[sA][:, jj * 128:(jj + 1) * 128] if jj < 4 else banks[sB2][:, 0:128]
                    ob = bankB[sA] if jj < 4 else bankB[sB2]
                    P.op("tensor", lambda e, o=o, pb=pb: e.matmul(o, lhsT=k_fm[po:po + 64, hp, pb * 128:(pb + 1) * 128], rhs=q_fm[po:po + 64, hp, qs], start=True, stop=True),
                         reads=[kB[pb // 4], qB], writes=[ob], inc=(jj >= 3))

            def expmul(i):
                m, h = items[i]
                sA, sB2 = (3, 4) if i % 2 == 0 else (5, 6)
                Pv, PvB = Pb[i % 2], PbB[i % 2]
                P.op("scalar", lambda e: e.activation(out=Pv[:, 0:4, :], in_=banks[sA][:, :].rearrange("p (a b) -> p a b", a=4), func=AF.Exp),
                     reads=[bankB[sA]], writes=[PvB])
                P.op("scalar", lambda e: e.activation(out=Pv[:, 4, :], in_=banks[sB2][:, 0:128], func=AF.Exp), reads=[bankB[sB2]], writes=[PvB])
                meng = "vector" if i % 2 == 0 else "gpsimd"
                P.op(meng, lambda e: e.tensor_tensor(out=Pv, in0=Pv, in1=ET[:, h, :, :], op=ALU.mult), reads=[ETB], writes=[PvB])

            def pv(i):
                m, h = items[i]
                Pv, PvB = Pb[i % 2], PbB[i % 2]
                OB = 7 if h < 4 else 2
                oc = (h % 4) * 65
                for jj in range(5):
                    pb = pblk(m + jj)
                    P.op("tensor", lambda e, jj=jj, pb=pb: e.matmul(banks[OB][:, oc:oc + 65], lhsT=Pv[:, jj, :], rhs=v_tm[:, pb, h, :],
                                                                     start=(jj == 0), stop=(jj == 4), skip_group_check=True),
                         reads=[PvB, vB[pb]], writes=[bankB[OB]], inc=(jj == 4))
                if h % 4 == 3:
                    g = h // 4
                    ab = att[m % 2]
                    ov = banks[OB][:, 0:260].rearrange("p (a b) -> p a b", a=4)
                    P.op("vector", lambda e: e.reciprocal(out=rec[:, 4 * g:4 * g + 4], in_=ov[:, :, 64:65].rearrange("p a o -> p (a o)")),
                         reads=[bankB[OB]], writes=[recB])
                    P.op("vector", lambda e: e.tensor_tensor(out=ab[:, 4 * g:4 * g + 4, :], in0=ov[:, :, 0:64], in1=bc_last(rec[:, 4 * g:4 * g + 4], 64), op=ALU.mult),
                         reads=[bankB[OB], recB], writes=[attB[m % 2]])
                    if h == 7:
                        r0 = (t - 1) * TT + m * 128
                        P.dma("sync", "st", lambda e: e.dma_start(out=attS[r0:r0 + 128, :], in_=ab.rearrange("p a b -> p (a b)")), reads=[attB[m % 2]])

            nf = len(fillers)
            fi = 0
            qk(0)
            for i in range(len(items)):
                if i + 1 < len(items):
                    qk(i + 1)
                expmul(i)
                while fi < nf and fi * len(items) <= i * nf:
                    fillers[fi]()
                    fi += 1
                pv(i)
            while fi < nf:
                fillers[fi]()
                fi += 1

        load_hm(0)
        for t in range(NTH):
            if t + 1 < NTH:
                load_hm(t + 1)
            projections(t)
            if t == 0:
                build_ET()
                continue
            attn_tile(t)
            ssd_tile(t)
        P.dma("sync", "st", lambda e: e.dma_start(out=hloc, in_=H.rearrange("p a b -> p (a b)")), reads=[HB], writes=[hlocB])
        if debug_out:
            P.dma("sync", "st", lambda e: e.dma_start(out=hdbg, in_=H.rearrange("p a b -> p (a b)")), reads=[HB])
        P.dma("gpsimd", "cc", lambda e: e.collective_compute("AllGather", ALU.bypass, replica_groups=RG, ins=[hloc.opt()], outs=[hpair.opt()]),
              reads=[hlocB], writes=[hpairB], incv=-1)

    def mixer_pass2():
        wout = sb.view(WOUT_OFF, [128, 12, D], BF16)
        c = Carver(sb, ARENA, SB_TOTAL - PERS)
        A = 16
        x1t = c.get([128, 8, TT], F32, A)
        yl = [c.get([128, 1024], F32, A) for _ in range(2)]
        szb = [c.get([128, 1024], BF16, A) for _ in range(2)]
        attb = [c.get([128, 512], BF16, A) for _ in range(2)]
        cfm = [c.get([128, 2, TT], BF16, A) for _ in range(2)]
        gb = [c.get([128, 1024], F32, A) for _ in range(2)]
        gn = [c.get([128, 1024], BF16, A) for _ in range(2)]
        gT = c.get([128, 12, TT], BF16, A)
        Hin = c.get([128, 1024], F32, A)
        Hinb = c.get([128, 2, 512], BF16, A)
        ssd = c.get([128, 1024], F32, A)
        ident = c.get([128, 128], BF16, A)
        sm = [c.get([128, 4], F32, A) for _ in range(2)]
        x1B = Buf("x1t")
        ylB = [Buf("yl0"), Buf("yl1")]
        szB = [Buf("sz0"), Buf("sz1")]
        atB = [Buf("at0"), Buf("at1")]
        cfB = [Buf("cf0"), Buf("cf1")]
        gbB = [Buf("gb0"), Buf("gb1")]
        gnB = [Buf("gn0"), Buf("gn1")]
        gTB, HinB, HinbB, ssdB, idB = Buf("gT"), Buf("Hin"), Buf("Hinb"), Buf("ssd"), Buf("ident")
        smB = [Buf("sm0"), Buf("sm1")]
        b2bf = banks[2][:, :].bitcast(BF16)
        b3bf = banks[3][:, :].bitcast(BF16)

        P.dma("gpsimd", "wq", lambda e: e.dma_start(out=wout, in_=w_out.rearrange("(kt p) d -> p kt d", p=128)), writes=[woutB])
        P.dma("gpsimd", "wq", lambda e: e.dma_start(out=ident, in_=identb), writes=[idB])
        if 4 in phases:
            load_ffn_weights(w2g, w2u, w2d, do_parts=(0,))
        P.dma("sync", "ld", lambda e: e.dma_start(out=ssd, in_=ssdn), writes=[ssdB])
        P.dma("sync", "ld", lambda e: e.dma_start(out=Hin, in_=hpair[0:128, :]), reads=[hpairB], writes=[HinB])
        if debug_out:
            P.dma("sync", "st", lambda e: e.dma_start(out=hindbg, in_=Hin), reads=[HinB])
        P.op("vector", lambda e: e.tensor_scalar(out=Hinb.rearrange("p a b -> p (a b)"), in0=Hin, scalar1=flag_sb[:, 0:1], scalar2=None, op0=ALU.mult),
             reads=[HinB, flagB], writes=[HinbB])

        x1v = x1S.rearrange("(kt p) t -> p kt t", p=128)
        x2v = x2S.rearrange("(kt p) t -> p kt t", p=128)
        cSv = cS.rearrange("(g p) t -> p g t", p=128)
        def p3_loads(cidx):
            i = cidx % 2
            r0 = cidx * 128
            P.dma("sync", "ld", lambda e: e.dma_start(out=yl[i], in_=ylS[r0:r0 + 128, :]), writes=[ylB[i]])
            P.dma("sync", "ld", lambda e: e.dma_start(out=szb[i], in_=szS[r0:r0 + 128, :]), writes=[szB[i]])
            P.dma("sync", "ld", lambda e: e.dma_start(out=attb[i], in_=attS[r0:r0 + 128, :]), writes=[atB[i]])

        def p3_corr(cidx):
            tt, j, i = cidx // 4, cidx % 4, cidx % 2
            cf, cfb = cfm[tt % 2], cfB[tt % 2]
            for g in range(2):
                P.op("tensor", lambda e, g=g: e.matmul(banks[g][:, :], lhsT=cf[:, g, j * 128:(j + 1) * 128], rhs=Hinb[:, g, :], start=True, stop=True),
                     reads=[cfb, HinbB], writes=[bankB[g]])
                P.op("vector", lambda e, g=g: e.tensor_tensor(out=gb[i][:, g * 512:(g + 1) * 512].rearrange("p (a b) -> p a b", a=8),
                                                              in0=banks[g][:, :].rearrange("p (a b) -> p a b", a=8),
                                                              in1=bc_last(eag_all[:, cidx, 8 * g:8 * g + 8], 64), op=ALU.mult),
                     reads=[bankB[g], eagB], writes=[gbB[i]])

        def p3_chain1(cidx):
            i = cidx % 2
            P.op("vector", lambda e: e.tensor_tensor(out=gb[i], in0=gb[i], in1=yl[i], op=ALU.add), reads=[ylB[i]], writes=[gbB[i]])
            P.op("gpsimd", lambda e: e.tensor_tensor(out=gb[i], in0=gb[i], in1=szb[i], op=ALU.mult), reads=[szB[i]], writes=[gbB[i]])

        def p3_chain(cidx):
            i = cidx % 2
            P.op("scalar", lambda e: e.activation(out=gn[i], in_=gb[i], func=AF.Square, accum_out=sm[i][:, 0:1]), reads=[gbB[i]], writes=[gnB[i], smB[i]])
            P.op("vector", lambda e: e.tensor_scalar(out=sm[i][:, 1:2], in0=sm[i][:, 0:1], scalar1=1.0 / 1024.0, scalar2=EPS, op0=ALU.mult, op1=ALU.add),
                 reads=[smB[i]], writes=[smB[i]])
            P.op("scalar", lambda e: e.activation(out=sm[i][:, 2:3], in_=sm[i][:, 1:2], func=AF.Sqrt), reads=[smB[i]], writes=[smB[i]])
            P.op("vector", lambda e: e.reciprocal(out=sm[i][:, 3:4], in_=sm[i][:, 2:3]), reads=[smB[i]], writes=[smB[i]])
            P.op("vector", lambda e: e.scalar_tensor_tensor(out=gn[i], in0=gb[i], scalar=sm[i][:, 3:4], in1=ssd, op0=ALU.mult, op1=ALU.mult),
                 reads=[gbB[i], smB[i], ssdB], writes=[gnB[i]])

        def p3_tr(cidx):
            j, i = cidx % 4, cidx % 2
            for kt in range(8):
                P.op("tensor", lambda e, kt=kt: e.transpose(out=b2bf[:, kt * 128:(kt + 1) * 128], in_=gn[i][:, kt * 128:(kt + 1) * 128], identity=ident),
                     reads=[gnB[i], idB], writes=[bankB[2]], inc=(kt == 7))
            for kt in range(4):
                P.op("tensor", lambda e, kt=kt: e.transpose(out=b3bf[:, kt * 128:(kt + 1) * 128], in_=attb[i][:, kt * 128:(kt + 1) * 128], identity=ident),
                     reads=[atB[i], idB], writes=[bankB[3]], inc=(kt == 3))
            P.op("scalar", lambda e: e.activation(out=gT[:, 0:8, j * 128:(j + 1) * 128], in_=b2bf.rearrange("p (a b) -> p a b", a=8), func=AF.Copy),
                 reads=[bankB[2]], writes=[gTB])
            P.op("scalar", lambda e: e.activation(out=gT[:, 8:12, j * 128:(j + 1) * 128], in_=b3bf[:, 0:512].rearrange("p (a b) -> p a b", a=4), func=AF.Copy),
                 reads=[bankB[3]], writes=[gTB])

        def p3_wout(tt):
            for dtile in range(8):
                b = 4 + dtile % 2
                for kt in range(12):
                    P.op("tensor", lambda e, kt=kt, b=b, dtile=dtile: e.matmul(banks[b][:, :], lhsT=wout[:, kt, dtile * 128:(dtile + 1) * 128], rhs=gT[:, kt, :],
                                                                               start=(kt == 0), stop=(kt == 11)),
                         reads=[gTB, woutB], writes=[bankB[b]], inc=(kt == 11))
                P.op("vector", lambda e, b=b, dtile=dtile: e.tensor_tensor(out=x1t[:, dtile, :], in0=banks[b][:, :], in1=x1t[:, dtile, :], op=ALU.add),
                     reads=[bankB[b]], writes=[x1B])
            P.dma("sync", "st", lambda e: e.dma_start(out=x2v[:, :, tt * TT:(tt + 1) * TT], in_=x1t), reads=[x1B])

        def p3_tile_loads(tt):
            P.dma("sync", "ld", lambda e: e.dma_start(out=x1t, in_=x1v[:, :, (tt + 1) * TT:(tt + 2) * TT]), writes=[x1B])

        def p3_cf_load(tt):
            cf, cfb = cfm[tt % 2], cfB[tt % 2]
            P.dma("sync", "ld", lambda e: e.dma_start(out=cf, in_=cSv[:, :, tt * TT:(tt + 1) * TT]), writes=[cfb])

        NCHK = NT * 4
        p3_cf_load(0)
        if NT > 1:
            p3_cf_load(1)
        p3_loads(0)
        if NCHK > 1:
            p3_loads(1)
        p3_tile_loads(0)
        p3_corr(0)
        p3_chain1(0)
        for cidx in range(NCHK):
            tt, j = cidx // 4, cidx % 4
            if j == 0 and tt > 0 and tt + 1 < NT:
                p3_cf_load(tt + 1)
            if cidx + 1 < NCHK:
                p3_corr(cidx + 1)
                p3_chain1(cidx + 1)
            if j == 0 and tt > 0:
                p3_wout(tt - 1)
                p3_tile_loads(tt)
            p3_chain(cidx)
            p3_tr(cidx)
            if cidx + 2 < NCHK:
                p3_loads(cidx + 2)
        p3_wout(NT - 1)
        if 4 in phases:
            wg, wu, wd = ffn_weight_views()
            dsrc = w2d.rearrange("(ft p) d -> p ft d", p=128)
            P.dma("gpsimd", "wq", lambda e: e.dma_start(out=wd[:, 10:NFT, :], in_=dsrc[:, 10:NFT, :]), writes=[wdB[1], woutB])

    if 1 in phases:
        load_ffn_weights(w1g, w1u, w1d)
        ffn_phase(xT, NTH, 0, 8, "mix", x1S, hmS, 0)
    if 2 in phases:
        load_w_in()
        P.fence(exclude=("wq",))
        mixer_pass1()
        P.fence(exclude=("wq",))
    if 3 in phases:
        mixer_pass2()
        P.fence(exclude=("wq",))
    if 4 in phases:
        if 3 not in phases:
            load_ffn_weights(w2g, w2u, w2d)
        ffn_phase(x2S if (3 in phases) else x1S[:, TT:], NT, 16, 24, "final", outT, None, 0)

    P.wait_all("sync", P.dma_toks() + [("cc", P.cnt["cc"])])

    with nc.Block() as block:
        run = P.emit(sems)
        block.sync(run("sync"))
        block.scalar(run("scalar"))
        block.tensor(run("tensor"))
        block.vector(run("vector"))
        block.gpsimd(run("gpsimd"))
    es.close()
    return nc


def make_in_maps(inputs, NT=8, n_cores=8):
    f = lambda a: np.ascontiguousarray(np.asarray(a, dtype=np.float32))
    x = f(inputs["x"])
    TOK = NT * TT
    shared = {
        "w1g": f(inputs["ffn1_w_gate"][0]), "w1u": f(inputs["ffn1_w_up"][0]), "w1d": f(inputs["ffn1_w_down"][0]),
        "w2g": f(inputs["ffn2_w_gate"][0]), "w2u": f(inputs["ffn2_w_up"][0]), "w2d": f(inputs["ffn2_w_down"][0]),
        "w_in": f(inputs["w_in"][0]), "w_out": f(inputs["w_out"][0]),
    }
    g = np.stack([f(inputs["ffn1_norm"][0]), f(inputs["mix_norm"][0]), f(inputs["ffn2_norm"][0]), f(inputs["final_norm"])])
    shared["gains"] = np.ascontiguousarray(g.reshape(4, 8, 128).transpose(2, 0, 1).reshape(128, 32))
    cw = np.concatenate([f(inputs["conv_w"][0]).T, f(inputs["conv_b"][0])[:, None]], axis=1)
    shared["convp"] = np.ascontiguousarray(cw.reshape(12, 128, 5).transpose(1, 0, 2).reshape(128, 60))
    hv = np.concatenate([f(inputs["dt_bias"][0]), f(inputs["a_log"][0]), f(inputs["d_skip"][0])])
    shared["hvec"] = np.ascontiguousarray(np.tile(hv[None, :], (128, 1)))
    shared["dcol"] = np.ascontiguousarray(np.repeat(f(inputs["d_skip"][0]), 64).reshape(8, 128).T)
    s_ = np.arange(128)
    triU = (s_[:, None] <= s_[None, :]).astype(np.float32)
    Lmat = (s_[:, None] > s_[None, :]).astype(np.float32)
    shared["cmats"] = np.ascontiguousarray(np.concatenate([triU, Lmat, np.ones((128, 128), np.float32)], axis=1))
    shared["identb"] = np.eye(128, dtype=np.float32)
    k_ = np.arange(128)[:, None, None]
    j_ = np.arange(5)[None, :, None]
    q_ = np.arange(128)[None, None, :]
    kabs = j_ * 128 + k_
    rel = np.clip(512 + q_ - kabs, -256, 256) + 256
    rb = f(inputs["rel_bias"][0])
    shared["biasT"] = np.ascontiguousarray(rb[:, rel].transpose(1, 0, 2, 3).reshape(128, 8 * 5 * 128))
    kc, qc = kabs // 64, q_ // 64
    shared["amask"] = np.ascontiguousarray(((kc >= qc) & (kc <= qc + 8)).astype(np.float32).reshape(128, 5 * 128))
    shared["ssdn"] = np.ascontiguousarray(np.tile(f(inputs["ssd_norm"][0])[None, :], (128, 1)))
    maps = []
    for c in range(n_cores):
        b, half = c // 2, c % 2
        start = half * TOK
        rows = np.zeros((TOK + TT, D), np.float32)
        if half == 1:
            rows[:] = x[b, start - TT:start + TOK]
        else:
            rows[TT:] = x[b, 0:TOK]
        m = dict(shared)
        m["xT"] = np.ascontiguousarray(rows.T)
        m["flag"] = np.full((128, 1), float(half), np.float32)
        maps.append(m)
    return maps


_NC_CACHE = {}


def kernel(**inputs):
    NT = 8
    if "nc" not in _NC_CACHE:
        _NC_CACHE["nc"] = build_program(NT=NT)
    nc = _NC_CACHE["nc"]
    maps = make_in_maps(inputs, NT=NT, n_cores=8)
    res = run_bass_kernel_spmd(nc, maps, core_ids=list(range(8)))
    TOK = NT * TT
    out = np.empty((4, 2 * TOK, D), np.float32)
    for c in range(8):
        b, half = c // 2, c % 2
        out[b, half * TOK:(half + 1) * TOK, :] = np.asarray(res.results[c]["outT"]).T
    return out
```

```python
import numpy as np
from contextlib import ExitStack
import concourse.bass as bass
import concourse.mybir as mybir
from concourse.bass_utils import run_bass_kernel_spmd

F32 = mybir.dt.float32
BF16 = mybir.dt.bfloat16
AF = mybir.ActivationFunctionType
ALU = mybir.AluOpType

ENGS = ("tensor", "vector", "scalar", "gpsimd", "sync")

D = 1024
DFF = 2816
NFT = DFF // 128
TT = 512
EPS = 1e-5
PROJ = 4112
O_Z, O_XBC, O_DT, O_Q, O_K, O_V = 0, 1024, 2560, 2576, 3088, 3600


class Buf:
    __slots__ = ("name", "w", "r")

    def __init__(self, name):
        self.name = name
        self.w = None
        self.r = []


class Prog:
    def __init__(self, nc):
        self.nc = nc
        self.q = {e: [] for e in ENGS}
        self.cnt = {e: 0 for e in ENGS}
        self.waited = {e: {} for e in ENGS}
        self.semkeys = list(ENGS)
        self.same_engine_sync = True
        self.fam = {}
        self.famn = {}

    def new_sem(self, key):
        self.semkeys.append(key)
        self.cnt[key] = 0
        return key

    def _deps_for(self, reads, writes):
        deps = []
        for b in reads:
            if b.w is not None:
                deps.append(b.w)
        for b in writes:
            if b.w is not None:
                deps.append(b.w)
            deps.extend(b.r)
        return deps

    def _mark(self, tok, reads, writes):
        for b in reads:
            b.r.append(tok)
            if len(b.r) > 8:
                best = {}
                for (k, v) in b.r:
                    if best.get(k, 0) < v:
                        best[k] = v
                b.r = list(best.items())
        for b in writes:
            b.w = tok
            b.r = []

    def _waits(self, eng, alld, skip_same):
        waits = []
        for d in alld:
            if d is None:
                continue
            k, v = d
            if k == eng and (skip_same or v > self.cnt[eng]):
                continue
            if self.waited[eng].get(k, 0) < v:
                self.waited[eng][k] = v
                waits.append((k, v))
        return waits

    def op(self, eng, fn, reads=(), writes=(), deps=(), inc=True):
        alld = list(deps) + self._deps_for(reads, writes)
        skip_same = (eng == "tensor") or (not self.same_engine_sync)
        waits = self._waits(eng, alld, skip_same)
        tok = None
        if inc:
            self.cnt[eng] += 1
            tok = (eng, self.cnt[eng])
        else:
            tok = (eng, self.cnt[eng] + 1)
        self.q[eng].append((waits, fn, eng if inc else None, 1))
        self._mark(tok, reads, writes)
        return tok

    def dma(self, eng, semkey, fn, reads=(), writes=(), deps=(), incv=16):
        alld = list(deps) + self._deps_for(reads, writes)
        if semkey in self.fam:
            names = self.fam[semkey]
            key = names[self.famn[semkey] % len(names)]
            self.famn[semkey] += 1
            if self.cnt[key] > 0:
                alld.append((key, self.cnt[key]))
        else:
            key = semkey
        waits = self._waits(eng, alld, False)
        self.cnt[key] += (1 if incv == -1 else incv)
        tok = (key, self.cnt[key])
        self.q[eng].append((waits, fn, key, incv))
        self._mark(tok, reads, writes)
        return tok

    def new_family(self, fam, n):
        self.fam[fam] = [self.new_sem(f"{fam}{i}") for i in range(n)]
        self.famn[fam] = 0

    def dma_toks(self):
        out = []
        for names in self.fam.values():
            out.extend((k, self.cnt[k]) for k in names if self.cnt[k] > 0)
        return out

    def wait_all(self, eng, toks):
        waits = self._waits(eng, toks, False)
        if waits:
            self.q[eng].append((waits, None, None, 0))

    def fence(self, exclude=()):
        ex = set(exclude)
        for f_ in exclude:
            ex.update(self.fam.get(f_, []))
        toks = [(k, self.cnt[k]) for k in self.semkeys if self.cnt[k] > 0 and k not in ex]
        for e in ENGS:
            self.wait_all(e, [tk for tk in toks if tk[0] != e])

    def emit(self, sems):
        def run(engname):
            def body(e):
                for (waits, fn, inckey, incv) in self.q[engname]:
                    if fn is None:
                        for (k, v) in waits:
                            e.wait_ge(sems[k], v)
                        continue
                    for (k, v) in waits[1:]:
                        e.wait_ge(sems[k], v)
                    ins = fn(e)
                    if waits:
                        ins._wait_ge(sems[waits[0][0]], waits[0][1])
                    if inckey is not None:
                        if incv == -1:
                            ins.then_inc(sems[inckey])
                        else:
                            ins.then_inc(sems[inckey], incv)
            return body
        return run


class SB:
    def __init__(self, nc, nbytes):
        self.nbytes = nbytes
        self.t = nc.alloc_sbuf_tensor("sb_all", [128, nbytes // 2], BF16)

    def view(self, off, shape, dtype):
        assert off % 4 == 0
        esz = 4 if dtype == F32 else 2
        n = 1
        for s in shape[1:]:
            n *= s
        nb = n * esz
        assert off + nb <= self.nbytes, (off, nb, self.nbytes)
        v = self.t[:, off // 2:(off + nb) // 2]
        if dtype == F32:
            v = v.bitcast(F32)
        if len(shape) == 3:
            v = v.rearrange("p (a b) -> p a b", a=shape[1])
        elif len(shape) == 4:
            v = v.rearrange("p (a b c) -> p a b c", a=shape[1], b=shape[2])
        return v


class Carver:
    def __init__(self, sb, lo, hi):
        self.sb, self.lo, self.hi, self.cur = sb, lo, hi, lo

    def get(self, shape, dtype, align=32):
        self.cur = (self.cur + align - 1) // align * align
        esz = 4 if dtype == F32 else 2
        n = 1
        for s in shape[1:]:
            n *= s
        v = self.sb.view(self.cur, shape, dtype)
        self.cur += n * esz
        assert self.cur <= self.hi, ("SBUF carve overflow", self.cur, self.hi)
        return v


DBG = {}
ARENA = 3 * 8 * DFF * 2
SB_TOTAL = 212800
PERS = 2688
WIN_BYTES = 8 * PROJ * 2
WOUT_OFF = ARENA - 12 * D * 2


def bc_last(ap, n):
    p, a = ap.shape
    return ap.rearrange("p (a o) -> p a o", o=1).broadcast_to([p, a, n])


def bc_mid(ap, n):
    p, l = ap.shape
    return ap.rearrange("p (o l) -> p o l", o=1).broadcast_to([p, n, l])


def build_program(NT=8, phases=(1, 2, 3, 4), debug_out=False, n_cores=8):
    nc = bass.Bass("TRN2", target_bir_lowering=False)
    NTH = NT + 1
    TOK = NT * TT
    TOKH = NTH * TT
    NCH = NT * 4

    def din(name, shape, dt=F32):
        return nc.dram_tensor(name, list(shape), dt, kind="ExternalInput").ap()

    xT = din("xT", [D, TOKH])
    w1g, w1u, w1d = din("w1g", [D, DFF]), din("w1u", [D, DFF]), din("w1d", [DFF, D])
    w2g, w2u, w2d = din("w2g", [D, DFF]), din("w2u", [D, DFF]), din("w2d", [DFF, D])
    w_in = din("w_in", [D, PROJ])
    w_out = din("w_out", [1536, D])
    gains = din("gains", [128, 32])
    convp = din("convp", [128, 60])
    hvec = din("hvec", [128, 48])
    dcol = din("dcol", [128, 8])
    flag = din("flag", [128, 1])
    cmats = din("cmats", [128, 3 * 128])
    identb = din("identb", [128, 128])
    biasT = din("biasT", [128, 8 * 5 * 128])
    amask = din("amask", [128, 5 * 128])
    ssdn = din("ssdn", [128, 1024])
    outT = nc.dram_tensor("outT", [D, TOK], F32, kind="ExternalOutput").ap()

    kindS = "ExternalOutput" if debug_out else "Internal"
    x1S = nc.dram_tensor("x1S", [D, TOKH], F32, kind=kindS).ap()
    hmS = nc.dram_tensor("hmS", [D, TOKH], BF16, kind=kindS).ap()
    x2S = nc.dram_tensor("x2S", [D, TOK], F32, kind=kindS).ap()
    ylS = nc.dram_tensor("ylS", [TOK, 1024], F32, kind=kindS).ap()
    szS = nc.dram_tensor("szS", [TOK, 1024], BF16, kind=kindS).ap()
    attS = nc.dram_tensor("attS", [TOK, 512], BF16, kind=kindS).ap()
    cS = nc.dram_tensor("cS", [256, TOK], BF16, kind=kindS).ap()
    hloc = nc.dram_tensor("hloc", [128, 1024], F32, kind="Internal").ap()
    hpair = nc.dram_tensor("hpair", [256, 1024], F32, kind="Internal").ap()
    if debug_out:
        hdbg = nc.dram_tensor("hdbg", [128, 1024], F32, kind="ExternalOutput").ap()
        hindbg = nc.dram_tensor("hindbg", [128, 1024], F32, kind="ExternalOutput").ap()

    P = Prog(nc)
    P.new_family("ld", 24)
    P.new_family("st", 24)
    P.new_family("wq", 24)
    P.new_sem("cc")

    es = ExitStack()
    sb = SB(nc, SB_TOTAL)
    banks = [nc.alloc_psum_tensor(f"bank{i}", [128, 512], F32) for i in range(8)]
    bankB = [Buf(f"bank{i}") for i in range(8)]
    sems = {k: es.enter_context(nc.semaphore(k)) for k in P.semkeys}

    pers = Carver(sb, SB_TOTAL - PERS, SB_TOTAL)
    ones_s = pers.get([128, 128], BF16)
    gains_sb = pers.get([128, 32], F32)
    eps_sb = pers.get([128, 1], F32)
    one_c = pers.get([128, 1], F32)
    flag_sb = pers.get([128, 1], F32)
    runtot = pers.get([128, 16], F32)
    eag_all = pers.get([128, NCH, 16], F32)
    onesB, gainsB, epsB = Buf("ones_s"), Buf("gains"), Buf("eps")
    onecB, flagB, runtotB, eagB = Buf("one_c"), Buf("flag"), Buf("runtot"), Buf("eag")
    P.op("gpsimd", lambda e: e.memset(eps_sb, EPS), writes=[epsB])
    P.op("gpsimd", lambda e: e.memset(one_c, 1.0), writes=[onecB])
    P.op("gpsimd", lambda e: e.memset(runtot, 0.0), writes=[runtotB])
    P.op("gpsimd", lambda e: e.memset(ones_s, 1.0 / 1024.0), writes=[onesB])
    P.dma("sync", "ld", lambda e: e.dma_start(out=gains_sb, in_=gains), writes=[gainsB])
    P.dma("sync", "ld", lambda e: e.dma_start(out=flag_sb, in_=flag), writes=[flagB])

    def ffn_weight_views():
        wg = sb.view(0, [128, 8, DFF], BF16)
        wu = sb.view(8 * DFF * 2, [128, 8, DFF], BF16)
        wd = sb.view(2 * 8 * DFF * 2, [128, NFT, D], BF16)
        return wg, wu, wd

    NWC = 4
    WCW = DFF // NWC
    wgB = [Buf(f"wg{i}") for i in range(NWC)]
    wuB = [Buf(f"wu{i}") for i in range(NWC)]
    wdB = [Buf(f"wd{i}") for i in range(2)]

    def load_ffn_weights(gd, ud, dd, do_parts=(0, 1)):
        wg, wu, wd = ffn_weight_views()
        gsrc = gd.rearrange("(kt p) f -> p kt f", p=128)
        usrc = ud.rearrange("(kt p) f -> p kt f", p=128)
        dsrc = dd.rearrange("(ft p) d -> p ft d", p=128)
        for c in range(NWC):
            cs = slice(c * WCW, (c + 1) * WCW)
            P.dma("gpsimd", "wq", lambda e, cs=cs: e.dma_start(out=wg[:, :, cs], in_=gsrc[:, :, cs]), writes=[wgB[c]])
            P.dma("gpsimd", "wq", lambda e, cs=cs: e.dma_start(out=wu[:, :, cs], in_=usrc[:, :, cs]), writes=[wuB[c]])
        for part in do_parts:
            fs = slice(0, 10) if part == 0 else slice(10, NFT)
            P.dma("gpsimd", "wq", lambda e, fs=fs: e.dma_start(out=wd[:, fs, :], in_=dsrc[:, fs, :]), writes=[wdB[part]])

    def ffn_phase(src, ntiles, gcol_pre, gcol_post, post_mode, dst_x, dst_h, dst_tok0):
        wg, wu, wd = ffn_weight_views()
        c = Carver(sb, ARENA, SB_TOTAL - 1024)
        xt = [c.get([128, 8, TT], F32) for _ in range(2)]
        s1 = c.get([128, 8, TT], BF16)
        hb = c.get([128, 8, TT], BF16)
        hid = c.get([128, NFT, TT], BF16)
        rstd = c.get([128, TT], F32)
        xtB = [Buf("xt0"), Buf("xt1")]
        s1B, hB, rstdB = Buf("s1"), Buf("h"), Buf("rstd")
        hidB = [Buf(f"hid{i}") for i in range(NFT)]
        srcv = src.rearrange("(kt p) t -> p kt t", p=128)
        dxv = dst_x.rearrange("(kt p) t -> p kt t", p=128)
        dhv = dst_h.rearrange("(kt p) t -> p kt t", p=128) if dst_h is not None else None
        PB_G, PB_U, PB_D, PB_N = (0, 1), (2, 3), (4, 5), 6

        def load_x(t):
            b = t % 2
            P.dma("sync", "ld", lambda e: e.dma_start(out=xt[b], in_=srcv[:, :, t * TT:(t + 1) * TT]), writes=[xtB[b]])

        def norm_sq(t):
            b = t % 2
            P.op("scalar", lambda e: e.activation(out=s1, in_=xt[b], func=AF.Square), reads=[xtB[b]], writes=[s1B])

        def norm_mm(t):
            for kt in range(8):
                P.op("tensor", lambda e, kt=kt: e.matmul(banks[PB_N][:, :], lhsT=ones_s, rhs=s1[:, kt, :], start=(kt == 0), stop=(kt == 7)),
                     reads=[s1B, onesB], writes=[bankB[PB_N]], inc=(kt == 7))

        def norm_rstd(t):
            P.op("scalar", lambda e: e.activation(out=rstd, in_=banks[PB_N][:, :], func=AF.Sqrt, bias=eps_sb[:, 0:1]),
                 reads=[bankB[PB_N], epsB], writes=[rstdB])
            P.op("vector", lambda e: e.reciprocal(out=rstd, in_=rstd), reads=[rstdB], writes=[rstdB])

        def pre_scale(t):
            b = t % 2
            for kt in range(8):
                P.op("vector", lambda e, kt=kt: e.scalar_tensor_tensor(out=hb[:, kt, :], in0=xt[b][:, kt, :], scalar=gains_sb[:, gcol_pre + kt:gcol_pre + kt + 1],
                                                                        in1=rstd, op0=ALU.mult, op1=ALU.mult),
                     reads=[xtB[b], rstdB, gainsB], writes=[hB], inc=(kt == 7))

        def post_scale_store(t):
            b = t % 2
            ts = slice(dst_tok0 + t * TT, dst_tok0 + (t + 1) * TT)
            if post_mode == "mix":
                P.dma("sync", "st", lambda e: e.dma_start(out=dxv[:, :, ts], in_=xt[b]), reads=[xtB[b]])
                for kt in range(8):
                    P.op("vector", lambda e, kt=kt: e.scalar_tensor_tensor(out=s1[:, kt, :], in0=xt[b][:, kt, :], scalar=gains_sb[:, gcol_post + kt:gcol_post + kt + 1],
                                                                            in1=rstd, op0=ALU.mult, op1=ALU.mult),
                         reads=[xtB[b], rstdB, gainsB], writes=[s1B], inc=(kt == 7))
                return P.dma("sync", "st", lambda e: e.dma_start(out=dhv[:, :, ts], in_=s1), reads=[s1B])
            else:
                for kt in range(8):
                    P.op("vector", lambda e, kt=kt: e.scalar_tensor_tensor(out=xt[b][:, kt, :], in0=xt[b][:, kt, :], scalar=gains_sb[:, gcol_post + kt:gcol_post + kt + 1],
                                                                            in1=rstd, op0=ALU.mult, op1=ALU.mult),
                         reads=[rstdB, gainsB], writes=[xtB[b]], inc=(kt == 7))
                return P.dma("sync", "st", lambda e: e.dma_start(out=dxv[:, :, ts], in_=xt[b]), reads=[xtB[b]])

        def gate_up_ft(t, ft):
            pg, pu = PB_G[ft % 2], PB_U[ft % 2]
            fs = slice(ft * 128, (ft + 1) * 128)
            wcs = sorted(set([(ft * 128) // WCW, (ft * 128 + 127) // WCW]))
            for kt in range(8):
                P.op("tensor", lambda e, kt=kt: e.matmul(banks[pg][:, :], lhsT=wg[:, kt, fs], rhs=hb[:, kt, :], start=(kt == 0), stop=(kt == 7)),
                     reads=[hB] + [wgB[i] for i in wcs], writes=[bankB[pg]], inc=(kt == 7))
            for kt in range(8):
                P.op("tensor", lambda e, kt=kt: e.matmul(banks[pu][:, :], lhsT=wu[:, kt, fs], rhs=hb[:, kt, :], start=(kt == 0), stop=(kt == 7)),
                     reads=[hB] + [wuB[i] for i in wcs], writes=[bankB[pu]], inc=(kt == 7))
            P.op("scalar", lambda e: e.activation(out=hid[:, ft, :], in_=banks[pg][:, :], func=AF.Silu), reads=[bankB[pg]], writes=[hidB[ft]])
            P.op("vector", lambda e: e.tensor_tensor(out=hid[:, ft, :], in0=hid[:, ft, :], in1=banks[pu][:, :], op=ALU.mult),
                 reads=[bankB[pu]], writes=[hidB[ft]])

        def down(t):
            b = t % 2
            for dt_ in range(8):
                pd = PB_D[dt_ % 2]
                ds_ = slice(dt_ * 128, (dt_ + 1) * 128)
                for ft in range(NFT):
                    P.op("tensor", lambda e, ft=ft, pd=pd, ds_=ds_: e.matmul(banks[pd][:, :], lhsT=wd[:, ft, ds_], rhs=hid[:, ft, :], start=(ft == 0), stop=(ft == NFT - 1)),
                         reads=[hidB[ft], wdB[0 if ft < 10 else 1]], writes=[bankB[pd]], inc=(ft == NFT - 1))
                P.op("vector", lambda e, pd=pd, dt_=dt_: e.scalar_tensor_tensor(out=xt[b][:, dt_, :], in0=banks[pd][:, :], scalar=0.5, in1=xt[b][:, dt_, :], op0=ALU.mult, op1=ALU.add),
                     reads=[bankB[pd]], writes=[xtB[b]])

        last = None
        load_x(0)
        if ntiles > 1:
            load_x(1)
        norm_sq(0); norm_mm(0); norm_rstd(0); pre_scale(0)
        for t in range(ntiles + 1):
            for ft in range(NFT):
                if t < ntiles:
                    gate_up_ft(t, ft)
                if t >= 1:
                    if ft == 1:
                        norm_sq(t - 1)
                    if ft == 3:
                        norm_mm(t - 1)
                    if ft == 5:
                        norm_rstd(t - 1)
                        last = post_scale_store(t - 1)
                        if t + 1 < ntiles:
                            load_x(t + 1)
                if t + 1 < ntiles:
                    if ft == 13:
                        norm_sq(t + 1)
                    if ft == 16:
                        norm_mm(t + 1)
                    if ft == 18:
                        norm_rstd(t + 1)
            if t + 1 < ntiles:
                pre_scale(t + 1)
            if t < ntiles:
                down(t)
        return last

    WCH = [("k", O_K, O_K + 512), ("v", O_V, O_V + 512), ("xbc0", O_XBC, O_XBC + 768),
           ("xbc1", O_XBC + 768, O_XBC + 1536), ("dt", O_DT, O_DT + 16), ("q", O_Q, O_Q + 512),
           ("z0", O_Z, O_Z + 512), ("z1", O_Z + 512, O_Z + 1024)]
    winB = {n: Buf("win_" + n) for n, _, _ in WCH}
    woutB = Buf("wout")
    hlocB, hpairB = Buf("hloc"), Buf("hpair")
    ylSB, szSB, attSB, cSB, x2SB = Buf("ylS"), Buf("szS"), Buf("attS"), Buf("cS"), Buf("x2S")

    def win_buf(col):
        for n, lo, hi in WCH:
            if lo <= col < hi:
                return winB[n]
        raise AssertionError(col)

    def load_w_in():
        win = sb.view(0, [128, 8, PROJ], BF16)
        src = w_in.rearrange("(kt p) f -> p kt f", p=128)
        first = True
        for n, lo, hi in WCH:
            P.dma("gpsimd", "wq", lambda e, lo=lo, hi=hi: e.dma_start(out=win[:, :, lo:hi], in_=src[:, :, lo:hi]),
                  writes=[winB[n]] + ((wgB + wuB) if first else []))
            first = False

    RG = [[2 * i, 2 * i + 1] for i in range(n_cores // 2)]

    def mixer_pass1():
        win = sb.view(0, [128, 8, PROJ], BF16)
        c = Carver(sb, WIN_BYTES, SB_TOTAL - PERS)
        A = 16
        hm = [c.get([128, 8, TT], BF16, A) for _ in range(2)]
        xbc = c.get([128, 12, TT + 3], F32, A)
        xs_fm = c.get([128, 8, TT], BF16, A)
        xsD_fm = c.get([128, 8, TT], BF16, A)
        B_fm = c.get([128, 2, TT], BF16, A)
        C_fm = c.get([128, 2, TT], BF16, A)
        q_fm = c.get([128, 4, TT], BF16, A)
        k_fm = c.get([128, 4, 2 * TT], BF16, A)
        v_tm = c.get([128, 8, 8, 65], BF16, A)
        sz = [c.get([128, 1024], BF16, A)] * 2
        dtp = c.get([128, 4, 16], F32, A)
        SS = c.get([128, 7, 4, 16], F32, A)
        R = [c.get([128, 4, 128], F32, A) for _ in range(2)]
        E2 = [c.get([128, 16, 128], BF16, A) for _ in range(2)]
        E = E2[0]
        cbm2 = [c.get([128, 2, 128], BF16, A) for _ in range(2)]
        xdt2 = [c.get([128, 16, 64], BF16, A) for _ in range(2)]
        xw2 = [c.get([128, 16, 64], BF16, A) for _ in range(2)]
        Btm2 = [c.get([128, 2, 128], BF16, A) for _ in range(2)]
        H = c.get([128, 16, 64], F32, A)
        Hbf = c.get([128, 16, 64], BF16, A)
        big = c.get([128, 3, 1024], F32, A)
        t1 = big[:, 0, :]
        yl1 = big[:, 1, :]
        etmp = big.rearrange("p a b -> p (a b)")[:, 0:2560].rearrange("p (h j q) -> p h j q", h=4, j=5)
        mtmp = E.rearrange("p a b -> p (a b)")[:, 0:1280].bitcast(F32).rearrange("p (j q) -> p j q", j=5)
        ET = c.get([128, 8, 5, 128], BF16, A)
        Pb = [c.get([128, 5, 128], BF16, A) for _ in range(2)]
        att = [c.get([128, 8, 64], BF16, A)] * 2
        rec = c.get([128, 8], F32, A)
        cm = c.get([128, 3, 128], F32, A)
        ident = c.get([128, 128], BF16, A)
        hv = c.get([128, 48], F32, A)
        a_bc = c.get([128, 16], F32, A)
        cvp = c.get([128, 12, 5], F32, A)
        dcl = c.get([128, 8], F32, A)

        DBG.update({k_: v_ for k_, v_ in locals().items() if k_ not in ('c', 'win')})
        hmB = [Buf("hm0"), Buf("hm1")]
        xbcB = [Buf(f"xbc{i}") for i in range(12)]
        xsB = [Buf(f"xs{i}") for i in range(8)]
        xsDB = [Buf(f"xsD{i}") for i in range(8)]
        BfB = [Buf("Bf0"), Buf("Bf1")]
        CfB = [Buf("Cf0"), Buf("Cf1")]
        qB = Buf("q")
        kB = [Buf("k0"), Buf("k1")]
        vB = [Buf(f"v{i}") for i in range(8)]
        szB = [Buf("sz0")] * 2
        dtpB = Buf("dtp")
        SSB = Buf("SS")
        RB = [Buf("R0"), Buf("R1")]
        E2B = [Buf("E0"), Buf("E1")]
        EB = E2B[0]
        cbm2B = [Buf("cbm0"), Buf("cbm1")]
        xdt2B = [Buf("xdt0"), Buf("xdt1")]
        xw2B = [Buf("xw0"), Buf("xw1")]
        Btm2B = [Buf("Btm0"), Buf("Btm1")]
        HB, HbfB, t1B = Buf("H"), Buf("Hbf"), Buf("t1")
        yl1B = Buf("yl1")
        ylB = [yl1B, Buf("ylx")]
        ETB = Buf("ET")
        PbB = [Buf("P0"), Buf("P1")]
        attB = [Buf("att0")] * 2
        recB, cmB, identB_, hvB, abcB, cvpB, dclB = (Buf(n) for n in "rec cm ident hv abc cvp dcl".split())

        b2bf = banks[2][:, :].bitcast(BF16)
        b7bf = banks[7][:, :].bitcast(BF16)

        P.dma("sync", "ld", lambda e: e.dma_start(out=cm.rearrange("p a b -> p (a b)"), in_=cmats), writes=[cmB])
        P.dma("gpsimd", "wq", lambda e: e.dma_start(out=ident, in_=identb), writes=[identB_])
        P.dma("sync", "ld", lambda e: e.dma_start(out=hv, in_=hvec), writes=[hvB])
        P.dma("sync", "ld", lambda e: e.dma_start(out=cvp.rearrange("p a b -> p (a b)"), in_=convp), writes=[cvpB])
        P.dma("sync", "ld", lambda e: e.dma_start(out=dcl, in_=dcol), writes=[dclB])
        P.op("scalar", lambda e: e.activation(out=a_bc, in_=hv[:, 16:32], func=AF.Exp), reads=[hvB], writes=[abcB])
        P.op("vector", lambda e: e.tensor_scalar(out=a_bc, in0=a_bc, scalar1=-1.0, scalar2=None, op0=ALU.mult), reads=[abcB], writes=[abcB])
        P.op("gpsimd", lambda e: e.memset(H, 0.0), writes=[HB])
        P.op("gpsimd", lambda e: e.memset(Hbf, 0.0), writes=[HbfB])
        def build_ET():
            P.dma("sync", "ld", lambda e: e.dma_start(out=mtmp.rearrange("p a b -> p (a b)"), in_=amask), writes=[EB])
            for half in range(2):
                P.dma("sync", "ld", lambda e, half=half: e.dma_start(out=etmp.rearrange("p h j q -> p (h j q)"), in_=biasT[:, half * 2560:(half + 1) * 2560]),
                      writes=[t1B, ylB[0], ylB[1]])
                for hh in range(4):
                    h = half * 4 + hh
                    P.op("scalar", lambda e, hh=hh, h=h: e.activation(out=ET[:, h, :, :], in_=etmp[:, hh, :, :], func=AF.Exp),
                         reads=[t1B, ylB[0], ylB[1]], writes=[ETB])
                    P.op("vector", lambda e, h=h: e.tensor_tensor(out=ET[:, h, :, :], in0=ET[:, h, :, :], in1=mtmp, op=ALU.mult),
                         reads=[EB], writes=[ETB])

        rot = [0]

        def pbank():
            b = rot[0] % 2
            rot[0] += 1
            return b

        def load_hm(t):
            b = t % 2
            src = hmS.rearrange("(kt p) t -> p kt t", p=128)
            P.dma("sync", "ld", lambda e: e.dma_start(out=hm[b], in_=src[:, :, t * TT:(t + 1) * TT]), writes=[hmB[b]])

        def mm_fm(t, col0, ncols=TT, c0=0):
            b = pbank()
            hmv, hb_ = hm[t % 2], hmB[t % 2]
            for kt in range(8):
                P.op("tensor", lambda e, kt=kt: e.matmul(banks[b][:, 0:ncols], lhsT=win[:, kt, col0:col0 + 128], rhs=hmv[:, kt, c0:c0 + ncols],
                                                          start=(kt == 0), stop=(kt == 7)),
                     reads=[hb_, win_buf(col0)], writes=[bankB[b]], inc=(kt == 7))
            return b

        def mm_tm(t, tb, col0, ncols, out_ap, outB):
            hmv, hb_ = hm[t % 2], hmB[t % 2]
            for kt in range(8):
                P.op("tensor", lambda e, kt=kt: e.matmul(out_ap, lhsT=hmv[:, kt, tb * 128:(tb + 1) * 128], rhs=win[:, kt, col0:col0 + ncols],
                                                          start=(kt == 0), stop=(kt == 7)),
                     reads=[hb_, win_buf(col0)], writes=[outB], inc=(kt == 7))

        def projections(t):
            half = t % 2

            def k_unit(j):
                b = mm_fm(t, O_K + j * 128)
                P.op("scalar", lambda e: e.activation(out=k_fm[:, j, half * TT:(half + 1) * TT], in_=banks[b][:, :], func=AF.Copy),
                     reads=[bankB[b]], writes=[kB[half]])

            def v_unit(tb):
                b = pbank()
                blk = half * 4 + tb
                mm_tm(t, tb, O_V, 512, banks[b][:, :], bankB[b])
                P.op("vector", lambda e: e.tensor_copy(out=v_tm[:, blk, :, 0:64], in_=banks[b][:, :].rearrange("p (h d) -> p h d", h=8)),
                     reads=[bankB[b]], writes=[vB[blk]])
                if t == 0:
                    P.op("vector", lambda e: e.tensor_copy(out=v_tm[:, blk, :, 64:65],
                                                           in_=flag_sb.rearrange("p (a o) -> p a o", a=1).broadcast_to([128, 8, 1])),
                         reads=[flagB], writes=[vB[blk]])
                else:
                    P.op("gpsimd", lambda e: e.memset(v_tm[:, blk, :, 64:65], 1.0), writes=[vB[blk]])

            def xbc_unit(ct):
                b = mm_fm(t, O_XBC + ct * 128)
                P.op("scalar", lambda e: e.activation(out=xbc[:, ct, 3:TT + 3], in_=banks[b][:, :], func=AF.Copy),
                     reads=[bankB[b]], writes=[xbcB[ct]])

            def q_unit(j):
                b = mm_fm(t, O_Q + j * 128)
                P.op("scalar", lambda e: e.activation(out=q_fm[:, j, :], in_=banks[b][:, :], func=AF.Copy, scale=0.125),
                     reads=[bankB[b]], writes=[qB])

            def dt_unit(tb):
                mm_tm(t, tb, O_DT, 16, banks[2][:, tb * 16:(tb + 1) * 16], bankB[2])
                P.op("vector", lambda e: e.tensor_tensor(out=dtp[:, tb, :], in0=banks[2][:, tb * 16:(tb + 1) * 16], in1=hv[:, 0:16], op=ALU.add),
                     reads=[bankB[2], hvB], writes=[dtpB])

            def z_unit(tb, zh):
                i = tb % 2
                b = pbank()
                mm_tm(t, tb, O_Z + zh * 512, 512, banks[b][:, :], bankB[b])
                P.op("scalar", lambda e: e.activation(out=sz[i][:, zh * 512:(zh + 1) * 512], in_=banks[b][:, :], func=AF.Silu),
                     reads=[bankB[b]], writes=[szB[i]])
                if zh == 1:
                    r0 = (t - 1) * TT + tb * 128
                    P.dma("sync", "st", lambda e: e.dma_start(out=szS[r0:r0 + 128, :], in_=sz[i]), reads=[szB[i]])

            if t == 0:
                for j in range(4):
                    k_unit(j)
                for tb in range(4):
                    v_unit(tb)
                for ct in range(12):
                    b = mm_fm(t, O_XBC + ct * 128, ncols=16, c0=TT - 16)
                    P.op("scalar", lambda e, b=b, ct=ct: e.activation(out=xbc[:, ct, 0:3], in_=banks[b][:, 13:16], func=AF.Copy),
                         reads=[bankB[b]], writes=[xbcB[ct]])
                return
            for ct in range(12):
                xbc_unit(ct)
            rest = ([(lambda j=j: k_unit(j)) for j in range(4)] + [(lambda tb=tb: v_unit(tb)) for tb in range(4)]
                    + [(lambda j=j: q_unit(j)) for j in range(4)] + [(lambda tb=tb: dt_unit(tb)) for tb in range(4)]
                    + [(lambda tb=tb, zh=zh: z_unit(tb, zh)) for tb in range(4) for zh in range(2)])
            for idx, u in enumerate(rest):
                u()
                if idx % 2 == 1 and idx // 2 < 12:
                    conv_ct(t, idx // 2)
            conv_tail(t)

        def conv(t):
            for ct in range(12):
                conv_ct(t, ct)
            conv_tail(t)

        def conv_ct(t, ct):
            if True:
                ab = 3 + ct % 2
                acc = banks[ab][:, :]
                P.op("vector", lambda e, ct=ct, acc=acc: e.tensor_scalar(out=acc, in0=xbc[:, ct, 0:TT], scalar1=cvp[:, ct, 0:1], scalar2=cvp[:, ct, 4:5],
                                                                         op0=ALU.mult, op1=ALU.add),
                     reads=[xbcB[ct], cvpB], writes=[bankB[ab]])
                for k in range(1, 4):
                    P.op("vector", lambda e, ct=ct, acc=acc, k=k: e.scalar_tensor_tensor(out=acc, in0=xbc[:, ct, k:k + TT], scalar=cvp[:, ct, k:k + 1], in1=acc,
                                                                                         op0=ALU.mult, op1=ALU.add),
                         reads=[xbcB[ct], cvpB], writes=[bankB[ab]])
                if ct < 8:
                    dst, dB = xs_fm[:, ct, :], xsB[ct]
                elif ct < 10:
                    dst, dB = B_fm[:, ct - 8, :], BfB[ct - 8]
                else:
                    dst, dB = C_fm[:, ct - 10, :], CfB[ct - 10]
                P.op("scalar", lambda e, acc=acc, dst=dst: e.activation(out=dst, in_=acc, func=AF.Silu), reads=[bankB[ab]], writes=[dB])
                if ct < 8:
                    P.op("gpsimd", lambda e, ct=ct: e.tensor_scalar(out=xsD_fm[:, ct, :], in0=xs_fm[:, ct, :], scalar1=dcl[:, ct:ct + 1], scalar2=None, op0=ALU.mult),
                         reads=[xsB[ct], dclB], writes=[xsDB[ct]])
        def conv_tail(t):
            P.op("gpsimd", lambda e: e.tensor_copy(out=xbc[:, :, 0:3], in_=xbc[:, :, TT:TT + 3]), reads=xbcB, writes=xbcB)
            c0 = (t - 1) * TT
            P.dma("sync", "st", lambda e: e.dma_start(out=cS.rearrange("(g p) t -> p g t", p=128)[:, :, c0:c0 + TT], in_=C_fm), reads=CfB)

        ACT, DVE, PE, POOL = "scalar", "vector", "tensor", "gpsimd"

        def ssd_ctx(t, j):
            cidx = (t - 1) * 4 + j
            p = cidx % 2
            return dict(t=t, j=j, cs=slice(j * 128, (j + 1) * 128), cidx=cidx, p=p, SB_=SSB, dtv=SS[:, 0, j, :], dta=SS[:, 1, j, :], ea=SS[:, 3, j, :], cd=SS[:, 5, j, :], ddo=SS[:, 6, j, :],
                        E=E2[p], EB=E2B[p], cbm=cbm2[p], cbmB=cbm2B[p], xdt=xdt2[p], xdtB=xdt2B[p],
                        xw=xw2[p], xwB=xw2B[p], Btm=Btm2[p], BtmB=Btm2B[p])

        def smalls_tile(t):
            fl = lambda ap: ap.rearrange("p a b -> p (a b)")
            P.op(ACT, lambda e: e.activation(out=fl(dtp), in_=fl(dtp), func=AF.Exp), reads=[dtpB], writes=[dtpB])
            P.op(ACT, lambda e: e.activation(out=fl(SS[:, 0]), in_=fl(dtp), func=AF.Ln, bias=one_c[:, 0:1]), reads=[dtpB, onecB], writes=[SSB])
            P.op(DVE, lambda e: e.tensor_tensor(out=SS[:, 1], in0=SS[:, 0], in1=a_bc.rearrange("p (o h) -> p o h", o=1).broadcast_to([128, 4, 16]), op=ALU.mult),
                 reads=[SSB, abcB], writes=[SSB])
            acs_ps, tot_ps = banks[2][:, 64:128], banks[2][:, 0:64]
            P.op(PE, lambda e: e.matmul(acs_ps, lhsT=cm[:, 0, :], rhs=fl(SS[:, 1]), start=True, stop=True), reads=[SSB, cmB], writes=[bankB[2]])
            P.op(PE, lambda e: e.matmul(tot_ps, lhsT=cm[:, 2, :], rhs=fl(SS[:, 1]), start=True, stop=True), reads=[SSB, cmB], writes=[bankB[2]])
            P.op(DVE, lambda e: e.tensor_copy(out=fl(SS[:, 2]), in_=acs_ps), reads=[bankB[2]], writes=[SSB])
            for j in range(4):
                P.op(DVE, lambda e, j=j: e.tensor_tensor(out=dtp[:, j, :], in0=acs_ps[:, j * 16:(j + 1) * 16], in1=runtot, op=ALU.add), reads=[bankB[2], runtotB], writes=[dtpB])
                P.op(DVE, lambda e, j=j: e.tensor_tensor(out=runtot, in0=tot_ps[:, j * 16:(j + 1) * 16], in1=runtot, op=ALU.add), reads=[bankB[2]], writes=[runtotB])
            P.op(DVE, lambda e: e.tensor_tensor(out=fl(SS[:, 4]), in0=tot_ps, in1=fl(SS[:, 2]), op=ALU.subtract), reads=[bankB[2]], writes=[SSB])
            P.op(ACT, lambda e: e.activation(out=fl(SS[:, 5]), in_=tot_ps, func=AF.Exp), reads=[bankB[2]], writes=[SSB])
            P.op(ACT, lambda e: e.activation(out=fl(SS[:, 3]), in_=fl(SS[:, 2]), func=AF.Exp), reads=[SSB], writes=[SSB])
            c0 = (t - 1) * 4
            P.op(ACT, lambda e: e.activation(out=eag_all[:, c0:c0 + 4, :], in_=dtp, func=AF.Exp), reads=[dtpB], writes=[eagB])
            P.op(ACT, lambda e: e.activation(out=fl(SS[:, 4]), in_=fl(SS[:, 4]), func=AF.Exp), reads=[SSB], writes=[SSB])
            P.op(DVE, lambda e: e.tensor_tensor(out=SS[:, 6], in0=SS[:, 0], in1=SS[:, 4], op=ALU.mult), reads=[SSB], writes=[SSB])

        def a3(X, qd):
            dta, SB_, E, EB = X["dta"], X["SB_"], X["E"], X["EB"]
            r, rB = R[qd % 2], RB[qd % 2]
            sbk = 3 + qd % 2
            P.op(POOL, lambda e: e.tensor_tensor(out=r, in0=bc_last(dta[:, 4 * qd:4 * qd + 4], 128), in1=bc_mid(cm[:, 0, :], 4), op=ALU.mult),
                 reads=[SB_, cmB], writes=[rB])
            P.op(PE, lambda e: e.matmul(banks[sbk][:, :], lhsT=cm[:, 1, :], rhs=r.rearrange("p a b -> p (a b)"), start=True, stop=True),
                 reads=[rB, cmB], writes=[bankB[sbk]])
            P.op(ACT, lambda e: e.activation(out=E[:, 4 * qd:4 * qd + 4, :], in_=banks[sbk][:, :].rearrange("p (a b) -> p a b", a=4), func=AF.Exp),
                 reads=[bankB[sbk]], writes=[EB])

        def a4(X):
            cs, cbm, cbmB = X["cs"], X["cbm"], X["cbmB"]
            for g in range(2):
                P.op(PE, lambda e, g=g: e.matmul(banks[2][:, 128 + g * 128:256 + g * 128], lhsT=B_fm[:, g, cs], rhs=C_fm[:, g, cs], start=True, stop=True),
                     reads=[BfB[g], CfB[g]], writes=[bankB[2]])
            P.op(DVE, lambda e: e.tensor_tensor(out=cbm, in0=banks[2][:, 128:384].rearrange("p (g l) -> p g l", g=2), in1=bc_mid(cm[:, 0, :], 2), op=ALU.mult),
                 reads=[bankB[2], cmB], writes=[cbmB])
            for g in range(2):
                P.op(PE, lambda e, g=g: e.transpose(out=b2bf[:, 768 + g * 128:896 + g * 128], in_=B_fm[:, g, cs], identity=ident),
                     reads=[BfB[g], identB_], writes=[bankB[2]])
            Btm, BtmB = X["Btm"], X["BtmB"]
            P.op(ACT, lambda e: e.activation(out=Btm.rearrange("p a b -> p (a b)"), in_=b2bf[:, 768:1024], func=AF.Copy), reads=[bankB[2]], writes=[BtmB])

        def a5(X):
            E, EB, cbm, cbmB = X["E"], X["EB"], X["cbm"], X["cbmB"]
            for g in range(2):
                P.op(DVE, lambda e, g=g: e.tensor_tensor(out=E[:, 8 * g:8 * g + 8, :], in0=E[:, 8 * g:8 * g + 8, :], in1=cbm[:, g:g + 1, :].broadcast_to([128, 8, 128]), op=ALU.mult),
                     reads=[cbmB], writes=[EB])

        def a6(X):
            cs, dtv, ddo, SB_ = X["cs"], X["dtv"], X["ddo"], X["SB_"]
            for kt in range(8):
                P.op(PE, lambda e, kt=kt: e.transpose(out=b7bf[:, kt * 128:(kt + 1) * 128], in_=xs_fm[:, kt, cs], identity=ident),
                     reads=[xsB[kt], identB_], writes=[bankB[7]], inc=(kt == 7))
            xsT = b7bf.rearrange("p (h d) -> p h d", h=16)
            xdt, xdtB, xw, xwB = X["xdt"], X["xdtB"], X["xw"], X["xwB"]
            P.op(DVE, lambda e: e.tensor_tensor(out=xdt, in0=xsT, in1=bc_last(dtv, 64), op=ALU.mult), reads=[bankB[7], SB_], writes=[xdtB])
            P.op(DVE, lambda e: e.tensor_tensor(out=xw, in0=xsT, in1=bc_last(ddo, 64), op=ALU.mult), reads=[bankB[7], SB_], writes=[xwB])

        def b1(X, hh):
            cs, ea, SB_, E, EB, xdt, xdtB, cidx = X["cs"], X["ea"], X["SB_"], X["E"], X["EB"], X["xdt"], X["xdtB"], X["cidx"]
            for hq in range(8):
                h = 8 * hh + hq
                P.op(PE, lambda e, h=h, hq=hq: e.matmul(banks[5][:, hq * 64:(hq + 1) * 64], lhsT=E[:, h, :], rhs=xdt[:, h, :], start=(hq == 0), stop=False,
                                                        skip_group_check=True),
                     reads=[EB, xdtB], writes=[bankB[5]], inc=False)
            for kq in range(4):
                kt = 4 * hh + kq
                P.op(PE, lambda e, kt=kt, kq=kq: e.matmul(banks[5][:, kq * 128:(kq + 1) * 128], lhsT=xsD_fm[:, kt, cs], rhs=ident, start=False, stop=(kq == 3),
                                                          skip_group_check=True),
                     reads=[xsDB[kt], identB_], writes=[bankB[5]], inc=(kq == 3))
            P.op(PE, lambda e: e.matmul(banks[6][:, :], lhsT=C_fm[:, hh, cs], rhs=Hbf[:, 8 * hh:8 * hh + 8, :].rearrange("p a b -> p (a b)"), start=True, stop=True),
                 reads=[CfB[hh], HbfB], writes=[bankB[6]])
            P.op(DVE, lambda e: e.tensor_tensor(out=t1[:, hh * 512:(hh + 1) * 512].rearrange("p (a b) -> p a b", a=8),
                                                in0=banks[6][:, :].rearrange("p (a b) -> p a b", a=8),
                                                in1=bc_last(ea[:, 8 * hh:8 * hh + 8], 64), op=ALU.mult),
                 reads=[bankB[6], SB_], writes=[t1B])
            P.op(DVE, lambda e: e.tensor_tensor(out=yl1[:, hh * 512:(hh + 1) * 512], in0=banks[5][:, :], in1=t1[:, hh * 512:(hh + 1) * 512], op=ALU.add),
                 reads=[bankB[5], t1B], writes=[yl1B])
            if hh == 1:
                P.dma("sync", "st", lambda e: e.dma_start(out=ylS[cidx * 128:(cidx + 1) * 128, :], in_=yl1), reads=[yl1B])

        def b2(X, g):
            cd, SB_, xw, xwB, Btm, BtmB = X["cd"], X["SB_"], X["xw"], X["xwB"], X["Btm"], X["BtmB"]
            P.op(PE, lambda e: e.matmul(banks[6][:, :], lhsT=Btm[:, g, :], rhs=xw[:, 8 * g:8 * g + 8, :].rearrange("p a b -> p (a b)"), start=True, stop=True),
                 reads=[BtmB, xwB], writes=[bankB[6]])
            P.op(POOL, lambda e: e.tensor_tensor(out=H[:, 8 * g:8 * g + 8, :], in0=H[:, 8 * g:8 * g + 8, :], in1=bc_last(cd[:, 8 * g:8 * g + 8], 64), op=ALU.mult),
                 reads=[SB_], writes=[HB])
            P.op(DVE, lambda e: e.tensor_tensor(out=H[:, 8 * g:8 * g + 8, :], in0=banks[6][:, :].rearrange("p (a b) -> p a b", a=8), in1=H[:, 8 * g:8 * g + 8, :], op=ALU.add),
                 reads=[bankB[6]], writes=[HB])
            if g == 1:
                P.op(ACT, lambda e: e.activation(out=Hbf, in_=H, func=AF.Copy), reads=[HB], writes=[HbfB])

        def ssd_tile(t):
            Xs = [ssd_ctx(t, j) for j in range(4)]

            def run(XA, XB):
                if XB is not None:
                    b1(XB, 0)
                if XA is not None:
                    a3(XA, 0); a3(XA, 1)
                if XB is not None:
                    b1(XB, 1)
                if XA is not None:
                    a4(XA); a3(XA, 2); a3(XA, 3)
                if XB is not None:
                    b2(XB, 0)
                if XA is not None:
                    a5(XA); a6(XA)
                if XB is not None:
                    b2(XB, 1)

            smalls_tile(t)
            run(Xs[0], None)
            for j in range(4):
                run(Xs[j + 1] if j + 1 < 4 else None, Xs[j])

        def attn_tile(t, fillers=()):
            def pblk(lb):
                return ((t + 1) % 2) * 4 + lb if lb < 4 else (t % 2) * 4 + (lb - 4)

            items = [(m, h) for m in range(4) for h in range(8)]

            def qk(i):
                m, h = items[i]
                qs = slice(m * 128, (m + 1) * 128)
                hp, po = h // 2, (h % 2) * 64
                sA, sB2 = (3, 4) if i % 2 == 0 else (5, 6)
                for jj in range(5):
                    pb = pblk(m + jj)
                    o = banks[sA][:, jj * 128:(jj + 1) * 128] if jj < 4 else banks[sB2][:, 0:128]
                    ob = bankB[sA] if jj < 4 else bankB[sB2]
                    P.op("tensor", lambda e, o=o, pb=pb: e.matmul(o, lhsT=k_fm[po:po + 64, hp, pb * 128:(pb + 1) * 128], rhs=q_fm[po:po + 64, hp, qs], start=True, stop=True),
                         reads=[kB[pb // 4], qB], writes=[ob], inc=(jj >= 3))

            def expmul(i):
                m, h = items[i]
                sA, sB2 = (3, 4) if i % 2 == 0 else (5, 6)
                Pv, PvB = Pb[i % 2], PbB[i % 2]
                P.op("scalar", lambda e: e.activation(out=Pv[:, 0:4, :], in_=banks[sA][:, :].rearrange("p (a b) -> p a b", a=4), func=AF.Exp),
                     reads=[bankB[sA]], writes=[PvB])
                P.op("scalar", lambda e: e.activation(out=Pv[:, 4, :], in_=banks[sB2][:, 0:128], func=AF.Exp), reads=[bankB[sB2]], writes=[PvB])
                meng = "gpsimd"
                P.op(meng, lambda e: e.tensor_tensor(out=Pv, in0=Pv, in1=ET[:, h, :, :], op=ALU.mult), reads=[ETB], writes=[PvB])

            def pv(i):
                m, h = items[i]
                Pv, PvB = Pb[i % 2], PbB[i % 2]
                OB = 7 if h < 4 else 2
                oc = (h % 4) * 65
                for jj in range(5):
                    pb = pblk(m + jj)
                    P.op("tensor", lambda e, jj=jj, pb=pb: e.matmul(banks[OB][:, oc:oc + 65], lhsT=Pv[:, jj, :], rhs=v_tm[:, pb, h, :],
                                                                     start=(jj == 0), stop=(jj == 4), skip_group_check=True),
                         reads=[PvB, vB[pb]], writes=[bankB[OB]], inc=(jj == 4))
                if h % 4 == 3:
                    g = h // 4
                    ab = att[m % 2]
                    ov = banks[OB][:, 0:260].rearrange("p (a b) -> p a b", a=4)
                    P.op("vector", lambda e: e.reciprocal(out=rec[:, 4 * g:4 * g + 4], in_=ov[:, :, 64:65].rearrange("p a o -> p (a o)")),
                         reads=[bankB[OB]], writes=[recB])
                    P.op("vector", lambda e: e.tensor_tensor(out=ab[:, 4 * g:4 * g + 4, :], in0=ov[:, :, 0:64], in1=bc_last(rec[:, 4 * g:4 * g + 4], 64), op=ALU.mult),
                         reads=[bankB[OB], recB], writes=[attB[m % 2]])
                    if h == 7:
                        r0 = (t - 1) * TT + m * 128
                        P.dma("sync", "st", lambda e: e.dma_start(out=attS[r0:r0 + 128, :], in_=ab.rearrange("p a b -> p (a b)")), reads=[attB[m % 2]])

            nf = len(fillers)
            fi = 0
            qk(0)
            for i in range(len(items)):
                if i + 1 < len(items):
                    qk(i + 1)
                expmul(i)
                while fi < nf and fi * len(items) <= i * nf:
                    fillers[fi]()
                    fi += 1
                pv(i)
            while fi < nf:
                fillers[fi]()
                fi += 1

        load_hm(0)
        for t in range(NTH):
            if t + 1 < NTH:
                load_hm(t + 1)
            projections(t)
            if t == 0:
                build_ET()
                continue
            attn_tile(t)
            ssd_tile(t)
        P.dma("sync", "st", lambda e: e.dma_start(out=hloc, in_=H.rearrange("p a b -> p (a b)")), reads=[HB], writes=[hlocB])
        if debug_out:
            P.dma("sync", "st", lambda e: e.dma_start(out=hdbg, in_=H.rearrange("p a b -> p (a b)")), reads=[HB])
        P.dma("gpsimd", "cc", lambda e: e.collective_compute("AllGather", ALU.bypass, replica_groups=RG, ins=[hloc.opt()], outs=[hpair.opt()]),
              reads=[hlocB], writes=[hpairB], incv=-1)

    def mixer_pass2():
        wout = sb.view(WOUT_OFF, [128, 12, D], BF16)
        c = Carver(sb, ARENA, SB_TOTAL - PERS)
        A = 16
        x1t = c.get([128, 8, TT], F32, A)
        yl = [c.get([128, 1024], F32, A) for _ in range(2)]
        szb = [c.get([128, 1024], BF16, A) for _ in range(2)]
        attb = [c.get([128, 512], BF16, A) for _ in range(2)]
        cfm = [c.get([128, 2, TT], BF16, A) for _ in range(2)]
        gb = [c.get([128, 1024], F32, A) for _ in range(2)]
        gn = [c.get([128, 1024], BF16, A) for _ in range(2)]
        gT = c.get([128, 12, TT], BF16, A)
        Hin = c.get([128, 1024], F32, A)
        Hinb = c.get([128, 2, 512], BF16, A)
        ssd = c.get([128, 1024], F32, A)
        ident = c.get([128, 128], BF16, A)
        sm = [c.get([128, 4], F32, A) for _ in range(2)]
        x1B = Buf("x1t")
        ylB = [Buf("yl0"), Buf("yl1")]
        szB = [Buf("sz0"), Buf("sz1")]
        atB = [Buf("at0"), Buf("at1")]
        cfB = [Buf("cf0"), Buf("cf1")]
        gbB = [Buf("gb0"), Buf("gb1")]
        gnB = [Buf("gn0"), Buf("gn1")]
        gTB, HinB, HinbB, ssdB, idB = Buf("gT"), Buf("Hin"), Buf("Hinb"), Buf("ssd"), Buf("ident")
        smB = [Buf("sm0"), Buf("sm1")]
        b2bf = banks[2][:, :].bitcast(BF16)
        b3bf = banks[3][:, :].bitcast(BF16)

        P.dma("gpsimd", "wq", lambda e: e.dma_start(out=wout, in_=w_out.rearrange("(kt p) d -> p kt d", p=128)), writes=[woutB])
        P.dma("gpsimd", "wq", lambda e: e.dma_start(out=ident, in_=identb), writes=[idB])
        if 4 in phases:
            load_ffn_weights(w2g, w2u, w2d, do_parts=(0,))
        P.dma("sync", "ld", lambda e: e.dma_start(out=ssd, in_=ssdn), writes=[ssdB])
        P.dma("sync", "ld", lambda e: e.dma_start(out=Hin, in_=hpair[0:128, :]), reads=[hpairB], writes=[HinB])
        if debug_out:
            P.dma("sync", "st", lambda e: e.dma_start(out=hindbg, in_=Hin), reads=[HinB])
        P.op("vector", lambda e: e.tensor_scalar(out=Hinb.rearrange("p a b -> p (a b)"), in0=Hin, scalar1=flag_sb[:, 0:1], scalar2=None, op0=ALU.mult),
             reads=[HinB, flagB], writes=[HinbB])

        x1v = x1S.rearrange("(kt p) t -> p kt t", p=128)
        x2v = x2S.rearrange("(kt p) t -> p kt t", p=128)
        cSv = cS.rearrange("(g p) t -> p g t", p=128)
        def p3_loads(cidx, which=(0, 1)):
            i = cidx % 2
            r0 = cidx * 128
            if 0 in which:
                P.dma("sync", "ld", lambda e: e.dma_start(out=yl[i], in_=ylS[r0:r0 + 128, :]), writes=[ylB[i]])
                P.dma("sync", "ld", lambda e: e.dma_start(out=szb[i], in_=szS[r0:r0 + 128, :]), writes=[szB[i]])
            if 1 in which:
                P.dma("sync", "ld", lambda e: e.dma_start(out=attb[i], in_=attS[r0:r0 + 128, :]), writes=[atB[i]])

        def p3_corr(cidx):
            tt, j, i = cidx // 4, cidx % 4, cidx % 2
            cf, cfb = cfm[tt % 2], cfB[tt % 2]
            for g in range(2):
                P.op("tensor", lambda e, g=g: e.matmul(banks[g][:, :], lhsT=cf[:, g, j * 128:(j + 1) * 128], rhs=Hinb[:, g, :], start=True, stop=True),
                     reads=[cfb, HinbB], writes=[bankB[g]])
                P.op("vector", lambda e, g=g: e.tensor_tensor(out=gb[i][:, g * 512:(g + 1) * 512].rearrange("p (a b) -> p a b", a=8),
                                                              in0=banks[g][:, :].rearrange("p (a b) -> p a b", a=8),
                                                              in1=bc_last(eag_all[:, cidx, 8 * g:8 * g + 8], 64), op=ALU.mult),
                     reads=[bankB[g], eagB], writes=[gbB[i]])

        def p3_chain1(cidx):
            i = cidx % 2
            P.op("vector", lambda e: e.tensor_tensor(out=gb[i], in0=gb[i], in1=yl[i], op=ALU.add), reads=[ylB[i]], writes=[gbB[i]])
            P.op("gpsimd", lambda e: e.tensor_tensor(out=gb[i], in0=gb[i], in1=szb[i], op=ALU.mult), reads=[szB[i]], writes=[gbB[i]])

        def p3_chain(cidx):
            i = cidx % 2
            P.op("scalar", lambda e: e.activation(out=gn[i], in_=gb[i], func=AF.Square, accum_out=sm[i][:, 0:1]), reads=[gbB[i]], writes=[gnB[i], smB[i]])
            P.op("vector", lambda e: e.tensor_scalar(out=sm[i][:, 1:2], in0=sm[i][:, 0:1], scalar1=1.0 / 1024.0, scalar2=EPS, op0=ALU.mult, op1=ALU.add),
                 reads=[smB[i]], writes=[smB[i]])
            P.op("scalar", lambda e: e.activation(out=sm[i][:, 2:3], in_=sm[i][:, 1:2], func=AF.Sqrt), reads=[smB[i]], writes=[smB[i]])
            P.op("vector", lambda e: e.reciprocal(out=sm[i][:, 3:4], in_=sm[i][:, 2:3]), reads=[smB[i]], writes=[smB[i]])
            P.op("vector", lambda e: e.scalar_tensor_tensor(out=gn[i], in0=gb[i], scalar=sm[i][:, 3:4], in1=ssd, op0=ALU.mult, op1=ALU.mult),
                 reads=[gbB[i], smB[i], ssdB], writes=[gnB[i]])

        def p3_tr(cidx):
            j, i = cidx % 4, cidx % 2
            for kt in range(8):
                P.op("tensor", lambda e, kt=kt: e.transpose(out=b2bf[:, kt * 128:(kt + 1) * 128], in_=gn[i][:, kt * 128:(kt + 1) * 128], identity=ident),
                     reads=[gnB[i], idB], writes=[bankB[2]], inc=(kt == 7))
            for kt in range(4):
                P.op("tensor", lambda e, kt=kt: e.transpose(out=b3bf[:, kt * 128:(kt + 1) * 128], in_=attb[i][:, kt * 128:(kt + 1) * 128], identity=ident),
                     reads=[atB[i], idB], writes=[bankB[3]], inc=(kt == 3))
            P.op("scalar", lambda e: e.activation(out=gT[:, 0:8, j * 128:(j + 1) * 128], in_=b2bf.rearrange("p (a b) -> p a b", a=8), func=AF.Copy),
                 reads=[bankB[2]], writes=[gTB])
            P.op("scalar", lambda e: e.activation(out=gT[:, 8:12, j * 128:(j + 1) * 128], in_=b3bf[:, 0:512].rearrange("p (a b) -> p a b", a=4), func=AF.Copy),
                 reads=[bankB[3]], writes=[gTB])

        def p3_wout(tt):
            for dtile in range(8):
                b = 4 + dtile % 2
                for kt in range(12):
                    P.op("tensor", lambda e, kt=kt, b=b, dtile=dtile: e.matmul(banks[b][:, :], lhsT=wout[:, kt, dtile * 128:(dtile + 1) * 128], rhs=gT[:, kt, :],
                                                                               start=(kt == 0), stop=(kt == 11)),
                         reads=[gTB, woutB], writes=[bankB[b]], inc=(kt == 11))
                P.op("vector", lambda e, b=b, dtile=dtile: e.tensor_tensor(out=x1t[:, dtile, :], in0=banks[b][:, :], in1=x1t[:, dtile, :], op=ALU.add),
                     reads=[bankB[b]], writes=[x1B])
            P.dma("sync", "st", lambda e: e.dma_start(out=x2v[:, :, tt * TT:(tt + 1) * TT], in_=x1t), reads=[x1B])

        def p3_tile_loads(tt):
            P.dma("sync", "ld", lambda e: e.dma_start(out=x1t, in_=x1v[:, :, (tt + 1) * TT:(tt + 2) * TT]), writes=[x1B])

        def p3_cf_load(tt):
            cf, cfb = cfm[tt % 2], cfB[tt % 2]
            P.dma("sync", "ld", lambda e: e.dma_start(out=cf, in_=cSv[:, :, tt * TT:(tt + 1) * TT]), writes=[cfb])

        NCHK = NT * 4
        p3_cf_load(0)
        if NT > 1:
            p3_cf_load(1)
        p3_loads(0)
        if NCHK > 1:
            p3_loads(1)
        p3_tile_loads(0)
        p3_corr(0)
        p3_chain1(0)
        for cidx in range(NCHK):
            tt, j = cidx // 4, cidx % 4
            if j == 0 and tt > 0 and tt + 1 < NT:
                p3_cf_load(tt + 1)
            if cidx + 1 < NCHK:
                p3_corr(cidx + 1)
                p3_chain1(cidx + 1)
            if cidx + 2 < NCHK:
                p3_loads(cidx + 2, which=(0,))
            if j == 0 and tt > 0:
                p3_wout(tt - 1)
                p3_tile_loads(tt)
            p3_chain(cidx)
            p3_tr(cidx)
            if cidx + 2 < NCHK:
                p3_loads(cidx + 2, which=(1,))
        p3_wout(NT - 1)
        if 4 in phases:
            wg, wu, wd = ffn_weight_views()
            dsrc = w2d.rearrange("(ft p) d -> p ft d", p=128)
            P.dma("gpsimd", "wq", lambda e: e.dma_start(out=wd[:, 10:NFT, :], in_=dsrc[:, 10:NFT, :]), writes=[wdB[1], woutB])

    if 1 in phases:
        load_ffn_weights(w1g, w1u, w1d)
        ffn_phase(xT, NTH, 0, 8, "mix", x1S, hmS, 0)
    if 2 in phases:
        load_w_in()
        P.fence(exclude=("wq",))
        mixer_pass1()
        P.fence(exclude=("wq",))
    if 3 in phases:
        mixer_pass2()
        P.fence(exclude=("wq",))
    if 4 in phases:
        if 3 not in phases:
            load_ffn_weights(w2g, w2u, w2d)
        ffn_phase(x2S if (3 in phases) else x1S[:, TT:], NT, 16, 24, "final", outT, None, 0)

    P.wait_all("sync", P.dma_toks() + [("cc", P.cnt["cc"])])

    with nc.Block() as block:
        run = P.emit(sems)
        block.sync(run("sync"))
        block.scalar(run("scalar"))
        block.tensor(run("tensor"))
        block.vector(run("vector"))
        block.gpsimd(run("gpsimd"))
    es.close()
    return nc


def make_in_maps(inputs, NT=8, n_cores=8):
    f = lambda a: np.ascontiguousarray(np.asarray(a, dtype=np.float32))
    x = f(inputs["x"])
    TOK = NT * TT
    shared = {
        "w1g": f(inputs["ffn1_w_gate"][0]), "w1u": f(inputs["ffn1_w_up"][0]), "w1d": f(inputs["ffn1_w_down"][0]),
        "w2g": f(inputs["ffn2_w_gate"][0]), "w2u": f(inputs["ffn2_w_up"][0]), "w2d": f(inputs["ffn2_w_down"][0]),
        "w_in": f(inputs["w_in"][0]), "w_out": f(inputs["w_out"][0]),
    }
    g = np.stack([f(inputs["ffn1_norm"][0]), f(inputs["mix_norm"][0]), f(inputs["ffn2_norm"][0]), f(inputs["final_norm"])])
    shared["gains"] = np.ascontiguousarray(g.reshape(4, 8, 128).transpose(2, 0, 1).reshape(128, 32))
    cw = np.concatenate([f(inputs["conv_w"][0]).T, f(inputs["conv_b"][0])[:, None]], axis=1)
    shared["convp"] = np.ascontiguousarray(cw.reshape(12, 128, 5).transpose(1, 0, 2).reshape(128, 60))
    hv = np.concatenate([f(inputs["dt_bias"][0]), f(inputs["a_log"][0]), f(inputs["d_skip"][0])])
    shared["hvec"] = np.ascontiguousarray(np.tile(hv[None, :], (128, 1)))
    shared["dcol"] = np.ascontiguousarray(np.repeat(f(inputs["d_skip"][0]), 64).reshape(8, 128).T)
    s_ = np.arange(128)
    triU = (s_[:, None] <= s_[None, :]).astype(np.float32)
    Lmat = (s_[:, None] > s_[None, :]).astype(np.float32)
    shared["cmats"] = np.ascontiguousarray(np.concatenate([triU, Lmat, np.ones((128, 128), np.float32)], axis=1))
    shared["identb"] = np.eye(128, dtype=np.float32)
    k_ = np.arange(128)[:, None, None]
    j_ = np.arange(5)[None, :, None]
    q_ = np.arange(128)[None, None, :]
    kabs = j_ * 128 + k_
    rel = np.clip(512 + q_ - kabs, -256, 256) + 256
    rb = f(inputs["rel_bias"][0])
    shared["biasT"] = np.ascontiguousarray(rb[:, rel].transpose(1, 0, 2, 3).reshape(128, 8 * 5 * 128))
    kc, qc = kabs // 64, q_ // 64
    shared["amask"] = np.ascontiguousarray(((kc >= qc) & (kc <= qc + 8)).astype(np.float32).reshape(128, 5 * 128))
    shared["ssdn"] = np.ascontiguousarray(np.tile(f(inputs["ssd_norm"][0])[None, :], (128, 1)))
    maps = []
    for c in range(n_cores):
        b, half = c // 2, c % 2
        start = half * TOK
        rows = np.zeros((TOK + TT, D), np.float32)
        if half == 1:
            rows[:] = x[b, start - TT:start + TOK]
        else:
            rows[TT:] = x[b, 0:TOK]
        m = dict(shared)
        m["xT"] = np.ascontiguousarray(rows.T)
        m["flag"] = np.full((128, 1), float(half), np.float32)
        maps.append(m)
    return maps


_NC_CACHE = {}


def kernel(**inputs):
    NT = 8
    if "nc" not in _NC_CACHE:
        _NC_CACHE["nc"] = build_program(NT=NT)
    nc = _NC_CACHE["nc"]
    maps = make_in_maps(inputs, NT=NT, n_cores=8)
    res = run_bass_kernel_spmd(nc, maps, core_ids=list(range(8)))
    TOK = NT * TT
    out = np.empty((4, 2 * TOK, D), np.float32)
    for c in range(8):
        b, half = c // 2, c % 2
        out[b, half * TOK:(half + 1) * TOK, :] = np.asarray(res.results[c]["outT"]).T
    return out
```

```python
import numpy as np
from contextlib import ExitStack
import concourse.bass as bass
import concourse.mybir as mybir
from concourse.bass_utils import run_bass_kernel_spmd

F32 = mybir.dt.float32
BF16 = mybir.dt.bfloat16
AF = mybir.ActivationFunctionType
ALU = mybir.AluOpType

ENGS = ("tensor", "vector", "scalar", "gpsimd", "sync")

D = 1024
DFF = 2816
NFT = DFF // 128
TT = 512
EPS = 1e-5
PROJ = 4112
O_Z, O_XBC, O_DT, O_Q, O_K, O_V = 0, 1024, 2560, 2576, 3088, 3600


class Buf:
    __slots__ = ("name", "w", "r")

    def __init__(self, name):
        self.name = name
        self.w = None
        self.r = []


class Prog:
    def __init__(self, nc):
        self.nc = nc
        self.q = {e: [] for e in ENGS}
        self.cnt = {e: 0 for e in ENGS}
        self.waited = {e: {} for e in ENGS}
        self.semkeys = list(ENGS)
        self.same_engine_sync = True
        self.fam = {}
        self.famn = {}

    def new_sem(self, key):
        self.semkeys.append(key)
        self.cnt[key] = 0
        return key

    def _deps_for(self, reads, writes):
        deps = []
        for b in reads:
            if b.w is not None:
                deps.append(b.w)
        for b in writes:
            if b.w is not None:
                deps.append(b.w)
            deps.extend(b.r)
        return deps

    def _mark(self, tok, reads, writes):
        for b in reads:
            b.r.append(tok)
            if len(b.r) > 8:
                best = {}
                for (k, v) in b.r:
                    if best.get(k, 0) < v:
                        best[k] = v
                b.r = list(best.items())
        for b in writes:
            b.w = tok
            b.r = []

    def _waits(self, eng, alld, skip_same):
        waits = []
        for d in alld:
            if d is None:
                continue
            k, v = d
            if k == eng and (skip_same or v > self.cnt[eng]):
                continue
            if self.waited[eng].get(k, 0) < v:
                self.waited[eng][k] = v
                waits.append((k, v))
        return waits

    def op(self, eng, fn, reads=(), writes=(), deps=(), inc=True):
        alld = list(deps) + self._deps_for(reads, writes)
        skip_same = (eng == "tensor") or (not self.same_engine_sync)
        waits = self._waits(eng, alld, skip_same)
        tok = None
        if inc:
            self.cnt[eng] += 1
            tok = (eng, self.cnt[eng])
        else:
            tok = (eng, self.cnt[eng] + 1)
        self.q[eng].append((waits, fn, eng if inc else None, 1))
        self._mark(tok, reads, writes)
        return tok

    def dma(self, eng, semkey, fn, reads=(), writes=(), deps=(), incv=16):
        alld = list(deps) + self._deps_for(reads, writes)
        if semkey in self.fam:
            names = self.fam[semkey]
            key = names[self.famn[semkey] % len(names)]
            self.famn[semkey] += 1
            if self.cnt[key] > 0:
                alld.append((key, self.cnt[key]))
        else:
            key = semkey
        waits = self._waits(eng, alld, False)
        self.cnt[key] += (1 if incv == -1 else incv)
        tok = (key, self.cnt[key])
        self.q[eng].append((waits, fn, key, incv))
        self._mark(tok, reads, writes)
        return tok

    def new_family(self, fam, n):
        self.fam[fam] = [self.new_sem(f"{fam}{i}") for i in range(n)]
        self.famn[fam] = 0

    def dma_toks(self):
        out = []
        for names in self.fam.values():
            out.extend((k, self.cnt[k]) for k in names if self.cnt[k] > 0)
        return out

    def wait_all(self, eng, toks):
        waits = self._waits(eng, toks, False)
        if waits:
            self.q[eng].append((waits, None, None, 0))

    def fence(self, exclude=()):
        ex = set(exclude)
        for f_ in exclude:
            ex.update(self.fam.get(f_, []))
        toks = [(k, self.cnt[k]) for k in self.semkeys if self.cnt[k] > 0 and k not in ex]
        for e in ENGS:
            self.wait_all(e, [tk for tk in toks if tk[0] != e])

    def emit(self, sems):
        def run(engname):
            def body(e):
                for (waits, fn, inckey, incv) in self.q[engname]:
                    if fn is None:
                        for (k, v) in waits:
                            e.wait_ge(sems[k], v)
                        continue
                    for (k, v) in waits[1:]:
                        e.wait_ge(sems[k], v)
                    ins = fn(e)
                    if waits:
                        ins._wait_ge(sems[waits[0][0]], waits[0][1])
                    if inckey is not None:
                        if incv == -1:
                            ins.then_inc(sems[inckey])
                        else:
                            ins.then_inc(sems[inckey], incv)
            return body
        return run


class SB:
    def __init__(self, nc, nbytes):
        self.nbytes = nbytes
        self.t = nc.alloc_sbuf_tensor("sb_all", [128, nbytes // 2], BF16)

    def view(self, off, shape, dtype):
        assert off % 4 == 0
        esz = 4 if dtype == F32 else 2
        n = 1
        for s in shape[1:]:
            n *= s
        nb = n * esz
        assert off + nb <= self.nbytes, (off, nb, self.nbytes)
        v = self.t[:, off // 2:(off + nb) // 2]
        if dtype == F32:
            v = v.bitcast(F32)
        if len(shape) == 3:
            v = v.rearrange("p (a b) -> p a b", a=shape[1])
        elif len(shape) == 4:
            v = v.rearrange("p (a b c) -> p a b c", a=shape[1], b=shape[2])
        return v


class Carver:
    def __init__(self, sb, lo, hi):
        self.sb, self.lo, self.hi, self.cur = sb, lo, hi, lo

    def get(self, shape, dtype, align=32):
        self.cur = (self.cur + align - 1) // align * align
        esz = 4 if dtype == F32 else 2
        n = 1
        for s in shape[1:]:
            n *= s
        v = self.sb.view(self.cur, shape, dtype)
        self.cur += n * esz
        assert self.cur <= self.hi, ("SBUF carve overflow", self.cur, self.hi)
        return v


DBG = {}
ARENA = 3 * 8 * DFF * 2
SB_TOTAL = 212800
PERS = 2688
WIN_BYTES = 8 * PROJ * 2
WOUT_OFF = ARENA - 12 * D * 2


def bc_last(ap, n):
    p, a = ap.shape
    return ap.rearrange("p (a o) -> p a o", o=1).broadcast_to([p, a, n])


def bc_mid(ap, n):
    p, l = ap.shape
    return ap.rearrange("p (o l) -> p o l", o=1).broadcast_to([p, n, l])


def build_program(NT=8, phases=(1, 2, 3, 4), debug_out=False, n_cores=8):
    nc = bass.Bass("TRN2", target_bir_lowering=False)
    NTH = NT + 1
    TOK = NT * TT
    TOKH = NTH * TT
    NCH = NT * 4

    def din(name, shape, dt=F32):
        return nc.dram_tensor(name, list(shape), dt, kind="ExternalInput").ap()

    xT = din("xT", [D, TOKH])
    w1g, w1u, w1d = din("w1g", [D, DFF]), din("w1u", [D, DFF]), din("w1d", [DFF, D])
    w2g, w2u, w2d = din("w2g", [D, DFF]), din("w2u", [D, DFF]), din("w2d", [DFF, D])
    w_in = din("w_in", [D, PROJ])
    w_out = din("w_out", [1536, D])
    gains = din("gains", [128, 32])
    convp = din("convp", [128, 60])
    hvec = din("hvec", [128, 48])
    dcol = din("dcol", [128, 8])
    flag = din("flag", [128, 1])
    cmats = din("cmats", [128, 3 * 128])
    identb = din("identb", [128, 128])
    biasT = din("biasT", [128, 8 * 5 * 128])
    amask = din("amask", [128, 5 * 128])
    ssdn = din("ssdn", [128, 1024])
    outT = nc.dram_tensor("outT", [D, TOK], F32, kind="ExternalOutput").ap()

    kindS = "ExternalOutput" if debug_out else "Internal"
    x1S = nc.dram_tensor("x1S", [D, TOKH], F32, kind=kindS).ap()
    hmS = nc.dram_tensor("hmS", [D, TOKH], BF16, kind=kindS).ap()
    x2S = nc.dram_tensor("x2S", [D, TOK], F32, kind=kindS).ap()
    ylS = nc.dram_tensor("ylS", [TOK, 1024], F32, kind=kindS).ap()
    szS = nc.dram_tensor("szS", [TOK, 1024], BF16, kind=kindS).ap()
    attS = nc.dram_tensor("attS", [TOK, 512], BF16, kind=kindS).ap()
    cS = nc.dram_tensor("cS", [256, TOK], BF16, kind=kindS).ap()
    hloc = nc.dram_tensor("hloc", [128, 1024], F32, kind="Internal").ap()
    hpair = nc.dram_tensor("hpair", [256, 1024], F32, kind="Internal").ap()
    if debug_out:
        hdbg = nc.dram_tensor("hdbg", [128, 1024], F32, kind="ExternalOutput").ap()
        hindbg = nc.dram_tensor("hindbg", [128, 1024], F32, kind="ExternalOutput").ap()

    P = Prog(nc)
    P.new_family("ld", 24)
    P.new_family("st", 24)
    P.new_family("wq", 24)
    P.new_sem("cc")

    es = ExitStack()
    sb = SB(nc, SB_TOTAL)
    banks = [nc.alloc_psum_tensor(f"bank{i}", [128, 512], F32) for i in range(8)]
    bankB = [Buf(f"bank{i}") for i in range(8)]
    sems = {k: es.enter_context(nc.semaphore(k)) for k in P.semkeys}

    pers = Carver(sb, SB_TOTAL - PERS, SB_TOTAL)
    ones_s = pers.get([128, 128], BF16)
    gains_sb = pers.get([128, 32], F32)
    eps_sb = pers.get([128, 1], F32)
    one_c = pers.get([128, 1], F32)
    flag_sb = pers.get([128, 1], F32)
    runtot = pers.get([128, 16], F32)
    eag_all = pers.get([128, NCH, 16], F32)
    onesB, gainsB, epsB = Buf("ones_s"), Buf("gains"), Buf("eps")
    onecB, flagB, runtotB, eagB = Buf("one_c"), Buf("flag"), Buf("runtot"), Buf("eag")
    P.op("gpsimd", lambda e: e.memset(eps_sb, EPS), writes=[epsB])
    P.op("gpsimd", lambda e: e.memset(one_c, 1.0), writes=[onecB])
    P.op("gpsimd", lambda e: e.memset(runtot, 0.0), writes=[runtotB])
    P.op("gpsimd", lambda e: e.memset(ones_s, 1.0 / 1024.0), writes=[onesB])
    P.dma("sync", "ld", lambda e: e.dma_start(out=gains_sb, in_=gains), writes=[gainsB])
    P.dma("sync", "ld", lambda e: e.dma_start(out=flag_sb, in_=flag), writes=[flagB])

    def ffn_weight_views():
        wg = sb.view(0, [128, 8, DFF], BF16)
        wu = sb.view(8 * DFF * 2, [128, 8, DFF], BF16)
        wd = sb.view(2 * 8 * DFF * 2, [128, NFT, D], BF16)
        return wg, wu, wd

    NWC = 4
    WCW = DFF // NWC
    wgB = [Buf(f"wg{i}") for i in range(NWC)]
    wuB = [Buf(f"wu{i}") for i in range(NWC)]
    wdB = [Buf(f"wd{i}") for i in range(2)]

    def load_ffn_weights(gd, ud, dd, do_parts=(0, 1)):
        wg, wu, wd = ffn_weight_views()
        gsrc = gd.rearrange("(kt p) f -> p kt f", p=128)
        usrc = ud.rearrange("(kt p) f -> p kt f", p=128)
        dsrc = dd.rearrange("(ft p) d -> p ft d", p=128)
        for c in range(NWC):
            cs = slice(c * WCW, (c + 1) * WCW)
            P.dma("gpsimd", "wq", lambda e, cs=cs: e.dma_start(out=wg[:, :, cs], in_=gsrc[:, :, cs]), writes=[wgB[c]])
            P.dma("gpsimd", "wq", lambda e, cs=cs: e.dma_start(out=wu[:, :, cs], in_=usrc[:, :, cs]), writes=[wuB[c]])
        for part in do_parts:
            fs = slice(0, 10) if part == 0 else slice(10, NFT)
            P.dma("gpsimd", "wq", lambda e, fs=fs: e.dma_start(out=wd[:, fs, :], in_=dsrc[:, fs, :]), writes=[wdB[part]])

    def ffn_phase(src, ntiles, gcol_pre, gcol_post, post_mode, dst_x, dst_h, dst_tok0):
        wg, wu, wd = ffn_weight_views()
        c = Carver(sb, ARENA, SB_TOTAL - 1024)
        xt = [c.get([128, 8, TT], F32) for _ in range(2)]
        s1 = c.get([128, 8, TT], BF16)
        hb = c.get([128, 8, TT], BF16)
        hid = c.get([128, NFT, TT], BF16)
        rstd = c.get([128, TT], F32)
        xtB = [Buf("xt0"), Buf("xt1")]
        s1B, hB, rstdB = Buf("s1"), Buf("h"), Buf("rstd")
        hidB = [Buf(f"hid{i}") for i in range(NFT)]
        srcv = src.rearrange("(kt p) t -> p kt t", p=128)
        dxv = dst_x.rearrange("(kt p) t -> p kt t", p=128)
        dhv = dst_h.rearrange("(kt p) t -> p kt t", p=128) if dst_h is not None else None
        PB_G, PB_U, PB_D, PB_N = (0, 1), (2, 3), (4, 5), 6

        def load_x(t):
            b = t % 2
            P.dma("sync", "ld", lambda e: e.dma_start(out=xt[b], in_=srcv[:, :, t * TT:(t + 1) * TT]), writes=[xtB[b]])

        def norm_sq(t):
            b = t % 2
            P.op("scalar", lambda e: e.activation(out=s1, in_=xt[b], func=AF.Square), reads=[xtB[b]], writes=[s1B])

        def norm_mm(t):
            for kt in range(8):
                P.op("tensor", lambda e, kt=kt: e.matmul(banks[PB_N][:, :], lhsT=ones_s, rhs=s1[:, kt, :], start=(kt == 0), stop=(kt == 7)),
                     reads=[s1B, onesB], writes=[bankB[PB_N]], inc=(kt == 7))

        def norm_rstd(t):
            P.op("scalar", lambda e: e.activation(out=rstd, in_=banks[PB_N][:, :], func=AF.Sqrt, bias=eps_sb[:, 0:1]),
                 reads=[bankB[PB_N], epsB], writes=[rstdB])
            P.op("vector", lambda e: e.reciprocal(out=rstd, in_=rstd), reads=[rstdB], writes=[rstdB])

        def pre_scale(t):
            b = t % 2
            for kt in range(8):
                P.op("vector", lambda e, kt=kt: e.scalar_tensor_tensor(out=hb[:, kt, :], in0=xt[b][:, kt, :], scalar=gains_sb[:, gcol_pre + kt:gcol_pre + kt + 1],
                                                                        in1=rstd, op0=ALU.mult, op1=ALU.mult),
                     reads=[xtB[b], rstdB, gainsB], writes=[hB], inc=(kt == 7))

        def post_scale_store(t):
            b = t % 2
            ts = slice(dst_tok0 + t * TT, dst_tok0 + (t + 1) * TT)
            if post_mode == "mix":
                P.dma("sync", "st", lambda e: e.dma_start(out=dxv[:, :, ts], in_=xt[b]), reads=[xtB[b]])
                for kt in range(8):
                    P.op("vector", lambda e, kt=kt: e.scalar_tensor_tensor(out=s1[:, kt, :], in0=xt[b][:, kt, :], scalar=gains_sb[:, gcol_post + kt:gcol_post + kt + 1],
                                                                            in1=rstd, op0=ALU.mult, op1=ALU.mult),
                         reads=[xtB[b], rstdB, gainsB], writes=[s1B], inc=(kt == 7))
                return P.dma("sync", "st", lambda e: e.dma_start(out=dhv[:, :, ts], in_=s1), reads=[s1B])
            else:
                for kt in range(8):
                    P.op("vector", lambda e, kt=kt: e.scalar_tensor_tensor(out=xt[b][:, kt, :], in0=xt[b][:, kt, :], scalar=gains_sb[:, gcol_post + kt:gcol_post + kt + 1],
                                                                            in1=rstd, op0=ALU.mult, op1=ALU.mult),
                         reads=[rstdB, gainsB], writes=[xtB[b]], inc=(kt == 7))
                return P.dma("sync", "st", lambda e: e.dma_start(out=dxv[:, :, ts], in_=xt[b]), reads=[xtB[b]])

        def gate_up_ft(t, ft):
            pg, pu = PB_G[ft % 2], PB_U[ft % 2]
            fs = slice(ft * 128, (ft + 1) * 128)
            wcs = sorted(set([(ft * 128) // WCW, (ft * 128 + 127) // WCW]))
            for kt in range(8):
                P.op("tensor", lambda e, kt=kt: e.matmul(banks[pg][:, :], lhsT=wg[:, kt, fs], rhs=hb[:, kt, :], start=(kt == 0), stop=(kt == 7)),
                     reads=[hB] + [wgB[i] for i in wcs], writes=[bankB[pg]], inc=(kt == 7))
            for kt in range(8):
                P.op("tensor", lambda e, kt=kt: e.matmul(banks[pu][:, :], lhsT=wu[:, kt, fs], rhs=hb[:, kt, :], start=(kt == 0), stop=(kt == 7)),
                     reads=[hB] + [wuB[i] for i in wcs], writes=[bankB[pu]], inc=(kt == 7))
            P.op("scalar", lambda e: e.activation(out=hid[:, ft, :], in_=banks[pg][:, :], func=AF.Silu), reads=[bankB[pg]], writes=[hidB[ft]])
            P.op("vector", lambda e: e.tensor_tensor(out=hid[:, ft, :], in0=hid[:, ft, :], in1=banks[pu][:, :], op=ALU.mult),
                 reads=[bankB[pu]], writes=[hidB[ft]])

        def down(t):
            b = t % 2
            for dt_ in range(8):
                pd = PB_D[dt_ % 2]
                ds_ = slice(dt_ * 128, (dt_ + 1) * 128)
                for ft in range(NFT):
                    P.op("tensor", lambda e, ft=ft, pd=pd, ds_=ds_: e.matmul(banks[pd][:, :], lhsT=wd[:, ft, ds_], rhs=hid[:, ft, :], start=(ft == 0), stop=(ft == NFT - 1)),
                         reads=[hidB[ft], wdB[0 if ft < 10 else 1]], writes=[bankB[pd]], inc=(ft == NFT - 1))
                P.op("vector", lambda e, pd=pd, dt_=dt_: e.scalar_tensor_tensor(out=xt[b][:, dt_, :], in0=banks[pd][:, :], scalar=0.5, in1=xt[b][:, dt_, :], op0=ALU.mult, op1=ALU.add),
                     reads=[bankB[pd]], writes=[xtB[b]])

        last = None
        load_x(0)
        if ntiles > 1:
            load_x(1)
        norm_sq(0); norm_mm(0); norm_rstd(0); pre_scale(0)
        for t in range(ntiles + 1):
            for ft in range(NFT):
                if t < ntiles:
                    gate_up_ft(t, ft)
                if t >= 1:
                    if ft == 1:
                        norm_sq(t - 1)
                    if ft == 3:
                        norm_mm(t - 1)
                    if ft == 5:
                        norm_rstd(t - 1)
                        last = post_scale_store(t - 1)
                        if t + 1 < ntiles:
                            load_x(t + 1)
                if t + 1 < ntiles:
                    if ft == 13:
                        norm_sq(t + 1)
                    if ft == 16:
                        norm_mm(t + 1)
                    if ft == 18:
                        norm_rstd(t + 1)
            if t + 1 < ntiles:
                pre_scale(t + 1)
            if t < ntiles:
                down(t)
        return last

    WCH = [("k", O_K, O_K + 512), ("v", O_V, O_V + 512), ("xbc0", O_XBC, O_XBC + 768),
           ("xbc1", O_XBC + 768, O_XBC + 1536), ("dt", O_DT, O_DT + 16), ("q", O_Q, O_Q + 512),
           ("z0", O_Z, O_Z + 512), ("z1", O_Z + 512, O_Z + 1024)]
    winB = {n: Buf("win_" + n) for n, _, _ in WCH}
    woutB = Buf("wout")
    hlocB, hpairB = Buf("hloc"), Buf("hpair")
    ylSB, szSB, attSB, cSB, x2SB = Buf("ylS"), Buf("szS"), Buf("attS"), Buf("cS"), Buf("x2S")

    def win_buf(col):
        for n, lo, hi in WCH:
            if lo <= col < hi:
                return winB[n]
        raise AssertionError(col)

    def load_w_in():
        win = sb.view(0, [128, 8, PROJ], BF16)
        src = w_in.rearrange("(kt p) f -> p kt f", p=128)
        first = True
        for n, lo, hi in WCH:
            P.dma("gpsimd", "wq", lambda e, lo=lo, hi=hi: e.dma_start(out=win[:, :, lo:hi], in_=src[:, :, lo:hi]),
                  writes=[winB[n]] + ((wgB + wuB) if first else []))
            first = False

    RG = [[2 * i, 2 * i + 1] for i in range(n_cores // 2)]

    def mixer_pass1():
        win = sb.view(0, [128, 8, PROJ], BF16)
        c = Carver(sb, WIN_BYTES, SB_TOTAL - PERS)
        A = 16
        hm = [c.get([128, 8, TT], BF16, A) for _ in range(2)]
        xbc = c.get([128, 12, TT + 3], F32, A)
        xs_fm = c.get([128, 8, TT], BF16, A)
        xsD_fm = c.get([128, 8, TT], BF16, A)
        B_fm = c.get([128, 2, TT], BF16, A)
        C_fm = c.get([128, 2, TT], BF16, A)
        q_fm = c.get([128, 4, TT], BF16, A)
        k_fm = c.get([128, 4, 2 * TT], BF16, A)
        v_tm = c.get([128, 8, 8, 65], BF16, A)
        sz = [c.get([128, 1024], BF16, A)] * 2
        dtp = c.get([128, 4, 16], F32, A)
        SS = c.get([128, 7, 4, 16], F32, A)
        R = [c.get([128, 4, 128], F32, A) for _ in range(2)]
        E2 = [c.get([128, 16, 128], BF16, A) for _ in range(2)]
        E = E2[0]
        cbm2 = [c.get([128, 2, 128], BF16, A) for _ in range(2)]
        xdt2 = [c.get([128, 16, 64], BF16, A) for _ in range(2)]
        xw2 = [c.get([128, 16, 64], BF16, A) for _ in range(2)]
        Btm2 = [c.get([128, 2, 128], BF16, A) for _ in range(2)]
        H = c.get([128, 16, 64], F32, A)
        Hbf = c.get([128, 16, 64], BF16, A)
        big = c.get([128, 3, 1024], F32, A)
        t1 = big[:, 0, :]
        yl1 = big[:, 1, :]
        etmp = big.rearrange("p a b -> p (a b)")[:, 0:2560].rearrange("p (h j q) -> p h j q", h=4, j=5)
        mtmp = E.rearrange("p a b -> p (a b)")[:, 0:1280].bitcast(F32).rearrange("p (j q) -> p j q", j=5)
        ET = c.get([128, 8, 5, 128], BF16, A)
        Pb = [c.get([128, 5, 128], BF16, A) for _ in range(2)]
        att = [c.get([128, 8, 64], BF16, A)] * 2
        rec = c.get([128, 8], F32, A)
        cm = c.get([128, 3, 128], F32, A)
        ident = c.get([128, 128], BF16, A)
        hv = c.get([128, 48], F32, A)
        a_bc = c.get([128, 16], F32, A)
        cvp = c.get([128, 12, 5], F32, A)
        dcl = c.get([128, 8], F32, A)

        DBG.update({k_: v_ for k_, v_ in locals().items() if k_ not in ('c', 'win')})
        hmB = [Buf("hm0"), Buf("hm1")]
        xbcB = [Buf(f"xbc{i}") for i in range(12)]
        xsB = [Buf(f"xs{i}") for i in range(8)]
        xsDB = [Buf(f"xsD{i}") for i in range(8)]
        BfB = [Buf("Bf0"), Buf("Bf1")]
        CfB = [Buf("Cf0"), Buf("Cf1")]
        qB = Buf("q")
        kB = [Buf("k0"), Buf("k1")]
        vB = [Buf(f"v{i}") for i in range(8)]
        szB = [Buf("sz0")] * 2
        dtpB = Buf("dtp")
        SSB = Buf("SS")
        RB = [Buf("R0"), Buf("R1")]
        E2B = [Buf("E0"), Buf("E1")]
        EB = E2B[0]
        cbm2B = [Buf("cbm0"), Buf("cbm1")]
        xdt2B = [Buf("xdt0"), Buf("xdt1")]
        xw2B = [Buf("xw0"), Buf("xw1")]
        Btm2B = [Buf("Btm0"), Buf("Btm1")]
        HB, HbfB, t1B = Buf("H"), Buf("Hbf"), Buf("t1")
        yl1B = Buf("yl1")
        ylB = [yl1B, Buf("ylx")]
        ETB = Buf("ET")
        PbB = [Buf("P0"), Buf("P1")]
        attB = [Buf("att0")] * 2
        recB, cmB, identB_, hvB, abcB, cvpB, dclB = (Buf(n) for n in "rec cm ident hv abc cvp dcl".split())

        b2bf = banks[2][:, :].bitcast(BF16)
        b7bf = banks[7][:, :].bitcast(BF16)

        P.dma("sync", "ld", lambda e: e.dma_start(out=cm.rearrange("p a b -> p (a b)"), in_=cmats), writes=[cmB])
        P.dma("gpsimd", "wq", lambda e: e.dma_start(out=ident, in_=identb), writes=[identB_])
        P.dma("sync", "ld", lambda e: e.dma_start(out=hv, in_=hvec), writes=[hvB])
        P.dma("sync", "ld", lambda e: e.dma_start(out=cvp.rearrange("p a b -> p (a b)"), in_=convp), writes=[cvpB])
        P.dma("sync", "ld", lambda e: e.dma_start(out=dcl, in_=dcol), writes=[dclB])
        P.op("scalar", lambda e: e.activation(out=a_bc, in_=hv[:, 16:32], func=AF.Exp), reads=[hvB], writes=[abcB])
        P.op("vector", lambda e: e.tensor_scalar(out=a_bc, in0=a_bc, scalar1=-1.0, scalar2=None, op0=ALU.mult), reads=[abcB], writes=[abcB])
        P.op("gpsimd", lambda e: e.memset(H, 0.0), writes=[HB])
        P.op("gpsimd", lambda e: e.memset(Hbf, 0.0), writes=[HbfB])
        def build_ET():
            P.dma("sync", "ld", lambda e: e.dma_start(out=mtmp.rearrange("p a b -> p (a b)"), in_=amask), writes=[EB])
            for half in range(2):
                P.dma("sync", "ld", lambda e, half=half: e.dma_start(out=etmp.rearrange("p h j q -> p (h j q)"), in_=biasT[:, half * 2560:(half + 1) * 2560]),
                      writes=[t1B, ylB[0], ylB[1]])
                for hh in range(4):
                    h = half * 4 + hh
                    P.op("scalar", lambda e, hh=hh, h=h: e.activation(out=ET[:, h, :, :], in_=etmp[:, hh, :, :], func=AF.Exp),
                         reads=[t1B, ylB[0], ylB[1]], writes=[ETB])
                    P.op("vector", lambda e, h=h: e.tensor_tensor(out=ET[:, h, :, :], in0=ET[:, h, :, :], in1=mtmp, op=ALU.mult),
                         reads=[EB], writes=[ETB])

        rot = [0]

        def pbank():
            b = rot[0] % 2
            rot[0] += 1
            return b

        def load_hm(t):
            b = t % 2
            src = hmS.rearrange("(kt p) t -> p kt t", p=128)
            P.dma("sync", "ld", lambda e: e.dma_start(out=hm[b], in_=src[:, :, t * TT:(t + 1) * TT]), writes=[hmB[b]])

        def mm_fm(t, col0, ncols=TT, c0=0):
            b = pbank()
            hmv, hb_ = hm[t % 2], hmB[t % 2]
            for kt in range(8):
                P.op("tensor", lambda e, kt=kt: e.matmul(banks[b][:, 0:ncols], lhsT=win[:, kt, col0:col0 + 128], rhs=hmv[:, kt, c0:c0 + ncols],
                                                          start=(kt == 0), stop=(kt == 7)),
                     reads=[hb_, win_buf(col0)], writes=[bankB[b]], inc=(kt == 7))
            return b

        def mm_tm(t, tb, col0, ncols, out_ap, outB):
            hmv, hb_ = hm[t % 2], hmB[t % 2]
            for kt in range(8):
                P.op("tensor", lambda e, kt=kt: e.matmul(out_ap, lhsT=hmv[:, kt, tb * 128:(tb + 1) * 128], rhs=win[:, kt, col0:col0 + ncols],
                                                          start=(kt == 0), stop=(kt == 7)),
                     reads=[hb_, win_buf(col0)], writes=[outB], inc=(kt == 7))

        def projections(t):
            half = t % 2

            def k_unit(j):
                b = mm_fm(t, O_K + j * 128)
                P.op("scalar", lambda e: e.activation(out=k_fm[:, j, half * TT:(half + 1) * TT], in_=banks[b][:, :], func=AF.Copy),
                     reads=[bankB[b]], writes=[kB[half]])

            def v_unit(tb):
                b = pbank()
                blk = half * 4 + tb
                mm_tm(t, tb, O_V, 512, banks[b][:, :], bankB[b])
                P.op("vector", lambda e: e.tensor_copy(out=v_tm[:, blk, :, 0:64], in_=banks[b][:, :].rearrange("p (h d) -> p h d", h=8)),
                     reads=[bankB[b]], writes=[vB[blk]])
                if t == 0:
                    P.op("vector", lambda e: e.tensor_copy(out=v_tm[:, blk, :, 64:65],
                                                           in_=flag_sb.rearrange("p (a o) -> p a o", a=1).broadcast_to([128, 8, 1])),
                         reads=[flagB], writes=[vB[blk]])
                else:
                    P.op("gpsimd", lambda e: e.memset(v_tm[:, blk, :, 64:65], 1.0), writes=[vB[blk]])

            def xbc_unit(ct):
                b = mm_fm(t, O_XBC + ct * 128)
                P.op("scalar", lambda e: e.activation(out=xbc[:, ct, 3:TT + 3], in_=banks[b][:, :], func=AF.Copy),
                     reads=[bankB[b]], writes=[xbcB[ct]])

            def q_unit(j):
                b = mm_fm(t, O_Q + j * 128)
                P.op("scalar", lambda e: e.activation(out=q_fm[:, j, :], in_=banks[b][:, :], func=AF.Copy, scale=0.125),
                     reads=[bankB[b]], writes=[qB])

            def dt_unit(tb):
                mm_tm(t, tb, O_DT, 16, banks[2][:, tb * 16:(tb + 1) * 16], bankB[2])
                P.op("vector", lambda e: e.tensor_tensor(out=dtp[:, tb, :], in0=banks[2][:, tb * 16:(tb + 1) * 16], in1=hv[:, 0:16], op=ALU.add),
                     reads=[bankB[2], hvB], writes=[dtpB])

            def z_unit(tb, zh):
                i = tb % 2
                b = pbank()
                mm_tm(t, tb, O_Z + zh * 512, 512, banks[b][:, :], bankB[b])
                P.op("scalar", lambda e: e.activation(out=sz[i][:, zh * 512:(zh + 1) * 512], in_=banks[b][:, :], func=AF.Silu),
                     reads=[bankB[b]], writes=[szB[i]])
                if zh == 1:
                    r0 = (t - 1) * TT + tb * 128
                    P.dma("sync", "st", lambda e: e.dma_start(out=szS[r0:r0 + 128, :], in_=sz[i]), reads=[szB[i]])

            if t == 0:
                for j in range(4):
                    k_unit(j)
                for tb in range(4):
                    v_unit(tb)
                for ct in range(12):
                    b = mm_fm(t, O_XBC + ct * 128, ncols=16, c0=TT - 16)
                    P.op("scalar", lambda e, b=b, ct=ct: e.activation(out=xbc[:, ct, 0:3], in_=banks[b][:, 13:16], func=AF.Copy),
                         reads=[bankB[b]], writes=[xbcB[ct]])
                return
            for ct in range(12):
                xbc_unit(ct)
            rest = ([(lambda j=j: k_unit(j)) for j in range(4)] + [(lambda tb=tb: v_unit(tb)) for tb in range(4)]
                    + [(lambda j=j: q_unit(j)) for j in range(4)] + [(lambda tb=tb: dt_unit(tb)) for tb in range(4)]
                    + [(lambda tb=tb, zh=zh: z_unit(tb, zh)) for tb in range(4) for zh in range(2)])
            for idx, u in enumerate(rest):
                u()
                if idx % 2 == 1 and idx // 2 < 12:
                    conv_ct(t, idx // 2)
            conv_tail(t)

        def conv(t):
            for ct in range(12):
                conv_ct(t, ct)
            conv_tail(t)

        def conv_ct(t, ct):
            if True:
                ab = 3 + ct % 2
                acc = banks[ab][:, :]
                P.op("vector", lambda e, ct=ct, acc=acc: e.tensor_scalar(out=acc, in0=xbc[:, ct, 0:TT], scalar1=cvp[:, ct, 0:1], scalar2=cvp[:, ct, 4:5],
                                                                         op0=ALU.mult, op1=ALU.add),
                     reads=[xbcB[ct], cvpB], writes=[bankB[ab]])
                for k in range(1, 4):
                    P.op("vector", lambda e, ct=ct, acc=acc, k=k: e.scalar_tensor_tensor(out=acc, in0=xbc[:, ct, k:k + TT], scalar=cvp[:, ct, k:k + 1], in1=acc,
                                                                                         op0=ALU.mult, op1=ALU.add),
                         reads=[xbcB[ct], cvpB], writes=[bankB[ab]])
                if ct < 8:
                    dst, dB = xs_fm[:, ct, :], xsB[ct]
                elif ct < 10:
                    dst, dB = B_fm[:, ct - 8, :], BfB[ct - 8]
                else:
                    dst, dB = C_fm[:, ct - 10, :], CfB[ct - 10]
                P.op("scalar", lambda e, acc=acc, dst=dst: e.activation(out=dst, in_=acc, func=AF.Silu), reads=[bankB[ab]], writes=[dB])
                if ct < 8:
                    P.op("gpsimd", lambda e, ct=ct: e.tensor_scalar(out=xsD_fm[:, ct, :], in0=xs_fm[:, ct, :], scalar1=dcl[:, ct:ct + 1], scalar2=1.0, op0=ALU.mult, op1=ALU.mult),
                         reads=[xsB[ct], dclB], writes=[xsDB[ct]])
        def conv_tail(t):
            P.op("gpsimd", lambda e: e.tensor_copy(out=xbc[:, :, 0:3], in_=xbc[:, :, TT:TT + 3]), reads=xbcB, writes=xbcB)
            c0 = (t - 1) * TT
            P.dma("sync", "st", lambda e: e.dma_start(out=cS.rearrange("(g p) t -> p g t", p=128)[:, :, c0:c0 + TT], in_=C_fm), reads=CfB)

        ACT, DVE, PE, POOL = "scalar", "vector", "tensor", "gpsimd"

        def ssd_ctx(t, j):
            cidx = (t - 1) * 4 + j
            p = cidx % 2
            return dict(t=t, j=j, cs=slice(j * 128, (j + 1) * 128), cidx=cidx, p=p, SB_=SSB, dtv=SS[:, 0, j, :], dta=SS[:, 1, j, :], ea=SS[:, 3, j, :], cd=SS[:, 5, j, :], ddo=SS[:, 6, j, :],
                        E=E2[p], EB=E2B[p], cbm=cbm2[p], cbmB=cbm2B[p], xdt=xdt2[p], xdtB=xdt2B[p],
                        xw=xw2[p], xwB=xw2B[p], Btm=Btm2[p], BtmB=Btm2B[p])

        def smalls_tile(t):
            fl = lambda ap: ap.rearrange("p a b -> p (a b)")
            P.op(ACT, lambda e: e.activation(out=fl(dtp), in_=fl(dtp), func=AF.Exp), reads=[dtpB], writes=[dtpB])
            P.op(ACT, lambda e: e.activation(out=fl(SS[:, 0]), in_=fl(dtp), func=AF.Ln, bias=one_c[:, 0:1]), reads=[dtpB, onecB], writes=[SSB])
            P.op(DVE, lambda e: e.tensor_tensor(out=SS[:, 1], in0=SS[:, 0], in1=a_bc.rearrange("p (o h) -> p o h", o=1).broadcast_to([128, 4, 16]), op=ALU.mult),
                 reads=[SSB, abcB], writes=[SSB])
            acs_ps, tot_ps = banks[2][:, 64:128], banks[2][:, 0:64]
            P.op(PE, lambda e: e.matmul(acs_ps, lhsT=cm[:, 0, :], rhs=fl(SS[:, 1]), start=True, stop=True), reads=[SSB, cmB], writes=[bankB[2]])
            P.op(PE, lambda e: e.matmul(tot_ps, lhsT=cm[:, 2, :], rhs=fl(SS[:, 1]), start=True, stop=True), reads=[SSB, cmB], writes=[bankB[2]])
            P.op(DVE, lambda e: e.tensor_copy(out=fl(SS[:, 2]), in_=acs_ps), reads=[bankB[2]], writes=[SSB])
            for j in range(4):
                P.op(DVE, lambda e, j=j: e.tensor_tensor(out=dtp[:, j, :], in0=acs_ps[:, j * 16:(j + 1) * 16], in1=runtot, op=ALU.add), reads=[bankB[2], runtotB], writes=[dtpB])
                P.op(DVE, lambda e, j=j: e.tensor_tensor(out=runtot, in0=tot_ps[:, j * 16:(j + 1) * 16], in1=runtot, op=ALU.add), reads=[bankB[2]], writes=[runtotB])
            P.op(DVE, lambda e: e.tensor_tensor(out=fl(SS[:, 4]), in0=tot_ps, in1=fl(SS[:, 2]), op=ALU.subtract), reads=[bankB[2]], writes=[SSB])
            P.op(ACT, lambda e: e.activation(out=fl(SS[:, 5]), in_=tot_ps, func=AF.Exp), reads=[bankB[2]], writes=[SSB])
            P.op(ACT, lambda e: e.activation(out=fl(SS[:, 3]), in_=fl(SS[:, 2]), func=AF.Exp), reads=[SSB], writes=[SSB])
            c0 = (t - 1) * 4
            P.op(ACT, lambda e: e.activation(out=eag_all[:, c0:c0 + 4, :], in_=dtp, func=AF.Exp), reads=[dtpB], writes=[eagB])
            P.op(ACT, lambda e: e.activation(out=fl(SS[:, 4]), in_=fl(SS[:, 4]), func=AF.Exp), reads=[SSB], writes=[SSB])
            P.op(DVE, lambda e: e.tensor_tensor(out=SS[:, 6], in0=SS[:, 0], in1=SS[:, 4], op=ALU.mult), reads=[SSB], writes=[SSB])

        def a3(X, qd):
            dta, SB_, E, EB = X["dta"], X["SB_"], X["E"], X["EB"]
            r, rB = R[qd % 2], RB[qd % 2]
            sbk = 3 + qd % 2
            P.op(POOL, lambda e: e.tensor_tensor(out=r, in0=bc_last(dta[:, 4 * qd:4 * qd + 4], 128), in1=bc_mid(cm[:, 0, :], 4), op=ALU.mult),
                 reads=[SB_, cmB], writes=[rB])
            P.op(PE, lambda e: e.matmul(banks[sbk][:, :], lhsT=cm[:, 1, :], rhs=r.rearrange("p a b -> p (a b)"), start=True, stop=True),
                 reads=[rB, cmB], writes=[bankB[sbk]])
            P.op(ACT, lambda e: e.activation(out=E[:, 4 * qd:4 * qd + 4, :], in_=banks[sbk][:, :].rearrange("p (a b) -> p a b", a=4), func=AF.Exp),
                 reads=[bankB[sbk]], writes=[EB])

        def a4(X):
            cs, cbm, cbmB = X["cs"], X["cbm"], X["cbmB"]
            for g in range(2):
                P.op(PE, lambda e, g=g: e.matmul(banks[2][:, 128 + g * 128:256 + g * 128], lhsT=B_fm[:, g, cs], rhs=C_fm[:, g, cs], start=True, stop=True),
                     reads=[BfB[g], CfB[g]], writes=[bankB[2]])
            P.op(DVE, lambda e: e.tensor_tensor(out=cbm, in0=banks[2][:, 128:384].rearrange("p (g l) -> p g l", g=2), in1=bc_mid(cm[:, 0, :], 2), op=ALU.mult),
                 reads=[bankB[2], cmB], writes=[cbmB])
            for g in range(2):
                P.op(PE, lambda e, g=g: e.transpose(out=b2bf[:, 768 + g * 128:896 + g * 128], in_=B_fm[:, g, cs], identity=ident),
                     reads=[BfB[g], identB_], writes=[bankB[2]])
            Btm, BtmB = X["Btm"], X["BtmB"]
            P.op(ACT, lambda e: e.activation(out=Btm.rearrange("p a b -> p (a b)"), in_=b2bf[:, 768:1024], func=AF.Copy), reads=[bankB[2]], writes=[BtmB])

        def a5(X):
            E, EB, cbm, cbmB = X["E"], X["EB"], X["cbm"], X["cbmB"]
            for g in range(2):
                P.op(DVE, lambda e, g=g: e.tensor_tensor(out=E[:, 8 * g:8 * g + 8, :], in0=E[:, 8 * g:8 * g + 8, :], in1=cbm[:, g:g + 1, :].broadcast_to([128, 8, 128]), op=ALU.mult),
                     reads=[cbmB], writes=[EB])

        def a6(X):
            cs, dtv, ddo, SB_ = X["cs"], X["dtv"], X["ddo"], X["SB_"]
            for kt in range(8):
                P.op(PE, lambda e, kt=kt: e.transpose(out=b7bf[:, kt * 128:(kt + 1) * 128], in_=xs_fm[:, kt, cs], identity=ident),
                     reads=[xsB[kt], identB_], writes=[bankB[7]], inc=(kt == 7))
            xsT = b7bf.rearrange("p (h d) -> p h d", h=16)
            xdt, xdtB, xw, xwB = X["xdt"], X["xdtB"], X["xw"], X["xwB"]
            P.op(DVE, lambda e: e.tensor_tensor(out=xdt, in0=xsT, in1=bc_last(dtv, 64), op=ALU.mult), reads=[bankB[7], SB_], writes=[xdtB])
            P.op(DVE, lambda e: e.tensor_tensor(out=xw, in0=xsT, in1=bc_last(ddo, 64), op=ALU.mult), reads=[bankB[7], SB_], writes=[xwB])

        def b1(X, hh):
            cs, ea, SB_, E, EB, xdt, xdtB, cidx = X["cs"], X["ea"], X["SB_"], X["E"], X["EB"], X["xdt"], X["xdtB"], X["cidx"]
            for hq in range(8):
                h = 8 * hh + hq
                P.op(PE, lambda e, h=h, hq=hq: e.matmul(banks[5][:, hq * 64:(hq + 1) * 64], lhsT=E[:, h, :], rhs=xdt[:, h, :], start=(hq == 0), stop=False,
                                                        skip_group_check=True),
                     reads=[EB, xdtB], writes=[bankB[5]], inc=False)
            for kq in range(4):
                kt = 4 * hh + kq
                P.op(PE, lambda e, kt=kt, kq=kq: e.matmul(banks[5][:, kq * 128:(kq + 1) * 128], lhsT=xsD_fm[:, kt, cs], rhs=ident, start=False, stop=(kq == 3),
                                                          skip_group_check=True),
                     reads=[xsDB[kt], identB_], writes=[bankB[5]], inc=(kq == 3))
            P.op(PE, lambda e: e.matmul(banks[6][:, :], lhsT=C_fm[:, hh, cs], rhs=Hbf[:, 8 * hh:8 * hh + 8, :].rearrange("p a b -> p (a b)"), start=True, stop=True),
                 reads=[CfB[hh], HbfB], writes=[bankB[6]])
            P.op(DVE, lambda e: e.tensor_tensor(out=t1[:, hh * 512:(hh + 1) * 512].rearrange("p (a b) -> p a b", a=8),
                                                in0=banks[6][:, :].rearrange("p (a b) -> p a b", a=8),
                                                in1=bc_last(ea[:, 8 * hh:8 * hh + 8], 64), op=ALU.mult),
                 reads=[bankB[6], SB_], writes=[t1B])
            P.op(DVE, lambda e: e.tensor_tensor(out=yl1[:, hh * 512:(hh + 1) * 512], in0=banks[5][:, :], in1=t1[:, hh * 512:(hh + 1) * 512], op=ALU.add),
                 reads=[bankB[5], t1B], writes=[yl1B])
            if hh == 1:
                P.dma("sync", "st", lambda e: e.dma_start(out=ylS[cidx * 128:(cidx + 1) * 128, :], in_=yl1), reads=[yl1B])

        def b2(X, g):
            cd, SB_, xw, xwB, Btm, BtmB = X["cd"], X["SB_"], X["xw"], X["xwB"], X["Btm"], X["BtmB"]
            P.op(PE, lambda e: e.matmul(banks[6][:, :], lhsT=Btm[:, g, :], rhs=xw[:, 8 * g:8 * g + 8, :].rearrange("p a b -> p (a b)"), start=True, stop=True),
                 reads=[BtmB, xwB], writes=[bankB[6]])
            P.op(POOL, lambda e: e.tensor_tensor(out=H[:, 8 * g:8 * g + 8, :], in0=H[:, 8 * g:8 * g + 8, :], in1=bc_last(cd[:, 8 * g:8 * g + 8], 64), op=ALU.mult),
                 reads=[SB_], writes=[HB])
            P.op(DVE, lambda e: e.tensor_tensor(out=H[:, 8 * g:8 * g + 8, :], in0=banks[6][:, :].rearrange("p (a b) -> p a b", a=8), in1=H[:, 8 * g:8 * g + 8, :], op=ALU.add),
                 reads=[bankB[6]], writes=[HB])
            if g == 1:
                P.op(ACT, lambda e: e.activation(out=Hbf, in_=H, func=AF.Copy), reads=[HB], writes=[HbfB])

        def ssd_tile(t):
            Xs = [ssd_ctx(t, j) for j in range(4)]

            def run(XA, XB):
                if XB is not None:
                    b1(XB, 0)
                if XA is not None:
                    a3(XA, 0); a3(XA, 1)
                if XB is not None:
                    b1(XB, 1)
                if XA is not None:
                    a4(XA); a3(XA, 2); a3(XA, 3)
                if XB is not None:
                    b2(XB, 0)
                if XA is not None:
                    a5(XA); a6(XA)
                if XB is not None:
                    b2(XB, 1)

            smalls_tile(t)
            run(Xs[0], None)
            for j in range(4):
                run(Xs[j + 1] if j + 1 < 4 else None, Xs[j])

        def attn_tile(t, fillers=()):
            def pblk(lb):
                return ((t + 1) % 2) * 4 + lb if lb < 4 else (t % 2) * 4 + (lb - 4)

            items = [(m, h) for m in range(4) for h in range(8)]

            def qk(i):
                m, h = items[i]
                qs = slice(m * 128, (m + 1) * 128)
                hp, po = h // 2, (h % 2) * 64
                sA, sB2 = (3, 4) if i % 2 == 0 else (5, 6)
                for jj in range(5):
                    pb = pblk(m + jj)
                    o = banks[sA][:, jj * 128:(jj + 1) * 128] if jj < 4 else banks[sB2][:, 0:128]
                    ob = bankB[sA] if jj < 4 else bankB[sB2]
                    P.op("tensor", lambda e, o=o, pb=pb: e.matmul(o, lhsT=k_fm[po:po + 64, hp, pb * 128:(pb + 1) * 128], rhs=q_fm[po:po + 64, hp, qs], start=True, stop=True),
                         reads=[kB[pb // 4], qB], writes=[ob], inc=(jj >= 3))

            def expmul(i):
                m, h = items[i]
                sA, sB2 = (3, 4) if i % 2 == 0 else (5, 6)
                Pv, PvB = Pb[i % 2], PbB[i % 2]
                P.op("scalar", lambda e: e.activation(out=Pv[:, 0:4, :], in_=banks[sA][:, :].rearrange("p (a b) -> p a b", a=4), func=AF.Exp),
                     reads=[bankB[sA]], writes=[PvB])
                P.op("scalar", lambda e: e.activation(out=Pv[:, 4, :], in_=banks[sB2][:, 0:128], func=AF.Exp), reads=[bankB[sB2]], writes=[PvB])
                meng = "gpsimd"
                P.op(meng, lambda e: e.tensor_tensor(out=Pv, in0=Pv, in1=ET[:, h, :, :], op=ALU.mult), reads=[ETB], writes=[PvB])

            def pv(i):
                m, h = items[i]
                Pv, PvB = Pb[i % 2], PbB[i % 2]
                OB = 7 if h < 4 else 2
                oc = (h % 4) * 65
                for jj in range(5):
                    pb = pblk(m + jj)
                    P.op("tensor", lambda e, jj=jj, pb=pb: e.matmul(banks[OB][:, oc:oc + 65], lhsT=Pv[:, jj, :], rhs=v_tm[:, pb, h, :],
                                                                     start=(jj == 0), stop=(jj == 4), skip_group_check=True),
                         reads=[PvB, vB[pb]], writes=[bankB[OB]], inc=(jj == 4))
                if h % 4 == 3:
                    g = h // 4
                    ab = att[m % 2]
                    ov = banks[OB][:, 0:260].rearrange("p (a b) -> p a b", a=4)
                    P.op("vector", lambda e: e.reciprocal(out=rec[:, 4 * g:4 * g + 4], in_=ov[:, :, 64:65].rearrange("p a o -> p (a o)")),
                         reads=[bankB[OB]], writes=[recB])
                    P.op("vector", lambda e: e.tensor_tensor(out=ab[:, 4 * g:4 * g + 4, :], in0=ov[:, :, 0:64], in1=bc_last(rec[:, 4 * g:4 * g + 4], 64), op=ALU.mult),
                         reads=[bankB[OB], recB], writes=[attB[m % 2]])
                    if h == 7:
                        r0 = (t - 1) * TT + m * 128
                        P.dma("sync", "st", lambda e: e.dma_start(out=attS[r0:r0 + 128, :], in_=ab.rearrange("p a b -> p (a b)")), reads=[attB[m % 2]])

            nf = len(fillers)
            fi = 0
            qk(0)
            for i in range(len(items)):
                if i + 1 < len(items):
                    qk(i + 1)
                expmul(i)
                while fi < nf and fi * len(items) <= i * nf:
                    fillers[fi]()
                    fi += 1
                pv(i)
            while fi < nf:
                fillers[fi]()
                fi += 1

        load_hm(0)
        for t in range(NTH):
            if t + 1 < NTH:
                load_hm(t + 1)
            projections(t)
            if t == 0:
                build_ET()
                continue
            attn_tile(t)
            ssd_tile(t)
        P.dma("sync", "st", lambda e: e.dma_start(out=hloc, in_=H.rearrange("p a b -> p (a b)")), reads=[HB], writes=[hlocB])
        if debug_out:
            P.dma("sync", "st", lambda e: e.dma_start(out=hdbg, in_=H.rearrange("p a b -> p (a b)")), reads=[HB])
        P.dma("gpsimd", "cc", lambda e: e.collective_compute("AllGather", ALU.bypass, replica_groups=RG, ins=[hloc.opt()], outs=[hpair.opt()]),
              reads=[hlocB], writes=[hpairB], incv=-1)

    def mixer_pass2():
        wout = sb.view(WOUT_OFF, [128, 12, D], BF16)
        c = Carver(sb, ARENA, SB_TOTAL - PERS)
        A = 16
        x1t = c.get([128, 8, TT], F32, A)
        yl = [c.get([128, 1024], F32, A) for _ in range(2)]
        szb = [c.get([128, 1024], BF16, A) for _ in range(2)]
        attb = [c.get([128, 512], BF16, A) for _ in range(2)]
        cfm = [c.get([128, 2, TT], BF16, A) for _ in range(2)]
        gb = [c.get([128, 1024], F32, A) for _ in range(2)]
        gn = [c.get([128, 1024], BF16, A) for _ in range(2)]
        gT = c.get([128, 12, TT], BF16, A)
        Hin = c.get([128, 1024], F32, A)
        Hinb = c.get([128, 2, 512], BF16, A)
        ssd = c.get([128, 1024], F32, A)
        ident = c.get([128, 128], BF16, A)
        sm = [c.get([128, 4], F32, A) for _ in range(2)]
        x1B = Buf("x1t")
        ylB = [Buf("yl0"), Buf("yl1")]
        szB = [Buf("sz0"), Buf("sz1")]
        atB = [Buf("at0"), Buf("at1")]
        cfB = [Buf("cf0"), Buf("cf1")]
        gbB = [Buf("gb0"), Buf("gb1")]
        gnB = [Buf("gn0"), Buf("gn1")]
        gTB, HinB, HinbB, ssdB, idB = Buf("gT"), Buf("Hin"), Buf("Hinb"), Buf("ssd"), Buf("ident")
        smB = [Buf("sm0"), Buf("sm1")]
        b2bf = banks[2][:, :].bitcast(BF16)
        b3bf = banks[3][:, :].bitcast(BF16)

        P.dma("gpsimd", "wq", lambda e: e.dma_start(out=wout, in_=w_out.rearrange("(kt p) d -> p kt d", p=128)), writes=[woutB])
        P.dma("gpsimd", "wq", lambda e: e.dma_start(out=ident, in_=identb), writes=[idB])
        if 4 in phases:
            load_ffn_weights(w2g, w2u, w2d, do_parts=(0,))
        P.dma("sync", "ld", lambda e: e.dma_start(out=ssd, in_=ssdn), writes=[ssdB])
        P.dma("sync", "ld", lambda e: e.dma_start(out=Hin, in_=hpair[0:128, :]), reads=[hpairB], writes=[HinB])
        if debug_out:
            P.dma("sync", "st", lambda e: e.dma_start(out=hindbg, in_=Hin), reads=[HinB])
        P.op("vector", lambda e: e.tensor_scalar(out=Hinb.rearrange("p a b -> p (a b)"), in0=Hin, scalar1=flag_sb[:, 0:1], scalar2=None, op0=ALU.mult),
             reads=[HinB, flagB], writes=[HinbB])

        x1v = x1S.rearrange("(kt p) t -> p kt t", p=128)
        x2v = x2S.rearrange("(kt p) t -> p kt t", p=128)
        cSv = cS.rearrange("(g p) t -> p g t", p=128)
        def p3_loads(cidx, which=(0, 1)):
            i = cidx % 2
            r0 = cidx * 128
            if 0 in which:
                P.dma("sync", "ld", lambda e: e.dma_start(out=yl[i], in_=ylS[r0:r0 + 128, :]), writes=[ylB[i]])
                P.dma("sync", "ld", lambda e: e.dma_start(out=szb[i], in_=szS[r0:r0 + 128, :]), writes=[szB[i]])
            if 1 in which:
                P.dma("sync", "ld", lambda e: e.dma_start(out=attb[i], in_=attS[r0:r0 + 128, :]), writes=[atB[i]])

        def p3_corr(cidx):
            tt, j, i = cidx // 4, cidx % 4, cidx % 2
            cf, cfb = cfm[tt % 2], cfB[tt % 2]
            for g in range(2):
                P.op("tensor", lambda e, g=g: e.matmul(banks[g][:, :], lhsT=cf[:, g, j * 128:(j + 1) * 128], rhs=Hinb[:, g, :], start=True, stop=True),
                     reads=[cfb, HinbB], writes=[bankB[g]])
                P.op("vector", lambda e, g=g: e.tensor_tensor(out=gb[i][:, g * 512:(g + 1) * 512].rearrange("p (a b) -> p a b", a=8),
                                                              in0=banks[g][:, :].rearrange("p (a b) -> p a b", a=8),
                                                              in1=bc_last(eag_all[:, cidx, 8 * g:8 * g + 8], 64), op=ALU.mult),
                     reads=[bankB[g], eagB], writes=[gbB[i]])

        def p3_chain1(cidx):
            i = cidx % 2
            P.op("vector", lambda e: e.tensor_tensor(out=gb[i], in0=gb[i], in1=yl[i], op=ALU.add), reads=[ylB[i]], writes=[gbB[i]])
            P.op("gpsimd", lambda e: e.tensor_tensor(out=gb[i], in0=gb[i], in1=szb[i], op=ALU.mult), reads=[szB[i]], writes=[gbB[i]])

        def p3_chain(cidx):
            i = cidx % 2
            P.op("scalar", lambda e: e.activation(out=gn[i], in_=gb[i], func=AF.Square, accum_out=sm[i][:, 0:1]), reads=[gbB[i]], writes=[gnB[i], smB[i]])
            P.op("vector", lambda e: e.tensor_scalar(out=sm[i][:, 1:2], in0=sm[i][:, 0:1], scalar1=1.0 / 1024.0, scalar2=EPS, op0=ALU.mult, op1=ALU.add),
                 reads=[smB[i]], writes=[smB[i]])
            P.op("scalar", lambda e: e.activation(out=sm[i][:, 2:3], in_=sm[i][:, 1:2], func=AF.Sqrt), reads=[smB[i]], writes=[smB[i]])
            P.op("vector", lambda e: e.reciprocal(out=sm[i][:, 3:4], in_=sm[i][:, 2:3]), reads=[smB[i]], writes=[smB[i]])
            P.op("vector", lambda e: e.scalar_tensor_tensor(out=gn[i], in0=gb[i], scalar=sm[i][:, 3:4], in1=ssd, op0=ALU.mult, op1=ALU.mult),
                 reads=[gbB[i], smB[i], ssdB], writes=[gnB[i]])

        def p3_tr(cidx):
            j, i = cidx % 4, cidx % 2
            for kt in range(8):
                P.op("tensor", lambda e, kt=kt: e.transpose(out=b2bf[:, kt * 128:(kt + 1) * 128], in_=gn[i][:, kt * 128:(kt + 1) * 128], identity=ident),
                     reads=[gnB[i], idB], writes=[bankB[2]], inc=(kt == 7))
            for kt in range(4):
                P.op("tensor", lambda e, kt=kt: e.transpose(out=b3bf[:, kt * 128:(kt + 1) * 128], in_=attb[i][:, kt * 128:(kt + 1) * 128], identity=ident),
                     reads=[atB[i], idB], writes=[bankB[3]], inc=(kt == 3))
            P.op("scalar", lambda e: e.activation(out=gT[:, 0:8, j * 128:(j + 1) * 128], in_=b2bf.rearrange("p (a b) -> p a b", a=8), func=AF.Copy),
                 reads=[bankB[2]], writes=[gTB])
            P.op("scalar", lambda e: e.activation(out=gT[:, 8:12, j * 128:(j + 1) * 128], in_=b3bf[:, 0:512].rearrange("p (a b) -> p a b", a=4), func=AF.Copy),
                 reads=[bankB[3]], writes=[gTB])

        def p3_wout(tt):
            for dtile in range(8):
                b = 4 + dtile % 2
                for kt in range(12):
                    P.op("tensor", lambda e, kt=kt, b=b, dtile=dtile: e.matmul(banks[b][:, :], lhsT=wout[:, kt, dtile * 128:(dtile + 1) * 128], rhs=gT[:, kt, :],
                                                                               start=(kt == 0), stop=(kt == 11)),
                         reads=[gTB, woutB], writes=[bankB[b]], inc=(kt == 11))
                P.op("vector", lambda e, b=b, dtile=dtile: e.tensor_tensor(out=x1t[:, dtile, :], in0=banks[b][:, :], in1=x1t[:, dtile, :], op=ALU.add),
                     reads=[bankB[b]], writes=[x1B])
            P.dma("sync", "st", lambda e: e.dma_start(out=x2v[:, :, tt * TT:(tt + 1) * TT], in_=x1t), reads=[x1B])

        def p3_tile_loads(tt):
            P.dma("sync", "ld", lambda e: e.dma_start(out=x1t, in_=x1v[:, :, (tt + 1) * TT:(tt + 2) * TT]), writes=[x1B])

        def p3_cf_load(tt):
            cf, cfb = cfm[tt % 2], cfB[tt % 2]
            P.dma("sync", "ld", lambda e: e.dma_start(out=cf, in_=cSv[:, :, tt * TT:(tt + 1) * TT]), writes=[cfb])

        NCHK = NT * 4
        p3_cf_load(0)
        if NT > 1:
            p3_cf_load(1)
        p3_loads(0)
        if NCHK > 1:
            p3_loads(1)
        p3_tile_loads(0)
        p3_corr(0)
        p3_chain1(0)
        for cidx in range(NCHK):
            tt, j = cidx // 4, cidx % 4
            if j == 0 and tt > 0 and tt + 1 < NT:
                p3_cf_load(tt + 1)
            if cidx + 1 < NCHK:
                p3_corr(cidx + 1)
                p3_chain1(cidx + 1)
            if cidx + 2 < NCHK:
                p3_loads(cidx + 2, which=(0,))
            if j == 0 and tt > 0:
                p3_wout(tt - 1)
                p3_tile_loads(tt)
            p3_chain(cidx)
            p3_tr(cidx)
            if cidx + 2 < NCHK:
                p3_loads(cidx + 2, which=(1,))
        p3_wout(NT - 1)
        if 4 in phases:
            wg, wu, wd = ffn_weight_views()
            dsrc = w2d.rearrange("(ft p) d -> p ft d", p=128)
            P.dma("gpsimd", "wq", lambda e: e.dma_start(out=wd[:, 10:NFT, :], in_=dsrc[:, 10:NFT, :]), writes=[wdB[1], woutB])

    if 1 in phases:
        load_ffn_weights(w1g, w1u, w1d)
        ffn_phase(xT, NTH, 0, 8, "mix", x1S, hmS, 0)
    if 2 in phases:
        load_w_in()
        P.fence(exclude=("wq",))
        mixer_pass1()
        P.fence(exclude=("wq",))
    if 3 in phases:
        mixer_pass2()
        P.fence(exclude=("wq",))
    if 4 in phases:
        if 3 not in phases:
            load_ffn_weights(w2g, w2u, w2d)
        ffn_phase(x2S if (3 in phases) else x1S[:, TT:], NT, 16, 24, "final", outT, None, 0)

    P.wait_all("sync", P.dma_toks() + [("cc", P.cnt["cc"])])

    with nc.Block() as block:
        run = P.emit(sems)
        block.sync(run("sync"))
        block.scalar(run("scalar"))
        block.tensor(run("tensor"))
        block.vector(run("vector"))
        block.gpsimd(run("gpsimd"))
    es.close()
    return nc


def make_in_maps(inputs, NT=8, n_cores=8):
    f = lambda a: np.ascontiguousarray(np.asarray(a, dtype=np.float32))
    x = f(inputs["x"])
    TOK = NT * TT
    shared = {
        "w1g": f(inputs["ffn1_w_gate"][0]), "w1u": f(inputs["ffn1_w_up"][0]), "w1d": f(inputs["ffn1_w_down"][0]),
        "w2g": f(inputs["ffn2_w_gate"][0]), "w2u": f(inputs["ffn2_w_up"][0]), "w2d": f(inputs["ffn2_w_down"][0]),
        "w_in": f(inputs["w_in"][0]), "w_out": f(inputs["w_out"][0]),
    }
    g = np.stack([f(inputs["ffn1_norm"][0]), f(inputs["mix_norm"][0]), f(inputs["ffn2_norm"][0]), f(inputs["final_norm"])])
    shared["gains"] = np.ascontiguousarray(g.reshape(4, 8, 128).transpose(2, 0, 1).reshape(128, 32))
    cw = np.concatenate([f(inputs["conv_w"][0]).T, f(inputs["conv_b"][0])[:, None]], axis=1)
    shared["convp"] = np.ascontiguousarray(cw.reshape(12, 128, 5).transpose(1, 0, 2).reshape(128, 60))
    hv = np.concatenate([f(inputs["dt_bias"][0]), f(inputs["a_log"][0]), f(inputs["d_skip"][0])])
    shared["hvec"] = np.ascontiguousarray(np.tile(hv[None, :], (128, 1)))
    shared["dcol"] = np.ascontiguousarray(np.repeat(f(inputs["d_skip"][0]), 64).reshape(8, 128).T)
    s_ = np.arange(128)
    triU = (s_[:, None] <= s_[None, :]).astype(np.float32)
    Lmat = (s_[:, None] > s_[None, :]).astype(np.float32)
    shared["cmats"] = np.ascontiguousarray(np.concatenate([triU, Lmat, np.ones((128, 128), np.float32)], axis=1))
    shared["identb"] = np.eye(128, dtype=np.float32)
    k_ = np.arange(128)[:, None, None]
    j_ = np.arange(5)[None, :, None]
    q_ = np.arange(128)[None, None, :]
    kabs = j_ * 128 + k_
    rel = np.clip(512 + q_ - kabs, -256, 256) + 256
    rb = f(inputs["rel_bias"][0])
    shared["biasT"] = np.ascontiguousarray(rb[:, rel].transpose(1, 0, 2, 3).reshape(128, 8 * 5 * 128))
    kc, qc = kabs // 64, q_ // 64
    shared["amask"] = np.ascontiguousarray(((kc >= qc) & (kc <= qc + 8)).astype(np.float32).reshape(128, 5 * 128))
    shared["ssdn"] = np.ascontiguousarray(np.tile(f(inputs["ssd_norm"][0])[None, :], (128, 1)))
    maps = []
    for c in range(n_cores):
        b, half = c // 2, c % 2
        start = half * TOK
        rows = np.zeros((TOK + TT, D), np.float32)
        if half == 1:
            rows[:] = x[b, start - TT:start + TOK]
        else:
            rows[TT:] = x[b, 0:TOK]
        m = dict(shared)
        m["xT"] = np.ascontiguousarray(rows.T)
        m["flag"] = np.full((128, 1), float(half), np.float32)
        maps.append(m)
    return maps


_NC_CACHE = {}


def kernel(**inputs):
    NT = 8
    if "nc" not in _NC_CACHE:
        _NC_CACHE["nc"] = build_program(NT=NT)
    nc = _NC_CACHE["nc"]
    maps = make_in_maps(inputs, NT=NT, n_cores=8)
    res = run_bass_kernel_spmd(nc, maps, core_ids=list(range(8)))
    TOK = NT * TT
    out = np.empty((4, 2 * TOK, D), np.float32)
    for c in range(8):
        b, half = c // 2, c % 2
        out[b, half * TOK:(half + 1) * TOK, :] = np.asarray(res.results[c]["outT"]).T
    return out
```
